# Optimizing a Trainium2 kernel written in Bass

```python
import math
import jax
import jax.numpy as jnp
from jax import lax
import numpy as np

D_MODEL = 1024
BATCH = 4
SEQ = 8192
DEPTH = 2

GRID_W = 64
CTX_LEN = 256
EPS = 1e-6
F32 = jnp.float32

MIX_WIDTH = D_MODEL
GROUP_W = MIX_WIDTH // 4
A_WIDTH = GROUP_W
A_HD = 64
A_HEADS = A_WIDTH // A_HD
MLP_CHUNK = 128
B_WIDTH = GROUP_W
B_HD = 64
B_HEADS = B_WIDTH // B_HD
GATE_RANK = 16
GATE_TEMP = 16.0
GLA_CHUNK = 64
C_WIDTH = GROUP_W
S5_IN = 16
S5_GROUPS = C_WIDTH // S5_IN
S5_STATE = 64
DT_MIN = 1e-3
DT_MAX = 1e-1
D_WIDTH = GROUP_W
MLA_HEADS = 4
MLA_NOPE = 64
MLA_ROPE = 32
MLA_V = D_WIDTH // MLA_HEADS
MLA_Q_RANK = 224
MLA_KV_RANK = 96
ROPE_BASE = 10000.0
Q_BLOCK = 128
D_FF = 2816
CONV_W = 3

A_COLS = 2 * A_WIDTH
B_COLS = 4 * B_WIDTH + 2 * GATE_RANK
C_COLS = C_WIDTH
D_COLS = MLA_Q_RANK + MLA_KV_RANK + MLA_ROPE
IN_COLS = A_COLS + B_COLS + C_COLS + D_COLS
SPLITS = (A_COLS, A_COLS + B_COLS, A_COLS + B_COLS + C_COLS)

kernel_name = "hybrid_headgroup_diffusion_block"


def rmsnorm(x, g):
    xf = x.astype(F32)
    y = xf * lax.rsqrt(jnp.mean(xf * xf, axis=-1, keepdims=True) + EPS)
    return (y * g.astype(F32)).astype(x.dtype)


def _ident(t):
    return t


def _flip(t):
    return jnp.flip(t, axis=1)


def _rotate(t, pos):
    nf = t.shape[-1] // 2
    inv = ROPE_BASE ** (-jnp.arange(nf, dtype=F32) / nf)
    ang = pos[:, None] * inv[None, :]
    cos = jnp.cos(ang)[None, :, None, :].astype(t.dtype)
    sin = jnp.sin(ang)[None, :, None, :].astype(t.dtype)
    t1, t2 = t[..., :nf], t[..., nf:]
    return jnp.concatenate([t1 * cos - t2 * sin, t1 * sin + t2 * cos], axis=-1)


def axial_rope(t, row, col):
    half = t.shape[-1] // 2
    return jnp.concatenate([_rotate(t[..., :half], row), _rotate(t[..., half:], col)], axis=-1)


def attend(q, k, v):
    b, nq, h, dh = q.shape
    scale = dh ** -0.5
    qb = q.reshape(b, nq // Q_BLOCK, Q_BLOCK, h, dh).swapaxes(0, 1)

    def block(qblk):
        s = jnp.einsum('bqhd,bkhd->bhqk', qblk, k).astype(F32) * scale
        p = jax.nn.softmax(s, axis=-1).astype(v.dtype)
        return jnp.einsum('bhqk,bkhd->bqhd', p, v)

    o = lax.map(block, qb)
    return o.swapaxes(0, 1).reshape(b, nq, h, v.shape[-1])


def dwconv_centred(h, w, bias):
    n = h.shape[1]
    pad = CONV_W // 2
    hp = jnp.pad(h, ((0, 0), (pad, pad), (0, 0)))
    out = bias
    for j in range(CONV_W):
        out = out + hp[:, j:j + n] * w[j]
    return out


def conv_ffn(h, w_up, conv_w, conv_b, w_down):
    z = dwconv_centred(h @ w_up, conv_w, conv_b)
    a, g = jnp.split(z, 2, axis=-1)
    return (jax.nn.gelu(a) * g) @ w_down


def chunk_mlp(p, g_norm, w_sp, b_sp):
    b, n, _ = p.shape
    u, v = jnp.split(jax.nn.gelu(p), 2, axis=-1)
    v = rmsnorm(v.reshape(b, n, A_HEADS, A_HD), g_norm)
    v = v.reshape(b, n // MLP_CHUNK, MLP_CHUNK, A_HEADS, A_HD)
    s = jnp.einsum('hts,bcshd->bcthd', w_sp, v) + b_sp.T[:, :, None]
    return u * s.reshape(b, n, A_WIDTH)


def gla_chunked(q, k, v, g, s0):
    b, n, h, dk = q.shape
    nc = n // GLA_CHUNK

    def rs(t):
        return t.reshape(b, nc, GLA_CHUNK, h, t.shape[-1])

    q, k, v, g = rs(q), rs(k), rs(v), rs(g)
    cum = jnp.cumsum(g, axis=2)
    tot = cum[:, :, -1:]
    q_in = q * jnp.exp(cum)
    k_in = k * jnp.exp(-cum)
    k_end = k * jnp.exp(tot - cum)
    tri = jnp.tril(jnp.ones((GLA_CHUNK, GLA_CHUNK), dtype=bool))
    scores = jnp.where(tri, jnp.einsum('bcthk,bcshk->bchts', q_in, k_in), 0.0)
    o_intra = jnp.einsum('bchts,bcshv->bcthv', scores, v)
    kv_chunk = jnp.einsum('bcshk,bcshv->bchkv', k_end, v)
    decay = jnp.exp(tot[:, :, 0])

    def step(s, inp):
        dec, kv = inp
        return dec[..., None] * s + kv, s

    s_fin, s_start = lax.scan(step, s0, (decay.swapaxes(0, 1), kv_chunk.swapaxes(0, 1)))
    o_inter = jnp.einsum('bcthk,bchkv->bcthv', q_in, s_start.swapaxes(0, 1))
    return (o_intra + o_inter).reshape(b, n, h, v.shape[-1]), s_fin


def gla_final_state(k, v, g):
    cum = jnp.cumsum(g, axis=1)
    w = jnp.exp(cum[:, -1:] - cum)
    return jnp.einsum('bnhk,bnhv->bhkv', k * w, v)


def gla_inputs(p, w_gate, b_gate):
    b, n, _ = p.shape
    q, k, v, r, gl = jnp.split(p, [B_WIDTH, 2 * B_WIDTH, 3 * B_WIDTH, 4 * B_WIDTH], axis=-1)

    def heads(t):
        return t.reshape(b, n, B_HEADS, B_HD).astype(F32)

    z = jnp.einsum('bndr,drc->bndc', gl.reshape(b, n, 2, GATE_RANK), w_gate) + b_gate
    logg = (jax.nn.log_sigmoid(z.astype(F32)) / GATE_TEMP).reshape(b, n, 2, B_HEADS, B_HD)
    return heads(q) * (B_HD ** -0.5), heads(k), heads(v), r, logg


def gla_mixer(pc, px, w_gate, b_gate, g_norm, ctx_out):
    qc, kc, vc, rc, gc = gla_inputs(pc, w_gate, b_gate)
    qx, kx, vx, rx, gx = gla_inputs(px, w_gate, b_gate)
    s_zero = jnp.zeros((px.shape[0], B_HEADS, B_HD, B_HD), F32)
    outs_c, outs_x = [], []
    for d, f in enumerate((_ident, _flip)):
        if ctx_out:
            oc_d, s_ctx = gla_chunked(f(qc), f(kc), f(vc), f(gc[:, :, d]), s_zero)
            outs_c.append(f(oc_d))
        else:
            s_ctx = gla_final_state(f(kc), f(vc), f(gc[:, :, d]))
        ox_d, _ = gla_chunked(f(qx), f(kx), f(vx), f(gx[:, :, d]), s_ctx)
        outs_x.append(f(ox_d))

    def finish(outs, r):
        o = rmsnorm(outs[0] + outs[1], g_norm).astype(r.dtype)
        return o.reshape(r.shape) * jax.nn.silu(r)

    return (finish(outs_c, rc) if ctx_out else None), finish(outs_x, rx)


def s5_discretize(a_re, a_im, log_dt, b_re, b_im):
    lam = lax.complex(a_re.astype(F32), a_im.astype(F32))
    dt = jnp.exp(log_dt.astype(F32))[:, None]
    lam_bar = jnp.exp(lam * dt)
    b = lax.complex(b_re.astype(F32), b_im.astype(F32))
    b_bar = ((lam_bar - 1.0) / lam)[..., None] * b
    return lam_bar, b_bar


def _linear_recurrence(e1, e2):
    a1, b1 = e1
    a2, b2 = e2
    return a1 * a2, a2 * b1 + b2


def s5_scan(u, lam_bar, b_bar, h0=None):
    bu = jnp.einsum('bngi,gpi->bngp', u.astype(jnp.complex64), b_bar)
    a = jnp.broadcast_to(lam_bar, bu.shape)
    a_cum, h = lax.associative_scan(_linear_recurrence, (a, bu), axis=1)
    if h0 is not None:
        h = h + a_cum * h0[:, None]
    return h


def s5_mixer(uc, ux, a_re, a_im, log_dt, b_re, b_im, c_re, c_im, d_skip, w_glu, b_glu, ctx_out):
    def groups(u):
        return u.reshape(u.shape[0], u.shape[1], S5_GROUPS, S5_IN).astype(F32)

    gu_c, gu_x = groups(uc), groups(ux)
    d32 = d_skip.astype(F32)
    ys_c, ys_x = [gu_c * d32], [gu_x * d32]
    for d, f in enumerate((_ident, _flip)):
        lam_bar, b_bar = s5_discretize(a_re[d], a_im[d], log_dt[d], b_re[d], b_im[d])
        cmat = lax.complex(c_re[d].astype(F32), c_im[d].astype(F32))
        hc = s5_scan(f(gu_c), lam_bar, b_bar)
        hx = s5_scan(f(gu_x), lam_bar, b_bar, hc[:, -1])
        ys_x.append(f(jnp.einsum('bngp,gop->bngo', hx, cmat).real))
        if ctx_out:
            ys_c.append(f(jnp.einsum('bngp,gop->bngo', hc, cmat).real))

    def finish(ys, like):
        y = jax.nn.gelu(sum(ys).reshape(like.shape)).astype(like.dtype)
        return y * jax.nn.sigmoid(y @ w_glu + b_glu)

    return (finish(ys_c, uc) if ctx_out else None), finish(ys_x, ux)


def mla_mixer(pc, px, q_norm, w_uq, kv_norm, w_ukv, row, col, ctx_out):
    def split(p):
        return jnp.split(p, [MLA_Q_RANK, MLA_Q_RANK + MLA_KV_RANK], axis=-1)

    def queries(cq, rope):
        b, n, _ = cq.shape
        q = (rmsnorm(cq, q_norm) @ w_uq).reshape(b, n, MLA_HEADS, MLA_NOPE + MLA_ROPE)
        if rope:
            q = jnp.concatenate([q[..., :MLA_NOPE], axial_rope(q[..., MLA_NOPE:], row, col)], axis=-1)
        return q

    def keys_values(ckv, kr, rope):
        b, n, _ = ckv.shape
        kv = (rmsnorm(ckv, kv_norm) @ w_ukv).reshape(b, n, MLA_HEADS, MLA_NOPE + MLA_V)
        kr = kr[:, :, None, :]
        if rope:
            kr = axial_rope(kr, row, col)
        k = jnp.concatenate([kv[..., :MLA_NOPE], jnp.broadcast_to(kr, (b, n, MLA_HEADS, MLA_ROPE))], axis=-1)
        return k, kv[..., MLA_NOPE:]

    cq_c, ckv_c, kr_c = split(pc)
    cq_x, ckv_x, kr_x = split(px)
    kc, vc = keys_values(ckv_c, kr_c, False)
    kx, vx = keys_values(ckv_x, kr_x, True)
    ox = attend(queries(cq_x, True), jnp.concatenate([kc, kx], axis=1), jnp.concatenate([vc, vx], axis=1))
    ox = ox.reshape(px.shape[0], px.shape[1], D_WIDTH)
    if ctx_out:
        oc = attend(queries(cq_c, False), kc, vc).reshape(pc.shape[0], pc.shape[1], D_WIDTH)
        return oc, ox
    return None, ox


def setup_inputs(seed: int = 0) -> dict:
    key = jax.random.key(seed)
    ks = iter(jax.random.split(key, 40))
    L = DEPTH

    def nrm(shape, scale):
        return jax.random.normal(next(ks), shape, F32) * scale

    def gain(shape):
        return 1.0 + nrm(shape, 0.05)

    inputs = {
        'x': nrm((BATCH, SEQ, D_MODEL), 1.0),
        'c': nrm((BATCH, D_MODEL), 1.0),
        'ctx': nrm((BATCH, CTX_LEN, D_MODEL), 1.0),
        'c_ctx': nrm((D_MODEL,), 1.0),
        'w_mod': nrm((L, D_MODEL, 6 * D_MODEL), 0.5 * D_MODEL ** -0.5),
        'b_mod': nrm((L, 6 * D_MODEL), 0.02),
        'g_pre_mix': gain((L, D_MODEL)),
        'g_post_mix': gain((L, D_MODEL)),
        'g_pre_ffn': gain((L, D_MODEL)),
        'g_post_ffn': gain((L, D_MODEL)),
        'w_in': nrm((L, D_MODEL, IN_COLS), D_MODEL ** -0.5),
        'sgu_norm': gain((L, A_HEADS, A_HD)),
        'sgu_w': nrm((L, A_HEADS, MLP_CHUNK, MLP_CHUNK), MLP_CHUNK ** -0.5),
        'sgu_b': 1.0 + nrm((L, A_HEADS, MLP_CHUNK), 0.02),
        'gla_w_gate': nrm((L, 2, GATE_RANK, B_WIDTH), GATE_RANK ** -0.5),
        'gla_b_gate': nrm((L, 2, B_WIDTH), 0.02),
        'gla_norm': gain((L, B_HEADS, B_HD)),
        's5_a_re': -0.5 + nrm((L, 2, S5_GROUPS, S5_STATE), 0.01),
        's5_a_im': math.pi * jnp.arange(S5_STATE, dtype=F32) + nrm((L, 2, S5_GROUPS, S5_STATE), 0.01),
        's5_log_dt': jax.random.uniform(next(ks), (L, 2, S5_GROUPS), F32, math.log(DT_MIN), math.log(DT_MAX)),
        's5_b_re': nrm((L, 2, S5_GROUPS, S5_STATE, S5_IN), (2 * S5_IN) ** -0.5),
        's5_b_im': nrm((L, 2, S5_GROUPS, S5_STATE, S5_IN), (2 * S5_IN) ** -0.5),
        's5_c_re': nrm((L, 2, S5_GROUPS, S5_IN, S5_STATE), (2 * S5_STATE) ** -0.5),
        's5_c_im': nrm((L, 2, S5_GROUPS, S5_IN, S5_STATE), (2 * S5_STATE) ** -0.5),
        's5_d': nrm((L, S5_GROUPS, S5_IN), 1.0),
        's5_w_glu': nrm((L, C_WIDTH, C_WIDTH), C_WIDTH ** -0.5),
        's5_b_glu': nrm((L, C_WIDTH), 0.02),
        'mla_q_norm': gain((L, MLA_Q_RANK)),
        'mla_w_uq': nrm((L, MLA_Q_RANK, MLA_HEADS * (MLA_NOPE + MLA_ROPE)), MLA_Q_RANK ** -0.5),
        'mla_kv_norm': gain((L, MLA_KV_RANK)),
        'mla_w_ukv': nrm((L, MLA_KV_RANK, MLA_HEADS * (MLA_NOPE + MLA_V)), MLA_KV_RANK ** -0.5),
        'w_out': nrm((L, MIX_WIDTH, D_MODEL), MIX_WIDTH ** -0.5),
        'ffn_w_up': nrm((L, D_MODEL, 2 * D_FF), D_MODEL ** -0.5),
        'ffn_conv_w': nrm((L, CONV_W, 2 * D_FF), CONV_W ** -0.5),
        'ffn_conv_b': nrm((L, 2 * D_FF), 0.02),
        'ffn_w_down': nrm((L, D_FF, D_MODEL), D_FF ** -0.5),
    }
    return inputs


def reference(x, c, ctx, c_ctx, w_mod, b_mod, g_pre_mix, g_post_mix, g_pre_ffn, g_post_ffn, w_in,
              sgu_norm, sgu_w, sgu_b, gla_w_gate, gla_b_gate, gla_norm,
              s5_a_re, s5_a_im, s5_log_dt, s5_b_re, s5_b_im, s5_c_re, s5_c_im, s5_d, s5_w_glu, s5_b_glu,
              mla_q_norm, mla_w_uq, mla_kv_norm, mla_w_ukv, w_out,
              ffn_w_up, ffn_conv_w, ffn_conv_b, ffn_w_down):
    n = x.shape[1]
    rows = n // GRID_W
    row = jnp.repeat(jnp.arange(rows, dtype=F32), GRID_W)
    col = jnp.tile(jnp.arange(GRID_W, dtype=F32), rows)
    silu_c = jax.nn.silu(c)
    silu_cc = jax.nn.silu(c_ctx)

    for l in range(DEPTH):
        ctx_out = l < DEPTH - 1
        mod_x = (silu_c @ w_mod[l] + b_mod[l])[:, None, :]
        mod_c = silu_cc @ w_mod[l] + b_mod[l]
        sh1x, sc1x, gt1x, sh2x, sc2x, gt2x = jnp.split(mod_x, 6, axis=-1)
        sh1c, sc1c, gt1c, sh2c, sc2c, gt2c = jnp.split(mod_c, 6, axis=-1)

        hx = rmsnorm(x, g_pre_mix[l]) * (1.0 + sc1x) + sh1x
        hc = rmsnorm(ctx, g_pre_mix[l]) * (1.0 + sc1c) + sh1c
        in_x = jnp.split(hx @ w_in[l], SPLITS, axis=-1)
        in_c = jnp.split(hc @ w_in[l], SPLITS, axis=-1)

        oa_x = chunk_mlp(in_x[0], sgu_norm[l], sgu_w[l], sgu_b[l])
        ob_c, ob_x = gla_mixer(in_c[1], in_x[1], gla_w_gate[l], gla_b_gate[l], gla_norm[l], ctx_out)
        oc_c, oc_x = s5_mixer(in_c[2], in_x[2], s5_a_re[l], s5_a_im[l], s5_log_dt[l], s5_b_re[l], s5_b_im[l],
                              s5_c_re[l], s5_c_im[l], s5_d[l], s5_w_glu[l], s5_b_glu[l], ctx_out)
        od_c, od_x = mla_mixer(in_c[3], in_x[3], mla_q_norm[l], mla_w_uq[l], mla_kv_norm[l], mla_w_ukv[l],
                               row, col, ctx_out)

        mix_x = jnp.concatenate([oa_x, ob_x, oc_x, od_x], axis=-1) @ w_out[l]
        x = x + gt1x * rmsnorm(mix_x, g_post_mix[l])

        hx = rmsnorm(x, g_pre_ffn[l]) * (1.0 + sc2x) + sh2x
        x = x + gt2x * rmsnorm(conv_ffn(hx, ffn_w_up[l], ffn_conv_w[l], ffn_conv_b[l], ffn_w_down[l]), g_post_ffn[l])

        if ctx_out:
            oa_c = chunk_mlp(in_c[0], sgu_norm[l], sgu_w[l], sgu_b[l])
            mix_c = jnp.concatenate([oa_c, ob_c, oc_c, od_c], axis=-1) @ w_out[l]
            ctx = ctx + gt1c * rmsnorm(mix_c, g_post_mix[l])
            hc = rmsnorm(ctx, g_pre_ffn[l]) * (1.0 + sc2c) + sh2c
            ctx = ctx + gt2c * rmsnorm(conv_ffn(hc, ffn_w_up[l], ffn_conv_w[l], ffn_conv_b[l], ffn_w_down[l]), g_post_ffn[l])
    return x
```

```python
import math, os
CUT = int(os.environ.get('K_CUT', '99'))
ASUB = int(os.environ.get('K_ASUB', '99'))
DSUB = int(os.environ.get('K_DSUB', '99'))
DX = int(os.environ.get('K_DX', '99'))
from contextlib import ExitStack
import numpy as np
import ml_dtypes
import concourse.bass as bass
import concourse.mybir as mybir
from concourse.bass_utils import run_bass_kernel_spmd

F32 = mybir.dt.float32
BF16 = mybir.dt.bfloat16
AF = mybir.ActivationFunctionType
ALU = mybir.AluOpType
AX = mybir.AxisListType
AP = bass.AP

D = 1024
NCTX = 256
DEPTH = 2
DFF = 2816
EPS = 1e-6


class _Buf:
    __slots__ = ("w", "r", "ep")

    def __init__(self):
        self.w = None
        self.r = {}
        self.ep = -1


class _Eng:
    def __init__(self, k, name, e, sem):
        self.k, self.name, self.e, self.sem = k, name, e, sem
        self.cnt = 0
        self.seen = {}
        self.prog = []

    def __getattr__(self, op):
        f = getattr(self.e, op)

        def call(*a, **kw):
            return self.k._emit(self, f, a, kw)
        return call


class KB:
    NDMA = 40

    def __init__(self, nc, stack):
        self.nc = nc
        mk = lambda n: stack.enter_context(nc.semaphore(n))
        self.pe = _Eng(self, "pe", nc.tensor, mk("s_pe"))
        self.dve = _Eng(self, "dve", nc.vector, mk("s_dve"))
        self.act = _Eng(self, "act", nc.scalar, mk("s_act"))
        self.pool = _Eng(self, "pool", nc.gpsimd, mk("s_pool"))
        self.sp = _Eng(self, "sp", nc.sync, mk("s_sp"))
        self.engs = [self.pe, self.dve, self.act, self.pool, self.sp]
        self.dsem = [mk("s_d%d" % i) for i in range(self.NDMA)]
        self.dval = [0] * self.NDMA
        self.dnext = 0
        self.bufs = {}
        self.epoch = 0
        self.ninst = 0

    def _buf(self, key):
        b = self.bufs.get(key)
        if b is None:
            b = self.bufs[key] = _Buf()
        if b.ep != self.epoch:
            b.w, b.r, b.ep = None, {}, self.epoch
        return b

    def _need(self, eng, ev, waits):
        sem, val, owner = ev
        if owner is eng and eng is self.pe:
            return
        if eng.seen.get(id(sem), 0) >= val:
            return
        eng.seen[id(sem)] = val
        waits.append((sem, val))

    def _emit(self, eng, f, a, kw, dma=False, wkey=None, rkey=None):
        wk, rk = [], []
        for i, x in enumerate(a):
            if isinstance(x, AP):
                (wk if i == 0 else rk).append(x)
        for n, x in kw.items():
            if isinstance(x, AP):
                (wk if n in ("out", "accum_out") else rk).append(x)

        def key(x, dk):
            if dk is not None and str(x.space) == "DRAM":
                return (x.name, dk)
            return x.name
        wb = [self._buf(key(x, wkey)) for x in wk]
        rb = [self._buf(key(x, rkey)) for x in rk]
        waits = []
        for b in rb:
            if b.w is not None:
                self._need(eng, b.w, waits)
        for b in wb:
            if b.w is not None:
                self._need(eng, b.w, waits)
            for ev in b.r.values():
                self._need(eng, ev, waits)
        if dma:
            i = self.dnext
            self.dnext = (i + 1) % self.NDMA
            if self.dval[i] > 0:
                self._need(eng, (self.dsem[i], self.dval[i], None), waits)
            self.dval[i] += 16
            ev = (self.dsem[i], self.dval[i], None)
            inc = (self.dsem[i], 16)
        else:
            eng.cnt += 1
            ev = (eng.sem, eng.cnt, eng)
            inc = (eng.sem, 1)
        eng.prog.append((waits, f, a, kw, inc))
        self.ninst += 1
        for b in wb:
            b.w = ev
            b.r = {}
        for b in rb:
            if b.w is not ev:
                b.r[id(ev[0])] = ev
        return ev

    def dma(self, out, in_, wkey=None, rkey=None):
        q = self.sp
        return self._emit(q, q.e.dma_start, (), {"out": out, "in_": in_}, dma=True, wkey=wkey, rkey=rkey)

    def barrier(self):
        for e in self.engs:
            waits = []
            for o in self.engs:
                if o.cnt > 0 and not (o is e and e is self.pe):
                    self._need(e, (o.sem, o.cnt, o), waits)
            for i in range(self.NDMA):
                if self.dval[i] > 0:
                    self._need(e, (self.dsem[i], self.dval[i], None), waits)
            if waits:
                e.prog.append((waits, None, (), {}, None))
        self.epoch += 1

    def finish(self):
        self.barrier()
        with self.nc.Block() as block:
            for e, deco in ((self.sp, block.sync), (self.pe, block.tensor), (self.dve, block.vector),
                            (self.act, block.scalar), (self.pool, block.gpsimd)):
                def body(_x, e=e):
                    for waits, f, a, kw, inc in e.prog:
                        for sem, val in waits:
                            e.e.wait_ge(sem, val)
                        if f is not None:
                            f(*a, **kw).then_inc(inc[0], inc[1])
                deco(body)


def _bf(a):
    return np.ascontiguousarray(a).astype(ml_dtypes.bfloat16)


def _rep(row, n=128):
    return np.ascontiguousarray(np.broadcast_to(np.asarray(row, np.float32).reshape(1, -1), (n, row.size)))


def _consts(nx):
    c = {}
    c["identb"] = _bf(np.eye(128, dtype=np.float32))
    c["identf"] = np.eye(128, dtype=np.float32)
    r = np.arange(128)[:, None]
    t = np.arange(128)[None, :]
    s = np.float32(-1.0 / 16.0)
    tri = np.stack([(r <= t), (r > t), (r >= t), (r < t)]).astype(np.float32) * s
    c["tri"] = np.ascontiguousarray(tri.transpose(1, 0, 2))
    mf = (r <= t).astype(np.float32)
    mb = (r >= t).astype(np.float32)
    c["maskf"] = np.ascontiguousarray(np.tile(mf, (1, 4)))
    c["maskb"] = np.ascontiguousarray(np.tile(mb, (1, 4)))
    pos = np.arange(nx)
    row = (pos // 64).astype(np.float32)
    col = (pos % 64).astype(np.float32)
    inv = (10000.0 ** (-np.arange(8, dtype=np.float32) / 8)).astype(np.float32)
    ar = row[:, None] * inv[None, :]
    ac = col[:, None] * inv[None, :]
    cosf = np.concatenate([np.cos(ar), np.cos(ar), np.cos(ac), np.cos(ac)], axis=1).astype(np.float32)
    sinf = np.concatenate([-np.sin(ar), np.sin(ar), -np.sin(ac), np.sin(ac)], axis=1).astype(np.float32)
    c["ropec"] = np.ascontiguousarray(np.tile(cosf, (1, 4)))
    c["ropes"] = np.ascontiguousarray(np.tile(sinf, (1, 4)))
    return c


def _layout_inputs(inp, b, nx):
    L = DEPTH
    m = {}
    m["xin"] = np.ascontiguousarray(inp["x"][b])
    m["cin"] = np.ascontiguousarray(inp["ctx"][b])
    cT = np.stack([inp["c"][b].reshape(8, 128).T, inp["c_ctx"].reshape(8, 128).T], axis=-1)
    m["cT"] = np.ascontiguousarray(cT.astype(np.float32))
    m["w_mod"] = inp["w_mod"]
    m["bmodT"] = np.ascontiguousarray(inp["b_mod"].reshape(L, 48, 128).transpose(0, 2, 1))
    gv = np.stack([inp[n].reshape(L, 8, 128).transpose(0, 2, 1) for n in
                   ("g_pre_mix", "g_post_mix", "g_pre_ffn", "g_post_ffn")], axis=2)
    m["gvT"] = np.ascontiguousarray(gv)
    m["w_in"] = inp["w_in"]
    m["sguwT"] = np.ascontiguousarray(inp["sgu_w"].transpose(0, 3, 1, 2))
    m["sgub"] = np.ascontiguousarray(inp["sgu_b"].transpose(0, 2, 1))
    m["sgun"] = np.stack([_rep(inp["sgu_norm"][l].reshape(-1)) for l in range(L)])
    wg = np.zeros((L, 32, 512), np.float32)
    wg[:, 0:16, 0:256] = inp["gla_w_gate"][:, 0]
    wg[:, 16:32, 256:512] = inp["gla_w_gate"][:, 1]
    m["wg"] = wg
    m["bg"] = np.stack([_rep(inp["gla_b_gate"][l].reshape(-1)) for l in range(L)])
    m["glan"] = np.stack([_rep(inp["gla_norm"][l].reshape(-1)) for l in range(L)])
    def st(a):
        return np.ascontiguousarray(a.reshape(L, 2, 8, 128).transpose(0, 1, 3, 2))
    m["s5are"] = st(inp["s5_a_re"])
    m["s5aim"] = st(inp["s5_a_im"])
    m["s5ldt"] = st(np.ascontiguousarray(np.broadcast_to(inp["s5_log_dt"][..., None], (L, 2, 16, 64))))
    def stb(a):
        return np.ascontiguousarray(a.reshape(L, 2, 8, 128, 16).transpose(0, 1, 3, 2, 4))
    m["s5bre"] = stb(inp["s5_b_re"])
    m["s5bim"] = stb(inp["s5_b_im"])
    def stc(a):
        return np.ascontiguousarray(a.reshape(L, 2, 8, 2, 16, 64).transpose(0, 1, 3, 5, 2, 4).reshape(L, 2, 128, 8, 16))
    m["s5cre"] = stc(inp["s5_c_re"])
    m["s5cim"] = stc(inp["s5_c_im"])
    m["s5dT"] = np.ascontiguousarray(inp["s5_d"].reshape(L, 2, 128).transpose(0, 2, 1))
    m["s5wglu"] = inp["s5_w_glu"]
    m["s5bgluT"] = np.ascontiguousarray(inp["s5_b_glu"].reshape(L, 2, 128).transpose(0, 2, 1))
    m["qn"] = np.stack([_rep(inp["mla_q_norm"][l]) for l in range(L)])
    m["kvn"] = np.stack([_rep(inp["mla_kv_norm"][l]) for l in range(L)])
    m["wuq"] = inp["mla_w_uq"]
    m["wukv"] = inp["mla_w_ukv"]
    m["w_out"] = inp["w_out"]
    m["w_up"] = inp["ffn_w_up"]
    m["convwT"] = np.ascontiguousarray(inp["ffn_conv_w"].reshape(L, 3, 44, 128).transpose(0, 3, 2, 1))
    m["convbT"] = np.ascontiguousarray(inp["ffn_conv_b"].reshape(L, 44, 128).transpose(0, 2, 1))
    m["w_dn"] = inp["ffn_w_down"]
    m.update(_consts(nx))
    return {k: np.ascontiguousarray(v) for k, v in m.items()}


def build(nx, in_shapes, depth=DEPTH, stop_after=None, dbg=False):
    nc = bass.Bass("TRN2", target_bir_lowering=False)
    NT = NCTX + nx
    ntile = NT // 128
    I = {}
    for name, (shape, dt) in in_shapes.items():
        I[name] = nc.dram_tensor(name, list(shape), BF16 if dt == "bf16" else F32, kind="ExternalInput").ap()
    yout = nc.dram_tensor("y", [nx, D], F32, kind="ExternalOutput").ap()
    okind = "ExternalOutput" if dbg else "Internal"

    def scratch(name, shape, dt):
        return nc.dram_tensor(name, shape, dt, kind=okind).ap()
    XS = scratch("XS", [NT, D], F32)
    MIXT = scratch("MIXT", [D, NT], BF16)
    QIT = scratch("QIT", [2, ntile, 64, 512], BF16)
    KVD = scratch("KVD", [2, ntile, 64, 260], F32)
    OACC = scratch("OACC", [NT, 256], F32)
    RSD = scratch("RSD", [NT, 256], F32)
    UT = scratch("UT", [256, NT], F32)
    YF = scratch("YF", [256, NT], F32)
    QTD = scratch("QTD", [4, 128, NT], BF16)
    KTD = scratch("KTD", [4, 128, NT], BF16)
    VAD = scratch("VAD", [NT, 264], BF16)
    WUPS = scratch("WUPS", [8, 128, 2 * DFF], BF16)

    top = ExitStack()
    k = KB(nc, top)
    pe, dve, act, pool = k.pe, k.dve, k.act, k.pool
    uid = [0]

    def sbuf(st, name, shape, dt):
        uid[0] += 1
        return st.enter_context(nc.sbuf_tensor("%s_%d" % (name, uid[0]), list(shape), dt))

    def psum(st, name, shape, dt):
        uid[0] += 1
        return st.enter_context(nc.psum_tensor("%s_%d" % (name, uid[0]), list(shape), dt))

    identb = sbuf(top, "identb", [128, 128], BF16)
    identf = sbuf(top, "identf", [128, 128], F32)
    onesf = sbuf(top, "onesf", [128, 128], F32)
    epsc = sbuf(top, "epsc", [128, 1], F32)
    onec = sbuf(top, "onec", [128, 1], F32)
    n16c = sbuf(top, "n16c", [128, 1], F32)
    cT = sbuf(top, "cT", [128, 8, 2], F32)
    scT = sbuf(top, "scT", [128, 8, 2], F32)
    QMAX = sbuf(top, "QMAX", [128, 4], F32)
    KMAX = sbuf(top, "KMAX", [128, 4], F32)
    GS = sbuf(top, "GS", [128, 2, 8, 2], F32)
    SH = sbuf(top, "SH", [128, 2, 8, 2], F32)
    GTT = sbuf(top, "GTT", [128, 2, 8, 2], F32)
    GB = [[sbuf(top, "GB%d%d" % (s, w), [128, D], F32) for w in range(2)] for s in range(2)]

    k.dma(out=identb[:], in_=I["identb"][:, :])
    k.dma(out=identf[:], in_=I["identf"][:, :])
    k.dma(out=cT[:], in_=I["cT"][:, :, :])
    dve.memset(onesf[:], 1.0)
    dve.memset(epsc[:], EPS)
    dve.memset(onec[:], 1.0)
    dve.memset(n16c[:], -1.0 / 16.0)
    act.activation(scT[:], cT[:], AF.Silu)
    k.dma(out=XS[0:NCTX, :], in_=I["cin"][:, :])
    step = max(128, nx // 8)
    for r0 in range(0, nx, step):
        k.dma(out=XS[NCTX + r0:NCTX + r0 + step, :], in_=I["xin"][r0:r0 + step, :])
    k.barrier()

    def rstd_from_ssq(ssq, out, n, tmp):
        act.activation(tmp, ssq, AF.Sqrt, scale=1.0 / n, bias=epsc[0:tmp.shape[0], 0:1])
        dve.reciprocal(out, tmp)

    def phase0(l):
        with ExitStack() as ph:
            wm = [sbuf(ph, "wm%d" % i, [128, 8, 512], F32) for i in range(2)]
            bmT = sbuf(ph, "bmT", [128, 48], F32)
            gvT = sbuf(ph, "gvT", [128, 4, 8], F32)
            modT = sbuf(ph, "modT", [128, 48, 2], F32)
            tmp1 = sbuf(ph, "tmp1", [128, 8, 2], F32)
            dg = [sbuf(ph, "dg%d" % i, [128, 128], F32) for i in range(2)]
            pm = psum(ph, "pm", [128, 512], F32)
            pbk = [psum(ph, "pbk%d" % i, [128, 512], F32) for i in range(2)]
            k.dma(out=bmT[:], in_=I["bmodT"][l, :, :])
            k.dma(out=gvT[:], in_=I["gvT"][l, :, :, :])
            for blk in range(12):
                w = wm[blk % 2]
                k.dma(out=w[:], in_=I["w_mod"][l, :, blk * 512:(blk + 1) * 512].rearrange("(kc p) n -> p kc n", p=128))
                for j4 in range(4):
                    j = blk * 4 + j4
                    for kc in range(8):
                        pe.matmul(pm[:, 2 * j:2 * j + 2], lhsT=w[:, kc, j4 * 128:(j4 + 1) * 128], rhs=scT[:, kc, :],
                                  start=(kc == 0), stop=(kc == 7))
            dve.tensor_tensor(modT[:], pm[:, 0:96].rearrange("p (j w) -> p j w", w=2),
                              bmT[:].unsqueeze(2).broadcast_to([128, 48, 2]), ALU.add)
            for s in range(2):
                o = 24 * s
                dve.tensor_copy(SH[:, s], modT[:, o:o + 8, :])
                dve.tensor_scalar(tmp1[:], modT[:, o + 8:o + 16, :], 1.0, None, ALU.add)
                dve.tensor_tensor(GS[:, s], tmp1[:], gvT[:, 2 * s, :].unsqueeze(2).broadcast_to([128, 8, 2]), ALU.mult)
                dve.tensor_tensor(GTT[:, s], modT[:, o + 16:o + 24, :],
                                  gvT[:, 2 * s + 1, :].unsqueeze(2).broadcast_to([128, 8, 2]), ALU.mult)
            n = 0
            for s in range(2):
                for w in range(2):
                    for c in range(8):
                        d_ = dg[n % 2]
                        pb = pbk[(c // 4) % 2]
                        dve.tensor_scalar(d_[:], identf[:], GTT[:, s, c, w:w + 1], None, ALU.mult)
                        pe.matmul(pb[:, (c % 4) * 128:(c % 4 + 1) * 128], lhsT=onesf[:], rhs=d_[:], start=True, stop=True)
                        if c % 4 == 3:
                            act.copy(GB[s][w][:, (c // 4) * 512:(c // 4 + 1) * 512], pb[:])
                        n += 1
        k.barrier()

    def phase1(l, ctxout):
        with ExitStack() as ph:
            win = sbuf(ph, "win", [128, 8, 2176], BF16)
            stg = [sbuf(ph, "stg%d" % i, [128, 2176], F32) for i in range(2)]
            for kc in range(8):
                k.dma(out=stg[kc % 2][:], in_=I["w_in"][l, kc * 128:(kc + 1) * 128, :])
                (pool if kc % 2 else dve).tensor_copy(win[:, kc, :], stg[kc % 2][:])
            sgw_f = sbuf(ph, "sgw_f", [128, 4, 128], F32)
            sgw = sbuf(ph, "sgw", [128, 4, 128], BF16)
            sgb = sbuf(ph, "sgb", [128, 4], F32)
            sgn = sbuf(ph, "sgn", [128, 256], F32)
            wg_f = sbuf(ph, "wg_f", [32, 512], F32)
            wgb = sbuf(ph, "wgb", [32, 512], BF16)
            bg = sbuf(ph, "bg", [128, 512], F32)
            tri = sbuf(ph, "tri", [128, 4, 128], F32)
            maskf = sbuf(ph, "maskf", [128, 512], F32)
            maskb = sbuf(ph, "maskb", [128, 512], F32)
            qn = sbuf(ph, "qn", [128, 224], F32)
            kvn = sbuf(ph, "kvn", [128, 96], F32)
            wuq_f = sbuf(ph, "wuq_f", [128, 2, 384], F32)
            wuq = sbuf(ph, "wuq", [128, 2, 384], BF16)
            wukv_f = sbuf(ph, "wukv_f", [128, 512], F32)
            wukv = sbuf(ph, "wukv", [128, 512], BF16)
            k.dma(out=sgw_f[:], in_=I["sguwT"][l, :, :, :])
            k.dma(out=sgb[:], in_=I["sgub"][l, :, :])
            k.dma(out=sgn[:], in_=I["sgun"][l, :, :])
            k.dma(out=wg_f[:], in_=I["wg"][l, :, :])
            k.dma(out=bg[:], in_=I["bg"][l, :, :])
            k.dma(out=tri[:], in_=I["tri"][:, :, :])
            k.dma(out=maskf[:], in_=I["maskf"][:, :])
            k.dma(out=maskb[:], in_=I["maskb"][:, :])
            k.dma(out=qn[:], in_=I["qn"][l, :, :])
            k.dma(out=kvn[:], in_=I["kvn"][l, :, :])
            dve.memset(wuq_f[:], 0.0)
            dve.memset(wukv_f[:], 0.0)
            k.dma(out=wuq_f[:, 0, :], in_=I["wuq"][l, 0:128, :])
            k.dma(out=wuq_f[0:96, 1, :], in_=I["wuq"][l, 128:224, :])
            k.dma(out=wukv_f[0:96, :], in_=I["wukv"][l, :, :])
            dve.tensor_copy(sgw[:], sgw_f[:])
            dve.tensor_copy(wgb[:], wg_f[:])
            dve.tensor_copy(wuq[:], wuq_f[:])
            dve.tensor_copy(wukv[:], wukv_f[:])
            dve.memset(QMAX[:], 0.0)
            dve.memset(KMAX[:], 0.0)

            xt = [sbuf(ph, "xt%d" % i, [128, D], F32) for i in range(2)]
            rc = [sbuf(ph, "rc%d" % i, [128, 128], F32) for i in range(2)]
            rs_ = [sbuf(ph, "rs%d" % i, [128, 128], F32) for i in range(2)]
            junk = sbuf(ph, "junk", [128, D], BF16)
            st4 = sbuf(ph, "st4", [128, 16], F32)
            xsb = sbuf(ph, "xsb", [128, D], BF16)
            hxT = sbuf(ph, "hxT", [128, 8, 128], BF16)
            P = sbuf(ph, "P", [128, 2176], F32)
            GA = sbuf(ph, "GA", [128, 512], F32)
            t256a = sbuf(ph, "t256a", [128, 256], F32)
            t256b = sbuf(ph, "t256b", [128, 256], F32)
            vnb = sbuf(ph, "vnb", [128, 256], BF16)
            oab = sbuf(ph, "oab", [128, 256], BF16)
            mxT = sbuf(ph, "mxT", [128, 2, 128], BF16)
            glb = sbuf(ph, "glb", [128, 32], BF16)
            glT = sbuf(ph, "glT", [32, 128], BF16)
            zb = sbuf(ph, "zb", [128, 512], F32)
            Lg = sbuf(ph, "Lg", [128, 512], F32)
            E1 = sbuf(ph, "E1", [128, 256], F32)
            E2 = sbuf(ph, "E2", [128, 256], F32)
            E3 = sbuf(ph, "E3", [128, 256], F32)
            qin = sbuf(ph, "qin", [128, 256], BF16)
            kin = sbuf(ph, "kin", [128, 256], BF16)
            kend = sbuf(ph, "kend", [128, 256], BF16)
            vb = sbuf(ph, "vb", [128, 256], BF16)
            qkT = [sbuf(ph, "qkT%d" % i, [64, 1024], BF16) for i in range(2)]
            kvd = [sbuf(ph, "kvd%d" % i, [64, 260], F32) for i in range(2)]
            scA = sbuf(ph, "scA", [128, 512], F32)
            scB = sbuf(ph, "scB", [128, 512], F32)
            scS = sbuf(ph, "scS", [128, 512], BF16)
            oin = sbuf(ph, "oin", [128, 256], F32)
            rsl = sbuf(ph, "rsl", [128, 256], F32)
            uTs = sbuf(ph, "uTs", [128, 2, 128], F32)
            cqn = sbuf(ph, "cqn", [128, 320], BF16)
            cTt = sbuf(ph, "cTt", [128, 384], BF16)
            QR = sbuf(ph, "QR", [128, 128], F32)
            KR = sbuf(ph, "KR", [128, 32], F32)
            KR2 = sbuf(ph, "KR2", [128, 32], F32)
            rA = sbuf(ph, "rA", [128, 128], F32)
            rB = sbuf(ph, "rB", [128, 128], F32)
            Qf = sbuf(ph, "Qf", [128, 4, 96], F32)
            Kf = sbuf(ph, "Kf", [128, 4, 96], F32)
            Qb = sbuf(ph, "Qb", [128, 4, 96], BF16)
            Kb = sbuf(ph, "Kb", [128, 4, 96], BF16)
            sq384 = sbuf(ph, "sq384", [128, 384], F32)
            VA = sbuf(ph, "VA", [128, 4, 66], BF16)
            QKT = sbuf(ph, "QKT", [128, 1024], BF16)
            dve.memset(VA[:], 1.0)
            dve.memset(cTt[:], 0.0)
            dve.memset(QKT[:], 0.0)

            tpb = psum(ph, "tpb", [128, 1024], BF16)
            wbb = psum(ph, "wbb", [128, 1024], BF16)
            pin = [psum(ph, "pin%d" % i, [128, 512], F32) for i in range(2)]
            w0 = psum(ph, "w0", [128, 512], F32)
            w1 = psum(ph, "w1", [128, 512], F32)
            w2 = psum(ph, "w2", [128, 512], F32)
            w3 = psum(ph, "w3", [128, 512], F32)

            def load(i):
                r0 = i * 128
                k.dma(out=xt[i % 2][:], in_=XS[r0:r0 + 128, :], rkey=i)
                if i >= 2:
                    p0 = r0 - NCTX
                    k.dma(out=rc[i % 2][:], in_=I["ropec"][p0:p0 + 128, :])
                    k.dma(out=rs_[i % 2][:], in_=I["ropes"][p0:p0 + 128, :])

            load(0)
            for i in range(ntile):
                if i + 1 < ntile:
                    load(i + 1)
                isctx = i < 2
                w = 1 if isctx else 0
                c0 = i * 128
                x_ = xt[i % 2]
                act.activation(junk[:], x_[:], AF.Square, accum_out=st4[:, 0:1])
                rstd_from_ssq(st4[:, 0:1], st4[:, 2:3], D, st4[:, 1:2])
                dve.tensor_scalar(xsb[:], x_[:], st4[:, 2:3], None, ALU.mult)
                for c in range(8):
                    pe.transpose(tpb[:, c * 128:(c + 1) * 128], xsb[:, c * 128:(c + 1) * 128], identb[:])
                for c in range(8):
                    if c % 2 == 0:
                        dve.tensor_scalar(hxT[:, c, :], tpb[:, c * 128:(c + 1) * 128], GS[:, 0, c, w:w + 1],
                                          SH[:, 0, c, w:w + 1], ALU.mult, ALU.add)
                    else:
                        act.activation(hxT[:, c, :], tpb[:, c * 128:(c + 1) * 128], AF.Identity,
                                       bias=SH[:, 0, c, w:w + 1], scale=GS[:, 0, c, w:w + 1])
                for n in range(5):
                    n0 = n * 512
                    nw = min(512, 2176 - n0)
                    pb = pin[n % 2]
                    for kc in range(8):
                        pe.matmul(pb[:, 0:nw], lhsT=hxT[:, kc, :], rhs=win[:, kc, n0:n0 + nw], start=(kc == 0), stop=(kc == 7))
                    if n % 2 == 0:
                        act.copy(P[:, n0:n0 + nw], pb[:, 0:nw])
                    else:
                        dve.tensor_copy(P[:, n0:n0 + nw], pb[:, 0:nw])
                if CUT < 1:
                    continue
                if (not isctx) or ctxout:
                    act.activation(GA[:], P[:, 0:512], AF.Gelu_apprx_tanh)
                    pool.tensor_tensor(t256a[:], GA[:, 256:512], GA[:, 256:512], ALU.mult)
                    dve.tensor_reduce(st4[:, 4:8], t256a[:].rearrange("p (h d) -> p h d", h=4), AX.X, ALU.add)
                    rstd_from_ssq(st4[:, 4:8], st4[:, 12:16], 64, st4[:, 8:12])
                    dve.tensor_tensor(t256b[:].rearrange("p (h d) -> p h d", h=4), GA[:, 256:512].rearrange("p (h d) -> p h d", h=4),
                                      st4[:, 12:16].unsqueeze(2).broadcast_to([128, 4, 64]), ALU.mult)
                    pool.tensor_tensor(vnb[:], t256b[:], sgn[:], ALU.mult)
                    if ASUB < 2:
                        continue
                    for h in range(4):
                        pe.matmul(w0[:, h * 64:(h + 1) * 64], lhsT=sgw[:, h, :], rhs=vnb[:, h * 64:(h + 1) * 64], start=True, stop=True)
                    dve.tensor_tensor(t256a[:].rearrange("p (h d) -> p h d", h=4), w0[:, 0:256].rearrange("p (h d) -> p h d", h=4),
                                      sgb[:].unsqueeze(2).broadcast_to([128, 4, 64]), ALU.add)
                    pool.tensor_tensor(oab[:], t256a[:], GA[:, 0:256], ALU.mult)
                    if ASUB < 3:
                        continue
                    for c in range(2):
                        pe.transpose(wbb[:, c * 128:(c + 1) * 128], oab[:, c * 128:(c + 1) * 128], identb[:])
                    act.copy(mxT[:].rearrange("p c t -> p (c t)"), wbb[:, 0:256])
                    if ASUB < 4:
                        continue
                    for c in range(2):
                        k.dma(out=MIXT[c * 128:(c + 1) * 128, c0:c0 + 128], in_=mxT[:, c, :], wkey=("a", i, c))
                if CUT < 2:
                    continue
                PB = P[:, 512:1568]
                dve.tensor_copy(glb[:], PB[:, 1024:1056])
                pe.transpose(wbb[0:32, 256:384], glb[:], identb[:])
                act.copy(glT[:], wbb[0:32, 256:384])
                pe.matmul(w0[:], lhsT=glT[:], rhs=wgb[:], start=True, stop=True)
                dve.tensor_tensor(zb[:], w0[:], bg[:], ALU.add)
                act.activation(zb[:], zb[:], AF.Exp, scale=-1.0)
                act.activation(Lg[:], zb[:], AF.Ln, bias=onec[:, 0:1])
                pool.tensor_copy(vb[:], PB[:, 512:768])
                act.activation(rsl[:], PB[:, 768:1024], AF.Silu)
                k.dma(out=RSD[c0:c0 + 128, :], in_=rsl[:], wkey=i)
                sps = [w2, w3]
                for d in range(2):
                    Ld = Lg[:, d * 256:(d + 1) * 256]
                    pe.matmul(w1[:, 0:256], lhsT=tri[:, 2 * d, :], rhs=Ld, start=True, stop=True)
                    pe.matmul(w1[:, 256:512], lhsT=tri[:, 2 * d + 1, :], rhs=Ld, start=True, stop=True)
                    act.activation(E1[:], w1[:, 0:256], AF.Exp)
                    act.activation(E2[:], w1[:, 0:256], AF.Exp, scale=-1.0)
                    act.activation(E3[:], w1[:, 256:512], AF.Exp)
                    dve.scalar_tensor_tensor(qin[:], PB[:, 0:256], 0.125, E1[:], ALU.mult, ALU.mult)
                    pool.tensor_tensor(kin[:], PB[:, 256:512], E2[:], ALU.mult)
                    pool.tensor_tensor(kend[:], PB[:, 256:512], E3[:], ALU.mult)
                    qk = qkT[d]
                    for h in range(4):
                        pe.transpose(wbb[0:64, h * 128:(h + 1) * 128], qin[:, h * 64:(h + 1) * 64], identb[:])
                    for h in range(4):
                        pe.transpose(wbb[0:64, 512 + h * 128:512 + (h + 1) * 128], kin[:, h * 64:(h + 1) * 64], identb[:])
                    dve.tensor_copy(qk[:], wbb[0:64, :])
                    k.dma(out=QIT[d, i, :, :], in_=qk[:, 0:512], wkey=(d, i))
                    for h in range(4):
                        pe.matmul(sps[d][:, h * 128:(h + 1) * 128], lhsT=qk[:, 512 + h * 128:512 + (h + 1) * 128],
                                  rhs=qk[:, h * 128:(h + 1) * 128], start=True, stop=True)
                    for h in range(4):
                        pe.matmul(w0[0:64, h * 64:(h + 1) * 64], lhsT=kend[:, h * 64:(h + 1) * 64], rhs=vb[:, h * 64:(h + 1) * 64],
                                  start=True, stop=True)
                    for h in range(4):
                        pe.matmul(w0[0:64, 256 + h:257 + h], lhsT=Lg[:, d * 256 + h * 64:d * 256 + (h + 1) * 64], rhs=n16c[:, 0:1],
                                  start=True, stop=True)
                    dve.tensor_copy(kvd[d][:, 0:256], w0[0:64, 0:256])
                    act.activation(kvd[d][:, 256:260], w0[0:64, 256:260], AF.Exp)
                    k.dma(out=KVD[d, i, :, :], in_=kvd[d][:], wkey=(d, i))
                dve.tensor_tensor(scA[:], w2[:], maskf[:], ALU.mult)
                dve.tensor_tensor(scB[:], w3[:], maskb[:], ALU.mult)
                pool.tensor_tensor(scS[:], scA[:], scB[:], ALU.add)
                for h in range(4):
                    pe.matmul(w1[:, h * 64:(h + 1) * 64], lhsT=scS[:, h * 128:(h + 1) * 128], rhs=vb[:, h * 64:(h + 1) * 64],
                              start=True, stop=True)
                act.copy(oin[:], w1[:, 0:256])
                k.dma(out=OACC[c0:c0 + 128, :], in_=oin[:], wkey=i)
                if CUT < 3:
                    continue
                for c in range(2):
                    pe.transpose(w0[:, c * 128:(c + 1) * 128], P[:, 1568 + c * 128:1568 + (c + 1) * 128], identf[:])
                dve.tensor_copy(uTs[:].rearrange("p c t -> p (c t)"), w0[:, 0:256])
                for c in range(2):
                    k.dma(out=UT[c * 128:(c + 1) * 128, c0:c0 + 128], in_=uTs[:, c, :], wkey=(i, c))
                if CUT < 4:
                    continue
                PD = P[:, 1824:2176]
                act.activation(junk[:, 0:224], PD[:, 0:224], AF.Square, accum_out=st4[:, 4:5])
                act.activation(junk[:, 256:352], PD[:, 224:320], AF.Square, accum_out=st4[:, 5:6])
                act.activation(st4[:, 8:9], st4[:, 4:5], AF.Sqrt, scale=1.0 / 224, bias=epsc[:, 0:1])
                act.activation(st4[:, 9:10], st4[:, 5:6], AF.Sqrt, scale=1.0 / 96, bias=epsc[:, 0:1])
                dve.reciprocal(st4[:, 12:14], st4[:, 8:10])
                dve.scalar_tensor_tensor(cqn[:, 0:224], PD[:, 0:224], st4[:, 12:13], qn[:], ALU.mult, ALU.mult)
                dve.scalar_tensor_tensor(cqn[:, 224:320], PD[:, 224:320], st4[:, 13:14], kvn[:], ALU.mult, ALU.mult)
                pe.transpose(wbb[:, 0:128], cqn[:, 0:128], identb[:])
                pe.transpose(wbb[0:96, 128:256], cqn[:, 128:224], identb[:])
                pe.transpose(wbb[0:96, 256:384], cqn[:, 224:320], identb[:])
                act.copy(cTt[:, 0:128], wbb[:, 0:128])
                act.copy(cTt[0:96, 128:384], wbb[0:96, 128:384])
                if DSUB < 1:
                    continue
                pe.matmul(w1[:, 0:384], lhsT=cTt[:, 0:128], rhs=wuq[:, 0, :], start=True, stop=False)
                pe.matmul(w1[:, 0:384], lhsT=cTt[:, 128:256], rhs=wuq[:, 1, :], start=False, stop=True)
                pe.matmul(w2[:], lhsT=cTt[:, 256:384], rhs=wukv[:], start=True, stop=True)
                q3 = w1[:, 0:384].rearrange("p (h e) -> p h e", h=4)
                kv3 = w2[:].rearrange("p (h e) -> p h e", h=4)
                if DX < 1:
                    continue
                dve.tensor_copy(Qf[:, :, 0:64], q3[:, :, 0:64])
                if DX < 2:
                    continue
                dve.tensor_copy(QR[:].rearrange("p (h e) -> p h e", h=4), q3[:, :, 64:96])
                if DX < 3:
                    continue
                dve.tensor_copy(Kf[:, :, 0:64], kv3[:, :, 0:64])
                if DX < 4:
                    continue
                dve.tensor_copy(VA[:, :, 0:64], kv3[:, :, 64:128])
                if DSUB < 2:
                    continue
                if isctx:
                    pool.tensor_copy(Qf[:, :, 64:96], QR[:].rearrange("p (h e) -> p h e", h=4))
                    pool.tensor_copy(Kf[:, :, 64:96], PD[:, 320:352].unsqueeze(1).broadcast_to([128, 4, 32]))
                else:
                    cs, sn = rc[i % 2], rs_[i % 2]
                    dve.tensor_tensor(rA[:], QR[:], cs[:], ALU.mult)
                    QRv = QR[:].rearrange("p (g a j) -> p g a j", g=8, a=2)
                    snv = sn[:].rearrange("p (g a j) -> p g a j", g=8, a=2)
                    rBv = rB[:].rearrange("p (g a j) -> p g a j", g=8, a=2)
                    pool.tensor_tensor(rBv[:, :, 0, :], QRv[:, :, 1, :], snv[:, :, 0, :], ALU.mult)
                    pool.tensor_tensor(rBv[:, :, 1, :], QRv[:, :, 0, :], snv[:, :, 1, :], ALU.mult)
                    dve.tensor_tensor(Qf[:, :, 64:96], rA[:].rearrange("p (h e) -> p h e", h=4),
                                      rB[:].rearrange("p (h e) -> p h e", h=4), ALU.add)
                    pool.tensor_copy(KR[:], PD[:, 320:352])
                    dve.tensor_tensor(rA[:, 0:32], KR[:], cs[:, 0:32], ALU.mult)
                    KRv = KR[:].rearrange("p (g a j) -> p g a j", g=2, a=2)
                    sn2 = sn[:, 0:32].rearrange("p (g a j) -> p g a j", g=2, a=2)
                    rB2 = rB[:, 0:32].rearrange("p (g a j) -> p g a j", g=2, a=2)
                    pool.tensor_tensor(rB2[:, :, 0, :], KRv[:, :, 1, :], sn2[:, :, 0, :], ALU.mult)
                    pool.tensor_tensor(rB2[:, :, 1, :], KRv[:, :, 0, :], sn2[:, :, 1, :], ALU.mult)
                    dve.tensor_tensor(KR2[:], rA[:, 0:32], rB[:, 0:32], ALU.add)
                    pool.tensor_copy(Kf[:, :, 64:96], KR2[:].unsqueeze(1).broadcast_to([128, 4, 32]))
                if DSUB < 3:
                    continue
                dve.tensor_copy(Qb[:], Qf[:])
                pool.tensor_copy(Kb[:], Kf[:])
                if not isctx or ctxout:
                    pool.tensor_tensor(sq384[:], Qf[:].rearrange("p h e -> p (h e)"), Qf[:].rearrange("p h e -> p (h e)"), ALU.mult)
                    dve.tensor_reduce(st4[:, 4:8], sq384[:].rearrange("p (h e) -> p h e", h=4), AX.X, ALU.add)
                    dve.tensor_tensor(QMAX[:], QMAX[:], st4[:, 4:8], ALU.max)
                pool.tensor_tensor(sq384[:], Kf[:].rearrange("p h e -> p (h e)"), Kf[:].rearrange("p h e -> p (h e)"), ALU.mult)
                dve.tensor_reduce(st4[:, 8:12], sq384[:].rearrange("p (h e) -> p h e", h=4), AX.X, ALU.add)
                dve.tensor_tensor(KMAX[:], KMAX[:], st4[:, 8:12], ALU.max)
                if DSUB < 4:
                    continue
                for h in range(4):
                    pe.transpose(wbb[0:96, h * 128:(h + 1) * 128], Qb[:, h, :], identb[:])
                for h in range(4):
                    pe.transpose(wbb[0:96, 512 + h * 128:512 + (h + 1) * 128], Kb[:, h, :], identb[:])
                act.copy(QKT[0:96, :], wbb[0:96, :])
                if DSUB < 5:
                    continue
                for h in range(4):
                    k.dma(out=QTD[h, :, c0:c0 + 128], in_=QKT[:, h * 128:(h + 1) * 128], wkey=(i, h))
                    k.dma(out=KTD[h, :, c0:c0 + 128], in_=QKT[:, 512 + h * 128:512 + (h + 1) * 128], wkey=(i, h))
                k.dma(out=VAD[c0:c0 + 128, :], in_=VA[:].rearrange("p h e -> p (h e)"), wkey=i)
        k.barrier()

    def phase2(l, ctxout):
        with ExitStack() as ph:
            S = sbuf(ph, "S", [64, 256], F32)
            Sb = sbuf(ph, "Sb", [64, 256], BF16)
            gln = sbuf(ph, "gln", [128, 256], F32)
            qit = [sbuf(ph, "qit%d" % i, [64, 512], BF16) for i in range(2)]
            kvd = [sbuf(ph, "kvd%d" % i, [64, 260], F32) for i in range(2)]
            oac = [sbuf(ph, "oac%d" % i, [128, 256], F32) for i in range(2)]
            rsl = [sbuf(ph, "rsl%d" % i, [128, 256], F32) for i in range(2)]
            osum = sbuf(ph, "osum", [128, 256], F32)
            t1 = sbuf(ph, "t1", [128, 256], F32)
            t2 = sbuf(ph, "t2", [128, 256], F32)
            st4 = sbuf(ph, "st4", [128, 16], F32)
            obb = sbuf(ph, "obb", [128, 256], BF16)
            mxT = sbuf(ph, "mxT", [128, 2, 128], BF16)
            ops_ = [psum(ph, "ops%d" % i, [128, 512], F32) for i in range(2)]
            wbb = psum(ph, "wbb", [128, 1024], BF16)
            k.dma(out=gln[:], in_=I["glan"][l, :, :])
            for d in range(2):
                order = list(range(ntile)) if d == 0 else [1, 0] + list(range(ntile - 1, 1, -1))
                dve.memset(S[:], 0.0)

                def load(n):
                    i = order[n]
                    k.dma(out=qit[n % 2][:], in_=QIT[d, i, :, :], rkey=(d, i))
                    k.dma(out=kvd[n % 2][:], in_=KVD[d, i, :, :], rkey=(d, i))
                    k.dma(out=oac[n % 2][:], in_=OACC[i * 128:(i + 1) * 128, :], rkey=i)
                    if d == 1:
                        k.dma(out=rsl[n % 2][:], in_=RSD[i * 128:(i + 1) * 128, :], rkey=i)
                load(0)
                for n, i in enumerate(order):
                    if n + 1 < ntile:
                        load(n + 1)
                    need_o = (i >= 2) or ctxout
                    q_, kv_, oa_ = qit[n % 2], kvd[n % 2], oac[n % 2]
                    if need_o:
                        pool.tensor_copy(Sb[:], S[:])
                        op = ops_[n % 2]
                        for h in range(4):
                            pe.matmul(op[:, h * 64:(h + 1) * 64], lhsT=q_[:, h * 128:(h + 1) * 128], rhs=Sb[:, h * 64:(h + 1) * 64],
                                      start=True, stop=True)
                        dve.tensor_tensor(osum[:], op[:, 0:256], oa_[:], ALU.add)
                        if d == 0:
                            k.dma(out=OACC[i * 128:(i + 1) * 128, :], in_=osum[:], wkey=i)
                        else:
                            pool.tensor_tensor(t1[:], osum[:], osum[:], ALU.mult)
                            dve.tensor_reduce(st4[:, 0:4], t1[:].rearrange("p (h e) -> p h e", h=4), AX.X, ALU.add)
                            rstd_from_ssq(st4[:, 0:4], st4[:, 8:12], 64, st4[:, 4:8])
                            dve.tensor_tensor(t2[:].rearrange("p (h e) -> p h e", h=4), osum[:].rearrange("p (h e) -> p h e", h=4),
                                              st4[:, 8:12].unsqueeze(2).broadcast_to([128, 4, 64]), ALU.mult)
                            pool.tensor_tensor(t1[:], t2[:], gln[:], ALU.mult)
                            dve.tensor_tensor(obb[:], t1[:], rsl[n % 2][:], ALU.mult)
                            for c in range(2):
                                pe.transpose(wbb[:, c * 128:(c + 1) * 128], obb[:, c * 128:(c + 1) * 128], identb[:])
                            act.copy(mxT[:].rearrange("p c t -> p (c t)"), wbb[:, 0:256])
                            for c in range(2):
                                k.dma(out=MIXT[256 + c * 128:256 + (c + 1) * 128, i * 128:(i + 1) * 128], in_=mxT[:, c, :], wkey=("b", i, c))
                    dve.tensor_tensor(S[:].rearrange("p (h e) -> p h e", h=4), S[:].rearrange("p (h e) -> p h e", h=4),
                                      kv_[:, 256:260].unsqueeze(2).broadcast_to([64, 4, 64]), ALU.mult)
                    dve.tensor_tensor(S[:], S[:], kv_[:, 0:256], ALU.add)
                k.barrier()

    def phase3(l, ctxout):
        LC = 512
        with ExitStack() as ph:
            pr = sbuf(ph, "pr", [128, 2, 3, 8], F32)
            bre = sbuf(ph, "bre", [128, 2, 8, 16], F32)
            bim = sbuf(ph, "bim", [128, 2, 8, 16], F32)
            cre = sbuf(ph, "cre", [128, 2, 8, 16], F32)
            cim = sbuf(ph, "cim", [128, 2, 8, 16], F32)
            dT = sbuf(ph, "dT", [128, 2], F32)
            bgl = sbuf(ph, "bgl", [128, 2], F32)
            wgl_f = sbuf(ph, "wgl_f", [128, 2, 256], F32)
            wgl = sbuf(ph, "wgl", [128, 2, 256], BF16)
            for d in range(2):
                k.dma(out=pr[:, d, 0, :], in_=I["s5are"][l, d, :, :])
                k.dma(out=pr[:, d, 1, :], in_=I["s5aim"][l, d, :, :])
                k.dma(out=pr[:, d, 2, :], in_=I["s5ldt"][l, d, :, :])
                k.dma(out=bre[:, d], in_=I["s5bre"][l, d, :, :, :])
                k.dma(out=bim[:, d], in_=I["s5bim"][l, d, :, :, :])
                k.dma(out=cre[:, d], in_=I["s5cre"][l, d, :, :, :])
                k.dma(out=cim[:, d], in_=I["s5cim"][l, d, :, :, :])
            k.dma(out=dT[:], in_=I["s5dT"][l, :, :])
            k.dma(out=bgl[:], in_=I["s5bgluT"][l, :, :])
            k.dma(out=wgl_f[:], in_=I["s5wglu"][l, :, :].rearrange("(c p) n -> p c n", p=128))
            dve.tensor_copy(wgl[:], wgl_f[:])
            e = sbuf(ph, "e", [128, 2, 16, 8], F32)
            dve.memset(e[:], 0.0)
            A_RE, A_IM, LDT = pr[:, :, 0, :], pr[:, :, 1, :], pr[:, :, 2, :]
            DT, AR, TH, RM, S16, S8, C8, T0, T1, LR, LI = [e[:, :, n, :] for n in range(11)]
            act.activation(DT, LDT, AF.Exp)
            dve.tensor_tensor(AR, A_RE, DT, ALU.mult)
            dve.tensor_tensor(TH, A_IM, DT, ALU.mult)
            act.activation(RM, AR, AF.Exp)
            act.activation(S16, TH, AF.Sin, scale=1.0 / 16)
            act.activation(S8, TH, AF.Sin, scale=1.0 / 8)
            dve.tensor_tensor(T0, S16, S16, ALU.mult)
            dve.tensor_scalar(C8, T0, -2.0, 1.0, ALU.mult, ALU.add)
            cc, ss = C8, S8
            for it in range(3):
                dve.tensor_tensor(T0, cc, cc, ALU.mult)
                dve.tensor_tensor(T1, ss, ss, ALU.mult)
                dve.tensor_tensor(LI, cc, ss, ALU.mult)
                dve.tensor_tensor(T0, T0, T1, ALU.subtract)
                dve.tensor_scalar(S8, LI, 2.0, None, ALU.mult)
                dve.tensor_copy(C8, T0)
                cc, ss = C8, S8
            CT, ST = C8, S8
            dve.tensor_tensor(LR, RM, CT, ALU.mult)
            dve.tensor_tensor(LI, RM, ST, ALU.mult)
            NR, DEN, CR, CI, T2 = [e[:, :, n, :] for n in range(11, 16)]
            dve.tensor_scalar(NR, LR, -1.0, None, ALU.add)
            dve.tensor_tensor(T0, A_RE, A_RE, ALU.mult)
            dve.tensor_tensor(T1, A_IM, A_IM, ALU.mult)
            dve.tensor_tensor(DEN, T0, T1, ALU.add)
            dve.reciprocal(DEN, DEN)
            dve.tensor_tensor(T0, NR, A_RE, ALU.mult)
            dve.tensor_tensor(T1, LI, A_IM, ALU.mult)
            dve.tensor_tensor(T0, T0, T1, ALU.add)
            dve.tensor_tensor(CR, T0, DEN, ALU.mult)
            dve.tensor_tensor(T0, LI, A_RE, ALU.mult)
            dve.tensor_tensor(T1, NR, A_IM, ALU.mult)
            dve.tensor_tensor(T0, T0, T1, ALU.subtract)
            dve.tensor_tensor(CI, T0, DEN, ALU.mult)
            bbr = sbuf(ph, "bbr", [128, 2, 8, 16], F32)
            bbi = sbuf(ph, "bbi", [128, 2, 8, 16], F32)
            tb = sbuf(ph, "tb", [128, 2, 8, 16], F32)
            for d in range(2):
                crb = e[:, d, 13, :].unsqueeze(2).broadcast_to([128, 8, 16])
                cib = e[:, d, 14, :].unsqueeze(2).broadcast_to([128, 8, 16])
                dve.tensor_tensor(bbr[:, d], bre[:, d], crb, ALU.mult)
                dve.tensor_tensor(tb[:, d], bim[:, d], cib, ALU.mult)
                dve.tensor_tensor(bbr[:, d], bbr[:, d], tb[:, d], ALU.subtract)
                dve.tensor_tensor(bbi[:, d], bim[:, d], crb, ALU.mult)
                dve.tensor_tensor(tb[:, d], bre[:, d], cib, ALU.mult)
                dve.tensor_tensor(bbi[:, d], bbi[:, d], tb[:, d], ALU.add)
            BT = sbuf(ph, "BT", [128, 2, 2, 8, 128], BF16)
            CP = sbuf(ph, "CP", [128, 2, 2, 8, 128], BF16)
            Z = [sbuf(ph, "Z%d" % i, [128, 128], F32) for i in range(2)]
            pz = [psum(ph, "pz%d" % i, [128, 512], F32) for i in range(2)]
            dve.memset(CP[:], 0.0)
            dve.memset(Z[0][:], 0.0)
            dve.memset(Z[1][:], 0.0)
            n = 0
            for d in range(2):
                for ri, src in enumerate((bbr, bbi)):
                    for j in range(8):
                        jj = j % 4
                        z = Z[n % 2]
                        if jj != (j - 1) % 4 or True:
                            pool.memset(z[:], 0.0)
                        pool.tensor_copy(z[0:64, 32 * jj:32 * jj + 16], src[0:64, d, j, :])
                        pool.tensor_copy(z[64:128, 32 * jj + 16:32 * jj + 32], src[64:128, d, j, :])
                        pe.transpose(pz[n % 2][:, 0:128], z[:], identf[:])
                        act.copy(BT[:, d, ri, j, :], pz[n % 2][:, 0:128])
                        n += 1
                for j in range(8):
                    jj = j % 4
                    dve.tensor_copy(CP[0:64, d, 0, j, 32 * jj:32 * jj + 16], cre[0:64, d, j, :])
                    dve.tensor_copy(CP[64:128, d, 0, j, 32 * jj + 16:32 * jj + 32], cre[64:128, d, j, :])
                    dve.tensor_scalar(CP[0:64, d, 1, j, 32 * jj:32 * jj + 16], cim[0:64, d, j, :], -1.0, None, ALU.mult)
                    dve.tensor_scalar(CP[64:128, d, 1, j, 32 * jj + 16:32 * jj + 32], cim[64:128, d, j, :], -1.0, None, ALU.mult)
            RR = sbuf(ph, "RR", [128, 2, 8, LC], F32)
            RI = sbuf(ph, "RI", [128, 2, 8, LC], F32)
            tt = sbuf(ph, "tt", [128, LC], F32)
            for d in range(2):
                for j in range(8):
                    dve.tensor_copy(RR[:, d, j, 0:1], e[:, d, 6, j:j + 1])
                    dve.tensor_copy(RI[:, d, j, 0:1], e[:, d, 5, j:j + 1])
                    wdt = 1
                    while wdt < LC:
                        cw = RR[:, d, j, wdt - 1:wdt]
                        sw = RI[:, d, j, wdt - 1:wdt]
                        a_r, a_i = RR[:, d, j, 0:wdt], RI[:, d, j, 0:wdt]
                        dve.tensor_scalar(tt[:, 0:wdt], a_i, sw, None, ALU.mult)
                        dve.scalar_tensor_tensor(RR[:, d, j, wdt:2 * wdt], a_r, cw, tt[:, 0:wdt], ALU.mult, ALU.subtract)
                        dve.tensor_scalar(tt[:, 0:wdt], a_i, cw, None, ALU.mult)
                        dve.scalar_tensor_tensor(RI[:, d, j, wdt:2 * wdt], a_r, sw, tt[:, 0:wdt], ALU.mult, ALU.add)
                        wdt *= 2
            uTf = [sbuf(ph, "uTf%d" % i, [128, 2, LC], F32) for i in range(2)]
            uTb = sbuf(ph, "uTb", [128, 2, LC], BF16)
            yfl = [sbuf(ph, "yfl%d" % i, [128, 2, LC], F32) for i in range(2)]
            ta = sbuf(ph, "ta", [128, LC], F32)
            tb2 = sbuf(ph, "tb2", [128, LC], F32)
            btr = sbuf(ph, "btr", [128, LC], F32)
            bti = sbuf(ph, "bti", [128, LC], F32)
            wr = sbuf(ph, "wr", [128, LC], F32)
            wi = sbuf(ph, "wi", [128, LC], F32)
            PR = [sbuf(ph, "PR0", [128, 4, 4, LC], BF16)] * 2
            h0 = sbuf(ph, "h0", [128, 2, 8], F32)
            hc = sbuf(ph, "hc", [128, 4], F32)
            ysum = sbuf(ph, "ysum", [128, 2, LC], F32)
            ygf = sbuf(ph, "ygf", [128, 2, LC], F32)
            ygb = sbuf(ph, "ygb", [128, 2, LC], BF16)
            sg = sbuf(ph, "sg", [128, LC], F32)
            ocb = sbuf(ph, "ocb", [128, 2, LC], BF16)
            pbr = psum(ph, "pbr", [128, 512], F32)
            pbi = psum(ph, "pbi", [128, 512], F32)
            py = [psum(ph, "py%d" % i, [128, 512], F32) for i in range(2)]
            pg = psum(ph, "pg", [128, 512], F32)
            chunks = [(0, NCTX)] + [(NCTX + c * LC, min(LC, nx - c * LC)) for c in range((nx + LC - 1) // LC)]
            for d in range(2):
                order = list(range(len(chunks))) if d == 0 else [0] + list(range(len(chunks) - 1, 0, -1))
                dve.memset(h0[:], 0.0)

                def load(n):
                    t0, Lc = chunks[order[n]]
                    k.dma(out=uTf[n % 2][:, :, 0:Lc], in_=UT[:, t0:t0 + Lc].rearrange("(c p) t -> p c t", p=128), rkey=None)
                    if d == 1:
                        k.dma(out=yfl[n % 2][:, :, 0:Lc], in_=YF[:, t0:t0 + Lc].rearrange("(c p) t -> p c t", p=128), rkey=None)
                load(0)
                for n, ci in enumerate(order):
                    if n + 1 < len(order):
                        load(n + 1)
                    t0, Lc = chunks[ci]
                    need_y = (ci > 0) or ctxout
                    uf = uTf[n % 2]
                    pool.tensor_copy(uTb[:, :, 0:Lc], uf[:, :, 0:Lc])

                    def tv(ap):
                        return ap if d == 0 else ap[:, ::-1]
                    for j in range(8):
                        jj = j % 4
                        Rr = tv(RR[:, d, j, 0:Lc])
                        Ri = tv(RI[:, d, j, 0:Lc])
                        pe.matmul(pbr[:, 0:Lc], lhsT=BT[:, d, 0, j, :], rhs=uTb[:, j // 4, 0:Lc], start=True, stop=True)
                        pe.matmul(pbi[:, 0:Lc], lhsT=BT[:, d, 1, j, :], rhs=uTb[:, j // 4, 0:Lc], start=True, stop=True)
                        dve.tensor_tensor(ta[:, 0:Lc], pbr[:, 0:Lc], Rr, ALU.mult)
                        dve.tensor_tensor(tb2[:, 0:Lc], pbi[:, 0:Lc], Ri, ALU.mult)
                        pool.tensor_tensor(btr[:, 0:Lc], ta[:, 0:Lc], tb2[:, 0:Lc], ALU.add)
                        dve.tensor_tensor(ta[:, 0:Lc], pbi[:, 0:Lc], Rr, ALU.mult)
                        dve.tensor_tensor(tb2[:, 0:Lc], pbr[:, 0:Lc], Ri, ALU.mult)
                        pool.tensor_tensor(bti[:, 0:Lc], ta[:, 0:Lc], tb2[:, 0:Lc], ALU.subtract)
                        rm = e[:, d, 3, j:j + 1].broadcast_to([128, Lc])
                        dve.tensor_tensor_scan(tv(wr[:, 0:Lc]), rm, tv(btr[:, 0:Lc]), h0[:, 0, j:j + 1], ALU.mult, ALU.add)
                        dve.tensor_tensor_scan(tv(wi[:, 0:Lc]), rm, tv(bti[:, 0:Lc]), h0[:, 1, j:j + 1], ALU.mult, ALU.add)
                        tl = Lc - 1 if d == 0 else 0
                        rl = RR[:, d, j, Lc - 1:Lc]
                        il = RI[:, d, j, Lc - 1:Lc]
                        dve.tensor_scalar(hc[:, 0:1], wi[:, tl:tl + 1], il, None, ALU.mult)
                        dve.tensor_scalar(hc[:, 1:2], wr[:, tl:tl + 1], il, None, ALU.mult)
                        dve.scalar_tensor_tensor(h0[:, 0, j:j + 1], wr[:, tl:tl + 1], rl, hc[:, 0:1], ALU.mult, ALU.subtract)
                        dve.scalar_tensor_tensor(h0[:, 1, j:j + 1], wi[:, tl:tl + 1], rl, hc[:, 1:2], ALU.mult, ALU.add)
                        if need_y:
                            pp = PR[(j // 4) % 2]
                            pool.tensor_tensor(pp[:, jj, 0, 0:Lc], wr[:, 0:Lc], Rr, ALU.mult)
                            dve.scalar_tensor_tensor(pp[:, jj, 1, 0:Lc], wi[:, 0:Lc], -1.0, Ri, ALU.mult, ALU.mult)
                            pool.tensor_tensor(pp[:, jj, 2, 0:Lc], wi[:, 0:Lc], Rr, ALU.mult)
                            dve.tensor_tensor(pp[:, jj, 3, 0:Lc], wr[:, 0:Lc], Ri, ALU.mult)
                            if jj == 3:
                                c = j // 4
                                for j2 in range(4):
                                    for pi in range(4):
                                        pe.matmul(py[c][:, 0:Lc], lhsT=CP[:, d, pi // 2, 4 * c + j2, :], rhs=pp[:, j2, pi, 0:Lc],
                                                  start=(j2 == 0 and pi == 0), stop=(j2 == 3 and pi == 3))
                    if need_y:
                        if d == 0:
                            for c in range(2):
                                act.copy(ysum[:, c, 0:Lc], py[c][:, 0:Lc])
                            for c in range(2):
                                k.dma(out=YF[c * 128:(c + 1) * 128, t0:t0 + Lc], in_=ysum[:, c, 0:Lc], wkey=(ci, c))
                        else:
                            for c in range(2):
                                dve.tensor_tensor(ysum[:, c, 0:Lc], py[c][:, 0:Lc], yfl[n % 2][:, c, 0:Lc], ALU.add)
                                dve.scalar_tensor_tensor(ysum[:, c, 0:Lc], uf[:, c, 0:Lc], dT[:, c:c + 1], ysum[:, c, 0:Lc], ALU.mult, ALU.add)
                                act.activation(ygf[:, c, 0:Lc], ysum[:, c, 0:Lc], AF.Gelu_apprx_tanh)
                                pool.tensor_copy(ygb[:, c, 0:Lc], ygf[:, c, 0:Lc])
                            for c2 in range(2):
                                for c in range(2):
                                    pe.matmul(pg[:, 0:Lc], lhsT=wgl[:, c, c2 * 128:(c2 + 1) * 128], rhs=ygb[:, c, 0:Lc],
                                              start=(c == 0), stop=(c == 1))
                                act.activation(sg[:, 0:Lc], pg[:, 0:Lc], AF.Sigmoid, bias=bgl[:, c2:c2 + 1])
                                dve.tensor_tensor(ocb[:, c2, 0:Lc], ygf[:, c2, 0:Lc], sg[:, 0:Lc], ALU.mult)
                            for c in range(2):
                                k.dma(out=MIXT[512 + c * 128:512 + (c + 1) * 128, t0:t0 + Lc], in_=ocb[:, c, 0:Lc], wkey=("c", ci, c))
                k.barrier()

    def phase4(l, ctxout):
        scale = 96.0 ** -0.5
        nkc = ntile
        with ExitStack() as ph:
            KT = sbuf(ph, "KT", [128, 4, NT], BF16)
            VAs = sbuf(ph, "VAs", [128, nkc, 264], BF16)
            QTb = [sbuf(ph, "QTb%d" % i, [128, 512], BF16) for i in range(2)]
            Pe = [sbuf(ph, "Pe%d" % i, [128, 1024], BF16) for i in range(2)]
            negb = sbuf(ph, "negb", [128, 1], F32)
            m2 = sbuf(ph, "m2", [128, 2], F32)
            m2t = sbuf(ph, "m2t", [2, 2], F32)
            rrow = sbuf(ph, "rrow", [128, 512], F32)
            bcs = sbuf(ph, "bcs", [64, 512], F32)
            odb = [sbuf(ph, "odb%d" % i, [64, 512], BF16) for i in range(2)]
            sp_ = [psum(ph, "sps%d" % i, [128, 1024], F32) for i in range(2)]
            acc = [psum(ph, "acc%d" % i, [128, 512], F32) for i in range(2)]
            pbc = psum(ph, "pbc", [128, 512], F32)
            for h in range(4):
                k.dma(out=KT[:, h, :], in_=KTD[h, :, :])
            k.dma(out=VAs[:], in_=VAD[:, :].rearrange("(kc p) c -> p kc c", p=128))
            dve.tensor_reduce(m2[:, 0:1], QMAX[:], AX.X, ALU.max)
            dve.tensor_reduce(m2[:, 1:2], KMAX[:], AX.X, ALU.max)
            pe.transpose(pbc[0:2, 0:128], m2[:], identf[:])
            dve.tensor_reduce(m2t[:, 0:1], pbc[0:2, 0:128], AX.X, ALU.max)
            act.activation(m2t[:, 1:2], m2t[:, 0:1], AF.Ln)
            pe.matmul(pbc[:, 256:257], lhsT=onesf[0:2, :], rhs=m2t[:, 1:2], start=True, stop=True)
            act.activation(negb[:], pbc[:, 256:257], AF.Exp, scale=0.5)
            dve.tensor_scalar(negb[:], negb[:], -scale, None, ALU.mult)
            jobs = []
            for h in range(4):
                if ctxout:
                    jobs.append((h, 0, NCTX, [0, 1]))
                for qb in range(nx // 512):
                    jobs.append((h, NCTX + qb * 512, 512, list(range(nkc))))

            def load(n):
                h, q0, qw, _ = jobs[n]
                k.dma(out=QTb[n % 2][:, 0:qw], in_=QTD[h, :, q0:q0 + qw])
            load(0)
            for n, (h, q0, qw, kcs) in enumerate(jobs):
                if n + 1 < len(jobs):
                    load(n + 1)
                Q = QTb[n % 2]
                ac = acc[n % 2]
                for pi in range(0, len(kcs), 2):
                    s_ = sp_[(pi // 2) % 2]
                    P_ = Pe[(pi // 2) % 2]
                    for u in range(2):
                        kc = kcs[pi + u]
                        pe.matmul(s_[:, u * 512:u * 512 + qw], lhsT=KT[:, h, kc * 128:(kc + 1) * 128], rhs=Q[:, 0:qw], start=True, stop=True)
                    if qw == 512:
                        act.activation(P_[:], s_[:], AF.Exp, bias=negb[:, 0:1], scale=scale)
                    else:
                        for u in range(2):
                            act.activation(P_[:, u * 512:u * 512 + qw], s_[:, u * 512:u * 512 + qw], AF.Exp, bias=negb[:, 0:1], scale=scale)
                    for u in range(2):
                        kc = kcs[pi + u]
                        pe.matmul(ac[0:65, 0:qw], lhsT=VAs[:, kc, h * 66:h * 66 + 65], rhs=P_[:, u * 512:u * 512 + qw],
                                  start=(pi == 0 and u == 0), stop=(pi + u == len(kcs) - 1))
                dve.reciprocal(rrow[64:65, 0:qw], ac[64:65, 0:qw])
                pe.matmul(pbc[0:64, 0:qw], lhsT=onesf[64:65, 0:64], rhs=rrow[64:65, 0:qw], start=True, stop=True)
                act.copy(bcs[:, 0:qw], pbc[0:64, 0:qw])
                dve.tensor_tensor(odb[n % 2][:, 0:qw], ac[0:64, 0:qw], bcs[:, 0:qw], ALU.mult)
                k.dma(out=MIXT[768 + 64 * h:768 + 64 * (h + 1), q0:q0 + qw], in_=odb[n % 2][:, 0:qw], wkey=("d", n))
        k.barrier()

    def epilogue(po, x_, gb, xo, st4, junk, tmpf):
        act.activation(junk[:], po[:], AF.Square, accum_out=st4[:, 0:1])
        rstd_from_ssq(st4[:, 0:1], st4[:, 2:3], D, st4[:, 1:2])
        dve.scalar_tensor_tensor(tmpf[:], po[:], st4[:, 2:3], gb[:], ALU.mult, ALU.mult)
        pool.tensor_tensor(xo[:], tmpf[:], x_[:], ALU.add)

    def load_cast_rows(dst, src_rows, nk, ncols, stg):
        for kc in range(nk):
            s_ = stg[kc % 2]
            k.dma(out=s_[:, 0:ncols], in_=src_rows[kc * 128:(kc + 1) * 128, :])
            (pool if kc % 2 else dve).tensor_copy(dst[:, kc, :], s_[:, 0:ncols])

    def phase5(l, ctxout):
        with ExitStack() as ph:
            wo = sbuf(ph, "wo", [128, 8, D], BF16)
            stg = [sbuf(ph, "stg%d" % i, [128, D], F32) for i in range(2)]
            load_cast_rows(wo, I["w_out"][l], 8, D, stg)
            mx = [sbuf(ph, "mx%d" % i, [128, 8, 128], BF16) for i in range(2)]
            xt = [sbuf(ph, "xt%d" % i, [128, D], F32) for i in range(2)]
            xo = [sbuf(ph, "xo%d" % i, [128, D], F32) for i in range(2)]
            junk = sbuf(ph, "junk", [128, D], BF16)
            tmpf = sbuf(ph, "tmpf", [128, D], F32)
            st4 = sbuf(ph, "st4", [128, 4], F32)
            po = [psum(ph, "po%d" % i, [128, 1024], F32) for i in range(2)]
            tiles = list(range(0 if ctxout else 2, ntile))

            def load(n):
                i = tiles[n]
                k.dma(out=mx[n % 2][:], in_=MIXT[:, i * 128:(i + 1) * 128].rearrange("(c p) t -> p c t", p=128))
                k.dma(out=xt[n % 2][:], in_=XS[i * 128:(i + 1) * 128, :], rkey=i)
            load(0)
            for n, i in enumerate(tiles):
                if n + 1 < len(tiles):
                    load(n + 1)
                p_ = po[n % 2]
                for nn in range(2):
                    for kc in range(8):
                        pe.matmul(p_[:, nn * 512:(nn + 1) * 512], lhsT=mx[n % 2][:, kc, :], rhs=wo[:, kc, nn * 512:(nn + 1) * 512],
                                  start=(kc == 0), stop=(kc == 7))
                epilogue(p_, xt[n % 2], GB[0][1 if i < 2 else 0], xo[n % 2], st4, junk, tmpf)
                k.dma(out=XS[i * 128:(i + 1) * 128, :], in_=xo[n % 2][:], wkey=i)
        k.barrier()

    def phase6(l, ctxout, last):
        OWN = 510
        with ExitStack() as ph:
            wdn = sbuf(ph, "wdn", [128, 22, D], BF16)
            cw = sbuf(ph, "cw", [128, 44, 3], F32)
            cb = sbuf(ph, "cb", [128, 44], F32)
            k.dma(out=cw[:], in_=I["convwT"][l, :, :, :])
            k.dma(out=cb[:], in_=I["convbT"][l, :, :])
            with ExitStack() as pp_:
                stg = [sbuf(pp_, "stg%d" % i, [128, DFF], F32) for i in range(2)]
                cbf = [sbuf(pp_, "cbf%d" % i, [128, DFF], BF16) for i in range(2)]
                load_cast_rows(wdn, I["w_dn"][l], 22, D, stg)
                n = 0
                for kc in range(8):
                    for half in range(2):
                        s_, c_ = stg[n % 2], cbf[n % 2]
                        k.dma(out=s_[:], in_=I["w_up"][l, kc * 128:(kc + 1) * 128, half * DFF:(half + 1) * DFF])
                        (pool if n % 2 else dve).tensor_copy(c_[:], s_[:])
                        k.dma(out=WUPS[kc, :, half * DFF:(half + 1) * DFF], in_=c_[:], wkey=(kc, half))
                        n += 1
                k.barrier()
            NW = 4
            wst = [sbuf(ph, "wst%d" % i, [128, 8, 256], BF16) for i in range(NW)]
            xb = [sbuf(ph, "xb%d" % i, [128, 4, D], F32) for i in range(2)]
            xsb = sbuf(ph, "xsb", [128, D], BF16)
            junk = sbuf(ph, "junk", [128, D], BF16)
            hxT = sbuf(ph, "hxT", [128, 8, 512], BF16)
            hid = sbuf(ph, "hid", [128, 22, 512], BF16)
            ca = sbuf(ph, "ca", [128, 512], F32)
            cg = sbuf(ph, "cg", [128, 512], F32)
            ga = sbuf(ph, "ga", [128, 512], F32)
            st4 = sbuf(ph, "st4", [128, 4], F32)
            tmpf = sbuf(ph, "tmpf", [128, D], F32)
            xo = [sbuf(ph, "xo%d" % i, [128, D], F32) for i in range(2)]
            tpb = psum(ph, "tpb", [128, 1024], BF16)
            pz = [psum(ph, "pz%d" % i, [128, 512], F32) for i in range(4)]
            po = psum(ph, "po", [128, 1024], F32)
            pool.memset(hid[:], 0.0)
            segs = ([(0, NCTX, 1)] if ctxout else []) + [(NCTX, nx, 0)]
            blocks = []
            for (s0, sl, w) in segs:
                for b0 in range(0, sl, OWN):
                    blocks.append((s0, sl, w, b0, min(OWN, sl - b0)))
            nwl = [0]
            total_w = len(blocks) * 22

            def loadw():
                if nwl[0] < total_w:
                    i_ = nwl[0] % 22
                    for hf in range(2):
                        k.dma(out=wst[nwl[0] % NW][:, :, hf * 128:(hf + 1) * 128],
                              in_=WUPS[:, :, hf * DFF + i_ * 128:hf * DFF + (i_ + 1) * 128].rearrange("kc p c -> p kc c"))
                    nwl[0] += 1

            def load(n):
                s0, sl, w, b0, own = blocks[n]
                X = xb[n % 2]
                lo, hi = b0 - 1, b0 - 1 + 512
                vlo, vhi = max(lo, 0), min(hi, sl)
                for s in range(4):
                    a, b_ = lo + 128 * s, lo + 128 * (s + 1)
                    va, vb_ = max(a, vlo), min(b_, vhi)
                    if va >= vb_:
                        pool.memset(X[:, s, :], 0.0)
                        continue
                    if va > a or vb_ < b_:
                        pool.memset(X[:, s, :], 0.0)
                    k.dma(out=X[va - a:vb_ - a, s, :], in_=XS[s0 + va:s0 + vb_, :])
            load(0)
            for _ in range(NW - 1):
                loadw()
            nuse = 0
            for n, (s0, sl, w, b0, own) in enumerate(blocks):
                if n + 1 < len(blocks):
                    load(n + 1)
                X = xb[n % 2]
                lo = b0 - 1
                vlo, vhi = max(lo, 0), min(lo + 512, sl)
                for s in range(4):
                    act.activation(junk[:], X[:, s, :], AF.Square, accum_out=st4[:, 0:1])
                    rstd_from_ssq(st4[:, 0:1], st4[:, 2:3], D, st4[:, 1:2])
                    dve.tensor_scalar(xsb[:], X[:, s, :], st4[:, 2:3], None, ALU.mult)
                    for c in range(8):
                        pe.transpose(tpb[:, c * 128:(c + 1) * 128], xsb[:, c * 128:(c + 1) * 128], identb[:])
                    for c in range(8):
                        if c % 2 == 0:
                            dve.tensor_scalar(hxT[:, c, s * 128:(s + 1) * 128], tpb[:, c * 128:(c + 1) * 128], GS[:, 1, c, w:w + 1],
                                              SH[:, 1, c, w:w + 1], ALU.mult, ALU.add)
                        else:
                            act.activation(hxT[:, c, s * 128:(s + 1) * 128], tpb[:, c * 128:(c + 1) * 128], AF.Identity,
                                           bias=SH[:, 1, c, w:w + 1], scale=GS[:, 1, c, w:w + 1])
                if vlo - lo > 0:
                    pool.memset(hxT[:, :, 0:vlo - lo], 0.0)
                if vhi - lo < 512:
                    pool.memset(hxT[:, :, vhi - lo:512], 0.0)
                for i in range(22):
                    loadw()
                    wt = wst[nuse % NW]
                    nuse += 1
                    pa, pg_ = pz[(2 * i) % 4], pz[(2 * i + 1) % 4]
                    for (pp, hf) in ((pa, 0), (pg_, 1)):
                        for kc in range(8):
                            pe.matmul(pp[:], lhsT=wt[:, kc, hf * 128:(hf + 1) * 128], rhs=hxT[:, kc, :], start=(kc == 0), stop=(kc == 7))
                    for (pp, f, cc_) in ((pa, i, ca), (pg_, 22 + i, cg)):
                        act.activation(cc_[:, 1:511], pp[:, 1:511], AF.Identity, bias=cb[:, f:f + 1], scale=cw[:, f, 1:2])
                        dve.scalar_tensor_tensor(cc_[:, 1:511], pp[:, 0:510], cw[:, f, 0:1], cc_[:, 1:511], ALU.mult, ALU.add)
                        dve.scalar_tensor_tensor(cc_[:, 1:511], pp[:, 2:512], cw[:, f, 2:3], cc_[:, 1:511], ALU.mult, ALU.add)
                    act.activation(ga[:, 1:511], ca[:, 1:511], AF.Gelu_apprx_tanh)
                    pool.tensor_tensor(hid[:, i, 1:511], ga[:, 1:511], cg[:, 1:511], ALU.mult)
                for s in range(4):
                    ca_, cb_ = max(1, 128 * s), min(own + 1, 128 * (s + 1))
                    if ca_ >= cb_:
                        continue
                    for nn in range(2):
                        for i in range(22):
                            pe.matmul(po[:, nn * 512:(nn + 1) * 512], lhsT=hid[:, i, s * 128:(s + 1) * 128], rhs=wdn[:, i, nn * 512:(nn + 1) * 512],
                                      start=(i == 0), stop=(i == 21))
                    xo_ = xo[s % 2]
                    epilogue(po, X[:, s, :], GB[1][w], xo_, st4, junk, tmpf)
                    r0 = lo + ca_
                    p0, p1 = ca_ - 128 * s, cb_ - 128 * s
                    if last and w == 0:
                        k.dma(out=yout[r0:r0 + (p1 - p0), :], in_=xo_[p0:p1, :], wkey=("y", n, s))
                    else:
                        k.dma(out=XS[s0 + r0:s0 + r0 + (p1 - p0), :], in_=xo_[p0:p1, :], wkey=("o", n, s))
        k.barrier()

    phases = []
    for l in range(depth):
        ctxout = l < depth - 1
        last = l == depth - 1
        phases += [("p0", lambda l=l: phase0(l)), ("p1", lambda l=l, c=ctxout: phase1(l, c)),
                   ("p2", lambda l=l, c=ctxout: phase2(l, c)), ("p3", lambda l=l, c=ctxout: phase3(l, c)),
                   ("p4", lambda l=l, c=ctxout: phase4(l, c)), ("p5", lambda l=l, c=ctxout: phase5(l, c)),
                   ("p6", lambda l=l, c=ctxout, la=last: phase6(l, c, la))]
    for n, (nm, fn) in enumerate(phases):
        fn()
        if stop_after is not None and n + 1 >= stop_after:
            break
    k.finish()
    top.close()
    return nc, k


_CACHE = {}


def _shapes(m):
    return {k_: (v.shape, "bf16" if v.dtype == ml_dtypes.bfloat16 else "f32") for k_, v in m.items()}


def run(inputs, nx, batches, stop_after=None, dbg=False, depth=DEPTH):
    inputs = {k_: np.asarray(v) for k_, v in inputs.items()}
    maps = [_layout_inputs(inputs, b, nx) for b in batches]
    key = (nx, stop_after, dbg, depth)
    if key not in _CACHE:
        _CACHE[key] = build(nx, _shapes(maps[0]), depth=depth, stop_after=stop_after, dbg=dbg)
    nc, kb = _CACHE[key]
    res = run_bass_kernel_spmd(nc, maps, core_ids=list(range(len(maps))))
    return res


def kernel(**inputs):
    nx = inputs["x"].shape[1]
    nb = inputs["x"].shape[0]
    batches = [i % nb for i in range(8)]
    res = run(inputs, nx, batches)
    return np.stack([np.asarray(res.results[b]["y"], dtype=np.float32) for b in range(nb)], axis=0)
```

```python
import math, os
CUT = int(os.environ.get('K_CUT', '99'))
ASUB = int(os.environ.get('K_ASUB', '99'))
DSUB = int(os.environ.get('K_DSUB', '99'))
DX = int(os.environ.get('K_DX', '99'))
from contextlib import ExitStack
import numpy as np
import ml_dtypes
import concourse.bass as bass
import concourse.mybir as mybir
from concourse.bass_utils import run_bass_kernel_spmd

F32 = mybir.dt.float32
BF16 = mybir.dt.bfloat16
AF = mybir.ActivationFunctionType
ALU = mybir.AluOpType
AX = mybir.AxisListType
AP = bass.AP

D = 1024
NCTX = 256
DEPTH = 2
DFF = 2816
EPS = 1e-6


class _Buf:
    __slots__ = ("w", "r", "ep")

    def __init__(self):
        self.w = None
        self.r = {}
        self.ep = -1


class _Eng:
    def __init__(self, k, name, e, sem):
        self.k, self.name, self.e, self.sem = k, name, e, sem
        self.cnt = 0
        self.seen = {}
        self.prog = []

    def __getattr__(self, op):
        f = getattr(self.e, op)

        def call(*a, **kw):
            return self.k._emit(self, f, a, kw)
        return call


class KB:
    NDMA = 40

    def __init__(self, nc, stack):
        self.nc = nc
        mk = lambda n: stack.enter_context(nc.semaphore(n))
        self.pe = _Eng(self, "pe", nc.tensor, mk("s_pe"))
        self.dve = _Eng(self, "dve", nc.vector, mk("s_dve"))
        self.act = _Eng(self, "act", nc.scalar, mk("s_act"))
        self.pool = _Eng(self, "pool", nc.gpsimd, mk("s_pool"))
        self.sp = _Eng(self, "sp", nc.sync, mk("s_sp"))
        self.engs = [self.pe, self.dve, self.act, self.pool, self.sp]
        self.dsem = [mk("s_d%d" % i) for i in range(self.NDMA)]
        self.dval = [0] * self.NDMA
        self.dnext = 0
        self.bufs = {}
        self.epoch = 0
        self.ninst = 0

    def _buf(self, key):
        b = self.bufs.get(key)
        if b is None:
            b = self.bufs[key] = _Buf()
        if b.ep != self.epoch:
            b.w, b.r, b.ep = None, {}, self.epoch
        return b

    def _need(self, eng, ev, waits):
        sem, val, owner = ev
        if owner is eng and eng is self.pe:
            return
        if eng.seen.get(id(sem), 0) >= val:
            return
        eng.seen[id(sem)] = val
        waits.append((sem, val))

    def _emit(self, eng, f, a, kw, dma=False, wkey=None, rkey=None):
        wk, rk = [], []
        for i, x in enumerate(a):
            if isinstance(x, AP):
                (wk if i == 0 else rk).append(x)
        for n, x in kw.items():
            if isinstance(x, AP):
                (wk if n in ("out", "accum_out") else rk).append(x)

        def key(x, dk):
            if dk is not None and str(x.space) == "DRAM":
                return (x.name, dk)
            return x.name
        wb = [self._buf(key(x, wkey)) for x in wk]
        rb = [self._buf(key(x, rkey)) for x in rk]
        waits = []
        for b in rb:
            if b.w is not None:
                self._need(eng, b.w, waits)
        for b in wb:
            if b.w is not None:
                self._need(eng, b.w, waits)
            for ev in b.r.values():
                self._need(eng, ev, waits)
        if dma:
            i = self.dnext
            self.dnext = (i + 1) % self.NDMA
            if self.dval[i] > 0:
                self._need(eng, (self.dsem[i], self.dval[i], None), waits)
            self.dval[i] += 16
            ev = (self.dsem[i], self.dval[i], None)
            inc = (self.dsem[i], 16)
        else:
            eng.cnt += 1
            ev = (eng.sem, eng.cnt, eng)
            inc = (eng.sem, 1)
        eng.prog.append((waits, f, a, kw, inc))
        self.ninst += 1
        for b in wb:
            b.w = ev
            b.r = {}
        for b in rb:
            if b.w is not ev:
                b.r[id(ev[0])] = ev
        return ev

    def dma(self, out, in_, wkey=None, rkey=None):
        q = self.sp
        return self._emit(q, q.e.dma_start, (), {"out": out, "in_": in_}, dma=True, wkey=wkey, rkey=rkey)

    def barrier(self):
        for e in self.engs:
            waits = []
            for o in self.engs:
                if o.cnt > 0 and not (o is e and e is self.pe):
                    self._need(e, (o.sem, o.cnt, o), waits)
            for i in range(self.NDMA):
                if self.dval[i] > 0:
                    self._need(e, (self.dsem[i], self.dval[i], None), waits)
            if waits:
                e.prog.append((waits, None, (), {}, None))
        self.epoch += 1

    def finish(self):
        self.barrier()
        with self.nc.Block() as block:
            for e, deco in ((self.sp, block.sync), (self.pe, block.tensor), (self.dve, block.vector),
                            (self.act, block.scalar), (self.pool, block.gpsimd)):
                def body(_x, e=e):
                    for waits, f, a, kw, inc in e.prog:
                        for sem, val in waits:
                            e.e.wait_ge(sem, val)
                        if f is not None:
                            f(*a, **kw).then_inc(inc[0], inc[1])
                deco(body)


def _bf(a):
    return np.ascontiguousarray(a).astype(ml_dtypes.bfloat16)


def _rep(row, n=128):
    return np.ascontiguousarray(np.broadcast_to(np.asarray(row, np.float32).reshape(1, -1), (n, row.size)))


def _consts(nx):
    c = {}
    c["identb"] = _bf(np.eye(128, dtype=np.float32))
    c["identf"] = np.eye(128, dtype=np.float32)
    r = np.arange(128)[:, None]
    t = np.arange(128)[None, :]
    s = np.float32(-1.0 / 16.0)
    tri = np.stack([(r <= t), (r > t), (r >= t), (r < t)]).astype(np.float32) * s
    c["tri"] = np.ascontiguousarray(tri.transpose(1, 0, 2))
    mf = (r <= t).astype(np.float32)
    mb = (r >= t).astype(np.float32)
    c["maskf"] = np.ascontiguousarray(np.tile(mf, (1, 4)))
    c["maskb"] = np.ascontiguousarray(np.tile(mb, (1, 4)))
    pos = np.arange(nx)
    row = (pos // 64).astype(np.float32)
    col = (pos % 64).astype(np.float32)
    inv = (10000.0 ** (-np.arange(8, dtype=np.float32) / 8)).astype(np.float32)
    ar = row[:, None] * inv[None, :]
    ac = col[:, None] * inv[None, :]
    cosf = np.concatenate([np.cos(ar), np.cos(ar), np.cos(ac), np.cos(ac)], axis=1).astype(np.float32)
    sinf = np.concatenate([-np.sin(ar), np.sin(ar), -np.sin(ac), np.sin(ac)], axis=1).astype(np.float32)
    c["ropec"] = np.ascontiguousarray(np.tile(cosf, (1, 4)))
    c["ropes"] = np.ascontiguousarray(np.tile(sinf, (1, 4)))
    return c


def _layout_inputs(inp, b, nx, flip=False):
    L = DEPTH
    m = {}
    if flip:
        inp = dict(inp)
        inp["x"] = inp["x"][:, ::-1]
        inp["ctx"] = inp["ctx"][:, ::-1]
        wi = inp["w_in"].copy()
        wi[:, :, 1536:1552] = inp["w_in"][:, :, 1552:1568]
        wi[:, :, 1552:1568] = inp["w_in"][:, :, 1536:1552]
        inp["w_in"] = wi
        inp["gla_w_gate"] = inp["gla_w_gate"][:, ::-1]
        inp["gla_b_gate"] = inp["gla_b_gate"][:, ::-1]
        inp["sgu_w"] = inp["sgu_w"][:, :, ::-1, ::-1]
        inp["sgu_b"] = inp["sgu_b"][:, :, ::-1]
        for n_ in ("s5_a_re", "s5_a_im", "s5_log_dt", "s5_b_re", "s5_b_im", "s5_c_re", "s5_c_im"):
            inp[n_] = np.ascontiguousarray(inp[n_][:, ::-1])
        inp["ffn_conv_w"] = inp["ffn_conv_w"][:, ::-1]
    m["xin"] = np.ascontiguousarray(inp["x"][b])
    m["cin"] = np.ascontiguousarray(inp["ctx"][b])
    cT = np.stack([inp["c"][b].reshape(8, 128).T, inp["c_ctx"].reshape(8, 128).T], axis=-1)
    m["cT"] = np.ascontiguousarray(cT.astype(np.float32))
    m["w_mod"] = inp["w_mod"]
    m["bmodT"] = np.ascontiguousarray(inp["b_mod"].reshape(L, 48, 128).transpose(0, 2, 1))
    gv = np.stack([inp[n].reshape(L, 8, 128).transpose(0, 2, 1) for n in
                   ("g_pre_mix", "g_post_mix", "g_pre_ffn", "g_post_ffn")], axis=2)
    m["gvT"] = np.ascontiguousarray(gv)
    m["w_in"] = inp["w_in"]
    m["sguwT"] = np.ascontiguousarray(inp["sgu_w"].transpose(0, 3, 1, 2))
    m["sgub"] = np.ascontiguousarray(inp["sgu_b"].transpose(0, 2, 1))
    m["sgun"] = np.stack([_rep(inp["sgu_norm"][l].reshape(-1)) for l in range(L)])
    wg = np.zeros((L, 32, 512), np.float32)
    wg[:, 0:16, 0:256] = inp["gla_w_gate"][:, 0]
    wg[:, 16:32, 256:512] = inp["gla_w_gate"][:, 1]
    m["wg"] = wg
    m["bg"] = np.stack([_rep(inp["gla_b_gate"][l].reshape(-1)) for l in range(L)])
    m["glan"] = np.stack([_rep(inp["gla_norm"][l].reshape(-1)) for l in range(L)])
    def st(a):
        return np.ascontiguousarray(a.reshape(L, 2, 8, 128).transpose(0, 1, 3, 2))
    m["s5are"] = st(inp["s5_a_re"])
    m["s5aim"] = st(inp["s5_a_im"])
    m["s5ldt"] = st(np.ascontiguousarray(np.broadcast_to(inp["s5_log_dt"][..., None], (L, 2, 16, 64))))
    def stb(a):
        return np.ascontiguousarray(a.reshape(L, 2, 8, 128, 16).transpose(0, 1, 3, 2, 4))
    m["s5bre"] = stb(inp["s5_b_re"])
    m["s5bim"] = stb(inp["s5_b_im"])
    def stc(a):
        return np.ascontiguousarray(a.reshape(L, 2, 8, 2, 16, 64).transpose(0, 1, 3, 5, 2, 4).reshape(L, 2, 128, 8, 16))
    m["s5cre"] = stc(inp["s5_c_re"])
    m["s5cim"] = stc(inp["s5_c_im"])
    m["s5dT"] = np.ascontiguousarray(inp["s5_d"].reshape(L, 2, 128).transpose(0, 2, 1))
    m["s5wglu"] = inp["s5_w_glu"]
    m["s5bgluT"] = np.ascontiguousarray(inp["s5_b_glu"].reshape(L, 2, 128).transpose(0, 2, 1))
    m["qn"] = np.stack([_rep(inp["mla_q_norm"][l]) for l in range(L)])
    m["kvn"] = np.stack([_rep(inp["mla_kv_norm"][l]) for l in range(L)])
    m["wuq"] = inp["mla_w_uq"]
    m["wukv"] = inp["mla_w_ukv"]
    m["w_out"] = inp["w_out"]
    m["w_up"] = inp["ffn_w_up"]
    m["convwT"] = np.ascontiguousarray(inp["ffn_conv_w"].reshape(L, 3, 44, 128).transpose(0, 3, 2, 1))
    m["convbT"] = np.ascontiguousarray(inp["ffn_conv_b"].reshape(L, 44, 128).transpose(0, 2, 1))
    m["w_dn"] = inp["ffn_w_down"]
    m.update(_consts(nx))
    if flip:
        m["ropec"] = np.ascontiguousarray(m["ropec"][::-1])
        m["ropes"] = np.ascontiguousarray(m["ropes"][::-1])
    return {k: np.ascontiguousarray(v) for k, v in m.items()}


def build(nx, in_shapes, depth=DEPTH, stop_after=None, dbg=False, split=True):
    nc = bass.Bass("TRN2", target_bir_lowering=False)
    NT = NCTX + nx
    ntile = NT // 128
    I = {}
    for name, (shape, dt) in in_shapes.items():
        I[name] = nc.dram_tensor(name, list(shape), BF16 if dt == "bf16" else F32, kind="ExternalInput").ap()
    H = nx // 2 if split else nx
    yout = nc.dram_tensor("y", [H, D], F32, kind="ExternalOutput").ap()
    okind = "ExternalOutput" if dbg else "Internal"

    def scratch(name, shape, dt):
        return nc.dram_tensor(name, shape, dt, kind=okind).ap()
    XS = scratch("XS", [NT, D], F32)
    MIXT = scratch("MIXT", [D, NT], BF16)
    QIT = scratch("QIT", [2, ntile, 64, 512], BF16)
    KVD = scratch("KVD", [2, ntile, 64, 260], F32)
    OACC = scratch("OACC", [NT, 256], F32)
    RSD = scratch("RSD", [NT, 256], F32)
    UT = scratch("UT", [256, NT], F32)
    YF = scratch("YF", [256, NT], F32)
    QTD = scratch("QTD", [4, 128, NT], BF16)
    KTD = scratch("KTD", [4, 128, NT], BF16)
    VAD = scratch("VAD", [NT, 264], BF16)
    WUPS = scratch("WUPS", [22, 128, 2048], BF16)

    top = ExitStack()
    k = KB(nc, top)
    pe, dve, act, pool = k.pe, k.dve, k.act, k.pool
    uid = [0]

    def sbuf(st, name, shape, dt):
        uid[0] += 1
        return st.enter_context(nc.sbuf_tensor("%s_%d" % (name, uid[0]), list(shape), dt))

    def psum(st, name, shape, dt):
        uid[0] += 1
        return st.enter_context(nc.psum_tensor("%s_%d" % (name, uid[0]), list(shape), dt))

    identb = sbuf(top, "identb", [128, 128], BF16)
    identf = sbuf(top, "identf", [128, 128], F32)
    onesf = sbuf(top, "onesf", [128, 128], F32)
    epsc = sbuf(top, "epsc", [128, 1], F32)
    onec = sbuf(top, "onec", [128, 1], F32)
    n16c = sbuf(top, "n16c", [128, 1], F32)
    cT = sbuf(top, "cT", [128, 8, 2], F32)
    scT = sbuf(top, "scT", [128, 8, 2], F32)
    QMAX = sbuf(top, "QMAX", [128, 4], F32)
    KMAX = sbuf(top, "KMAX", [128, 4], F32)
    GS = sbuf(top, "GS", [128, 2, 8, 2], F32)
    SH = sbuf(top, "SH", [128, 2, 8, 2], F32)
    GTT = sbuf(top, "GTT", [128, 2, 8, 2], F32)
    GB = [[sbuf(top, "GB%d%d" % (s, w), [128, D], F32) for w in range(2)] for s in range(2)]

    k.dma(out=identb[:], in_=I["identb"][:, :])
    k.dma(out=identf[:], in_=I["identf"][:, :])
    k.dma(out=cT[:], in_=I["cT"][:, :, :])
    dve.memset(onesf[:], 1.0)
    dve.memset(epsc[:], EPS)
    dve.memset(onec[:], 1.0)
    dve.memset(n16c[:], -1.0 / 16.0)
    act.activation(scT[:], cT[:], AF.Silu)
    k.dma(out=XS[0:NCTX, :], in_=I["cin"][:, :])
    step = max(128, nx // 8)
    for r0 in range(0, nx, step):
        k.dma(out=XS[NCTX + r0:NCTX + r0 + step, :], in_=I["xin"][r0:r0 + step, :])
    k.barrier()

    def rstd_from_ssq(ssq, out, n, tmp):
        act.activation(tmp, ssq, AF.Sqrt, scale=1.0 / n, bias=epsc[0:tmp.shape[0], 0:1])
        dve.reciprocal(out, tmp)

    def phase0(l):
        with ExitStack() as ph:
            wm = [sbuf(ph, "wm%d" % i, [128, 8, 512], F32) for i in range(2)]
            bmT = sbuf(ph, "bmT", [128, 48], F32)
            gvT = sbuf(ph, "gvT", [128, 4, 8], F32)
            modT = sbuf(ph, "modT", [128, 48, 2], F32)
            tmp1 = sbuf(ph, "tmp1", [128, 8, 2], F32)
            dg = [sbuf(ph, "dg%d" % i, [128, 128], F32) for i in range(2)]
            pm = psum(ph, "pm", [128, 512], F32)
            pbk = [psum(ph, "pbk%d" % i, [128, 512], F32) for i in range(2)]
            k.dma(out=bmT[:], in_=I["bmodT"][l, :, :])
            k.dma(out=gvT[:], in_=I["gvT"][l, :, :, :])
            for blk in range(12):
                w = wm[blk % 2]
                k.dma(out=w[:], in_=I["w_mod"][l, :, blk * 512:(blk + 1) * 512].rearrange("(kc p) n -> p kc n", p=128))
                for j4 in range(4):
                    j = blk * 4 + j4
                    for kc in range(8):
                        pe.matmul(pm[:, 2 * j:2 * j + 2], lhsT=w[:, kc, j4 * 128:(j4 + 1) * 128], rhs=scT[:, kc, :],
                                  start=(kc == 0), stop=(kc == 7))
            dve.tensor_tensor(modT[:], pm[:, 0:96].rearrange("p (j w) -> p j w", w=2),
                              bmT[:].unsqueeze(2).broadcast_to([128, 48, 2]), ALU.add)
            for s in range(2):
                o = 24 * s
                dve.tensor_copy(SH[:, s], modT[:, o:o + 8, :])
                dve.tensor_scalar(tmp1[:], modT[:, o + 8:o + 16, :], 1.0, None, ALU.add)
                dve.tensor_tensor(GS[:, s], tmp1[:], gvT[:, 2 * s, :].unsqueeze(2).broadcast_to([128, 8, 2]), ALU.mult)
                dve.tensor_tensor(GTT[:, s], modT[:, o + 16:o + 24, :],
                                  gvT[:, 2 * s + 1, :].unsqueeze(2).broadcast_to([128, 8, 2]), ALU.mult)
            n = 0
            for s in range(2):
                for w in range(2):
                    for c in range(8):
                        d_ = dg[n % 2]
                        pb = pbk[(c // 4) % 2]
                        dve.tensor_scalar(d_[:], identf[:], GTT[:, s, c, w:w + 1], None, ALU.mult)
                        pe.matmul(pb[:, (c % 4) * 128:(c % 4 + 1) * 128], lhsT=onesf[:], rhs=d_[:], start=True, stop=True)
                        if c % 4 == 3:
                            act.copy(GB[s][w][:, (c // 4) * 512:(c // 4 + 1) * 512], pb[:])
                        n += 1
        k.barrier()

    def phase1(l, ctxout, last=False):
        OWNM = (H + 512) if (last and split) else nx
        with ExitStack() as ph:
            win = sbuf(ph, "win", [128, 8, 2176], BF16)
            stg = [sbuf(ph, "stg%d" % i, [128, 2176], F32) for i in range(2)]
            for kc in range(8):
                k.dma(out=stg[kc % 2][:], in_=I["w_in"][l, kc * 128:(kc + 1) * 128, :])
                (pool if kc % 2 else dve).tensor_copy(win[:, kc, :], stg[kc % 2][:])
            sgw_f = sbuf(ph, "sgw_f", [128, 4, 128], F32)
            sgw = sbuf(ph, "sgw", [128, 4, 128], BF16)
            sgb = sbuf(ph, "sgb", [128, 4], F32)
            sgn = sbuf(ph, "sgn", [128, 256], F32)
            wg_f = sbuf(ph, "wg_f", [32, 512], F32)
            wgb = sbuf(ph, "wgb", [32, 512], BF16)
            bg = sbuf(ph, "bg", [128, 512], F32)
            tri = sbuf(ph, "tri", [128, 4, 128], F32)
            maskf = sbuf(ph, "maskf", [128, 512], F32)
            maskb = sbuf(ph, "maskb", [128, 512], F32)
            qn = sbuf(ph, "qn", [128, 224], F32)
            kvn = sbuf(ph, "kvn", [128, 96], F32)
            wuq_f = sbuf(ph, "wuq_f", [128, 2, 384], F32)
            wuq = sbuf(ph, "wuq", [128, 2, 384], BF16)
            wukv_f = sbuf(ph, "wukv_f", [128, 512], F32)
            wukv = sbuf(ph, "wukv", [128, 512], BF16)
            k.dma(out=sgw_f[:], in_=I["sguwT"][l, :, :, :])
            k.dma(out=sgb[:], in_=I["sgub"][l, :, :])
            k.dma(out=sgn[:], in_=I["sgun"][l, :, :])
            k.dma(out=wg_f[:], in_=I["wg"][l, :, :])
            k.dma(out=bg[:], in_=I["bg"][l, :, :])
            k.dma(out=tri[:], in_=I["tri"][:, :, :])
            k.dma(out=maskf[:], in_=I["maskf"][:, :])
            k.dma(out=maskb[:], in_=I["maskb"][:, :])
            k.dma(out=qn[:], in_=I["qn"][l, :, :])
            k.dma(out=kvn[:], in_=I["kvn"][l, :, :])
            dve.memset(wuq_f[:], 0.0)
            dve.memset(wukv_f[:], 0.0)
            k.dma(out=wuq_f[:, 0, :], in_=I["wuq"][l, 0:128, :])
            k.dma(out=wuq_f[0:96, 1, :], in_=I["wuq"][l, 128:224, :])
            k.dma(out=wukv_f[0:96, :], in_=I["wukv"][l, :, :])
            dve.tensor_copy(sgw[:], sgw_f[:])
            dve.tensor_copy(wgb[:], wg_f[:])
            dve.tensor_copy(wuq[:], wuq_f[:])
            dve.tensor_copy(wukv[:], wukv_f[:])
            dve.memset(QMAX[:], 0.0)
            dve.memset(KMAX[:], 0.0)

            xt = [sbuf(ph, "xt%d" % i, [128, D], F32) for i in range(2)]
            rc = [sbuf(ph, "rc%d" % i, [128, 128], F32) for i in range(3)]
            rs_ = [sbuf(ph, "rs%d" % i, [128, 128], F32) for i in range(3)]
            junk = sbuf(ph, "junk", [128, D], BF16)
            st4 = sbuf(ph, "st4", [128, 16], F32)
            st4y = sbuf(ph, "st4y", [128, 16], F32)
            junky = sbuf(ph, "junky", [128, 512], BF16)
            xsb = sbuf(ph, "xsb", [128, D], BF16)
            hxT = sbuf(ph, "hxT", [128, 8, 128], BF16)
            P2 = [sbuf(ph, "P%d" % i, [128, 2176], F32) for i in range(2)]
            GA = sbuf(ph, "GA", [128, 512], F32)
            t256a = sbuf(ph, "t256a", [128, 256], F32)
            t256b = sbuf(ph, "t256b", [128, 256], F32)
            vnb = sbuf(ph, "vnb", [128, 256], BF16)
            oab = sbuf(ph, "oab", [128, 256], BF16)
            mxT = sbuf(ph, "mxT", [128, 2, 128], BF16)
            glb = sbuf(ph, "glb", [128, 32], BF16)
            glT = sbuf(ph, "glT", [32, 128], BF16)
            zb = sbuf(ph, "zb", [128, 512], F32)
            Lg = sbuf(ph, "Lg", [128, 512], F32)
            E1 = sbuf(ph, "E1", [128, 256], F32)
            E2 = sbuf(ph, "E2", [128, 256], F32)
            E3 = sbuf(ph, "E3", [128, 256], F32)
            qin = sbuf(ph, "qin", [128, 256], BF16)
            kin = sbuf(ph, "kin", [128, 256], BF16)
            kend = sbuf(ph, "kend", [128, 256], BF16)
            vb = sbuf(ph, "vb", [128, 256], BF16)
            qkT = [sbuf(ph, "qkT%d" % i, [64, 1024], BF16) for i in range(2)]
            kvd = [sbuf(ph, "kvd%d" % i, [64, 260], F32) for i in range(2)]
            scA = sbuf(ph, "scA", [128, 512], F32)
            scB = sbuf(ph, "scB", [128, 512], F32)
            scS = sbuf(ph, "scS", [128, 512], BF16)
            oin = sbuf(ph, "oin", [128, 256], F32)
            rsl = sbuf(ph, "rsl", [128, 256], F32)
            uTs = sbuf(ph, "uTs", [128, 2, 128], F32)
            cqn = sbuf(ph, "cqn", [128, 320], BF16)
            cTt = sbuf(ph, "cTt", [128, 384], BF16)
            QR = sbuf(ph, "QR", [128, 128], F32)
            KR = sbuf(ph, "KR", [128, 32], F32)
            KR2 = sbuf(ph, "KR2", [128, 32], F32)
            rA = sbuf(ph, "rA", [128, 128], F32)
            rB = sbuf(ph, "rB", [128, 128], F32)
            Qf = sbuf(ph, "Qf", [128, 4, 96], F32)
            Kf = sbuf(ph, "Kf", [128, 4, 96], F32)
            Qb = sbuf(ph, "Qb", [128, 4, 96], BF16)
            Kb = sbuf(ph, "Kb", [128, 4, 96], BF16)
            sq384 = sbuf(ph, "sq384", [128, 384], F32)
            VA = sbuf(ph, "VA", [128, 4, 66], BF16)
            QKT = sbuf(ph, "QKT", [128, 1024], BF16)
            dve.memset(VA[:], 1.0)
            dve.memset(cTt[:], 0.0)
            dve.memset(QKT[:], 0.0)

            tpb = psum(ph, "tpb", [128, 1024], BF16)
            wbb = psum(ph, "wbb", [128, 1024], BF16)
            pin = [psum(ph, "pin%d" % i, [128, 512], F32) for i in range(2)]
            w0 = psum(ph, "w0", [128, 512], F32)
            w1 = psum(ph, "w1", [128, 512], F32)
            w2 = psum(ph, "w2", [128, 512], F32)
            wz = psum(ph, "wz", [128, 512], F32)

            def load(i):
                r0 = i * 128
                k.dma(out=xt[i % 2][:], in_=XS[r0:r0 + 128, :], rkey=i)
                if i >= 2:
                    p0 = r0 - NCTX
                    k.dma(out=rc[i % 3][:], in_=I["ropec"][p0:p0 + 128, :])
                    k.dma(out=rs_[i % 3][:], in_=I["ropes"][p0:p0 + 128, :])

            def X(i):
                isctx = i < 2
                w = 1 if isctx else 0
                c0 = i * 128
                own = isctx or ((i - 2) * 128 < OWNM)
                P = P2[i % 2]
                x_ = xt[i % 2]
                act.activation(junk[:], x_[:], AF.Square, accum_out=st4[:, 0:1])
                rstd_from_ssq(st4[:, 0:1], st4[:, 2:3], D, st4[:, 1:2])
                dve.tensor_scalar(xsb[:], x_[:], st4[:, 2:3], None, ALU.mult)
                for c in range(8):
                    pe.transpose(tpb[:, c * 128:(c + 1) * 128], xsb[:, c * 128:(c + 1) * 128], identb[:])
                for c in range(8):
                    if c % 2 == 0:
                        dve.tensor_scalar(hxT[:, c, :], tpb[:, c * 128:(c + 1) * 128], GS[:, 0, c, w:w + 1],
                                          SH[:, 0, c, w:w + 1], ALU.mult, ALU.add)
                    else:
                        act.activation(hxT[:, c, :], tpb[:, c * 128:(c + 1) * 128], AF.Identity,
                                       bias=SH[:, 0, c, w:w + 1], scale=GS[:, 0, c, w:w + 1])
                for n in range(5):
                    n0 = n * 512
                    nw = min(512, 2176 - n0)
                    pb = pin[n % 2]
                    if n == 0 and not own:
                        continue
                    for kc in range(8):
                        pe.matmul(pb[:, 0:nw], lhsT=hxT[:, kc, :], rhs=win[:, kc, n0:n0 + nw], start=(kc == 0), stop=(kc == 7))
                    if n % 2 == 0:
                        act.copy(P[:, n0:n0 + nw], pb[:, 0:nw])
                    else:
                        dve.tensor_copy(P[:, n0:n0 + nw], pb[:, 0:nw])
                if ((not isctx) or ctxout) and own:
                    act.activation(GA[:], P[:, 0:512], AF.Gelu_apprx_tanh)
                    act.activation(t256a[:], GA[:, 256:512], AF.Square)
                    dve.tensor_reduce(st4[:, 4:8], t256a[:].rearrange("p (h d) -> p h d", h=4), AX.X, ALU.add)
                    rstd_from_ssq(st4[:, 4:8], st4[:, 12:16], 64, st4[:, 8:12])
                    dve.tensor_tensor(t256b[:].rearrange("p (h d) -> p h d", h=4), GA[:, 256:512].rearrange("p (h d) -> p h d", h=4),
                                      st4[:, 12:16].unsqueeze(2).broadcast_to([128, 4, 64]), ALU.mult)
                    dve.tensor_tensor(vnb[:], t256b[:], sgn[:], ALU.mult)
                    for h in range(4):
                        pe.matmul(w0[:, h * 64:(h + 1) * 64], lhsT=sgw[:, h, :], rhs=vnb[:, h * 64:(h + 1) * 64], start=True, stop=True)
                    dve.tensor_tensor(t256a[:].rearrange("p (h d) -> p h d", h=4), w0[:, 0:256].rearrange("p (h d) -> p h d", h=4),
                                      sgb[:].unsqueeze(2).broadcast_to([128, 4, 64]), ALU.add)
                    dve.tensor_tensor(oab[:], t256a[:], GA[:, 0:256], ALU.mult)
                    for c in range(2):
                        pe.transpose(tpb[:, c * 128:(c + 1) * 128], oab[:, c * 128:(c + 1) * 128], identb[:])
                    act.copy(mxT[:].rearrange("p c t -> p (c t)"), tpb[:, 0:256])
                    for c in range(2):
                        k.dma(out=MIXT[c * 128:(c + 1) * 128, c0:c0 + 128], in_=mxT[:, c, :], wkey=("a", i, c))
                for c in range(2):
                    pe.transpose(w0[:, 256 + c * 128:256 + (c + 1) * 128], P[:, 1568 + c * 128:1568 + (c + 1) * 128], identf[:])
                dve.tensor_copy(uTs[:].rearrange("p c t -> p (c t)"), w0[:, 256:512])
                for c in range(2):
                    k.dma(out=UT[c * 128:(c + 1) * 128, c0:c0 + 128], in_=uTs[:, c, :], wkey=(i, c))

            def Y(i):
                isctx = i < 2
                w = 1 if isctx else 0
                c0 = i * 128
                own = isctx or ((i - 2) * 128 < OWNM)
                P = P2[i % 2]
                PB = P[:, 512:1568]
                dve.tensor_copy(glb[:], PB[:, 1024:1056])
                pe.transpose(wbb[0:32, 256:384], glb[:], identb[:])
                act.copy(glT[:], wbb[0:32, 256:384])
                pe.matmul(wz[:], lhsT=glT[:], rhs=wgb[:], start=True, stop=True)
                dve.tensor_tensor(zb[:], wz[:], bg[:], ALU.add)
                act.activation(zb[:], zb[:], AF.Exp, scale=-1.0)
                act.activation(Lg[:], zb[:], AF.Ln, bias=onec[:, 0:1])
                act.copy(vb[:], PB[:, 512:768])
                if own:
                    act.activation(rsl[:], PB[:, 768:1024], AF.Silu)
                    k.dma(out=RSD[c0:c0 + 128, :], in_=rsl[:], wkey=i)
                for d in range(2):
                    if not own and d == 0:
                        continue
                    Ld = Lg[:, d * 256:(d + 1) * 256]
                    pe.matmul(w1[:, 0:256], lhsT=tri[:, 2 * d, :], rhs=Ld, start=True, stop=True)
                    pe.matmul(w1[:, 256:512], lhsT=tri[:, 2 * d + 1, :], rhs=Ld, start=True, stop=True)
                    act.activation(E3[:], w1[:, 256:512], AF.Exp)
                    dve.tensor_tensor(kend[:], PB[:, 256:512], E3[:], ALU.mult)
                    if own:
                        act.activation(E1[:], w1[:, 0:256], AF.Exp)
                        act.activation(E2[:], w1[:, 0:256], AF.Exp, scale=-1.0)
                        dve.scalar_tensor_tensor(qin[:], PB[:, 0:256], 0.125, E1[:], ALU.mult, ALU.mult)
                        dve.tensor_tensor(kin[:], PB[:, 256:512], E2[:], ALU.mult)
                        qk = qkT[d]
                        for h in range(4):
                            pe.transpose(wbb[0:64, h * 128:(h + 1) * 128], qin[:, h * 64:(h + 1) * 64], identb[:])
                        for h in range(4):
                            pe.transpose(wbb[0:64, 512 + h * 128:512 + (h + 1) * 128], kin[:, h * 64:(h + 1) * 64], identb[:])
                        dve.tensor_copy(qk[:], wbb[0:64, :])
                        k.dma(out=QIT[d, i, :, :], in_=qk[:, 0:512], wkey=(d, i))
                        for h in range(4):
                            pe.matmul(w2[:, h * 128:(h + 1) * 128], lhsT=qk[:, 512 + h * 128:512 + (h + 1) * 128],
                                      rhs=qk[:, h * 128:(h + 1) * 128], start=True, stop=True)
                        dve.tensor_tensor((scA if d == 0 else scB)[:], w2[:], (maskf if d == 0 else maskb)[:], ALU.mult)
                    for h in range(4):
                        pe.matmul(wz[0:64, h * 64:(h + 1) * 64], lhsT=kend[:, h * 64:(h + 1) * 64], rhs=vb[:, h * 64:(h + 1) * 64],
                                  start=True, stop=True)
                    for h in range(4):
                        pe.matmul(wz[0:64, 256 + h:257 + h], lhsT=Lg[:, d * 256 + h * 64:d * 256 + (h + 1) * 64], rhs=n16c[:, 0:1],
                                  start=True, stop=True)
                    dve.tensor_copy(kvd[d][:, 0:256], wz[0:64, 0:256])
                    act.activation(kvd[d][:, 256:260], wz[0:64, 256:260], AF.Exp)
                    k.dma(out=KVD[d, i, :, :], in_=kvd[d][:], wkey=(d, i))
                if own:
                    dve.tensor_tensor(scS[:], scA[:], scB[:], ALU.add)
                    for h in range(4):
                        pe.matmul(w1[:, h * 64:(h + 1) * 64], lhsT=scS[:, h * 128:(h + 1) * 128], rhs=vb[:, h * 64:(h + 1) * 64],
                                  start=True, stop=True)
                    act.copy(oin[:], w1[:, 0:256])
                    k.dma(out=OACC[c0:c0 + 128, :], in_=oin[:], wkey=i)
                PD = P[:, 1824:2176]
                act.activation(junky[:, 0:224], PD[:, 0:224], AF.Square, accum_out=st4y[:, 4:5])
                act.activation(junky[:, 256:352], PD[:, 224:320], AF.Square, accum_out=st4y[:, 5:6])
                act.activation(st4y[:, 8:9], st4y[:, 4:5], AF.Sqrt, scale=1.0 / 224, bias=epsc[:, 0:1])
                act.activation(st4y[:, 9:10], st4y[:, 5:6], AF.Sqrt, scale=1.0 / 96, bias=epsc[:, 0:1])
                dve.reciprocal(st4y[:, 12:14], st4y[:, 8:10])
                if own:
                    dve.scalar_tensor_tensor(cqn[:, 0:224], PD[:, 0:224], st4y[:, 12:13], qn[:], ALU.mult, ALU.mult)
                dve.scalar_tensor_tensor(cqn[:, 224:320], PD[:, 224:320], st4y[:, 13:14], kvn[:], ALU.mult, ALU.mult)
                if own:
                    pe.transpose(wbb[:, 0:128], cqn[:, 0:128], identb[:])
                    pe.transpose(wbb[0:96, 128:256], cqn[:, 128:224], identb[:])
                pe.transpose(wbb[0:96, 256:384], cqn[:, 224:320], identb[:])
                if own:
                    act.copy(cTt[:, 0:128], wbb[:, 0:128])
                    act.copy(cTt[0:96, 128:384], wbb[0:96, 128:384])
                else:
                    act.copy(cTt[0:96, 256:384], wbb[0:96, 256:384])
                if own:
                    pe.matmul(w1[:, 0:384], lhsT=cTt[:, 0:128], rhs=wuq[:, 0, :], start=True, stop=False)
                    pe.matmul(w1[:, 0:384], lhsT=cTt[:, 128:256], rhs=wuq[:, 1, :], start=False, stop=True)
                pe.matmul(w2[:], lhsT=cTt[:, 256:384], rhs=wukv[:], start=True, stop=True)
                q3 = w1[:, 0:384].rearrange("p (h e) -> p h e", h=4)
                kv3 = w2[:].rearrange("p (h e) -> p h e", h=4)
                if own:
                    dve.tensor_copy(Qf[:, :, 0:64], q3[:, :, 0:64])
                    dve.tensor_copy(QR[:].rearrange("p (h e) -> p h e", h=4), q3[:, :, 64:96])
                dve.tensor_copy(Kf[:, :, 0:64], kv3[:, :, 0:64])
                dve.tensor_copy(VA[:, :, 0:64], kv3[:, :, 64:128])
                if isctx:
                    pool.tensor_copy(Qf[:, :, 64:96], QR[:].rearrange("p (h e) -> p h e", h=4))
                    pool.tensor_copy(Kf[:, :, 64:96], PD[:, 320:352].unsqueeze(1).broadcast_to([128, 4, 32]))
                else:
                    cs, sn = rc[i % 3], rs_[i % 3]
                    if own:
                        dve.tensor_tensor(rA[:], QR[:], cs[:], ALU.mult)
                        QRv = QR[:].rearrange("p (g a j) -> p g a j", g=8, a=2)
                        snv = sn[:].rearrange("p (g a j) -> p g a j", g=8, a=2)
                        rBv = rB[:].rearrange("p (g a j) -> p g a j", g=8, a=2)
                        pool.tensor_tensor(rBv[:, :, 0, :], QRv[:, :, 1, :], snv[:, :, 0, :], ALU.mult)
                        pool.tensor_tensor(rBv[:, :, 1, :], QRv[:, :, 0, :], snv[:, :, 1, :], ALU.mult)
                        dve.tensor_tensor(Qf[:, :, 64:96], rA[:].rearrange("p (h e) -> p h e", h=4),
                                          rB[:].rearrange("p (h e) -> p h e", h=4), ALU.add)
                    pool.tensor_copy(KR[:], PD[:, 320:352])
                    dve.tensor_tensor(rA[:, 0:32], KR[:], cs[:, 0:32], ALU.mult)
                    KRv = KR[:].rearrange("p (g a j) -> p g a j", g=2, a=2)
                    sn2 = sn[:, 0:32].rearrange("p (g a j) -> p g a j", g=2, a=2)
                    rB2 = rB[:, 0:32].rearrange("p (g a j) -> p g a j", g=2, a=2)
                    pool.tensor_tensor(rB2[:, :, 0, :], KRv[:, :, 1, :], sn2[:, :, 0, :], ALU.mult)
                    pool.tensor_tensor(rB2[:, :, 1, :], KRv[:, :, 0, :], sn2[:, :, 1, :], ALU.mult)
                    dve.tensor_tensor(KR2[:], rA[:, 0:32], rB[:, 0:32], ALU.add)
                    pool.tensor_copy(Kf[:, :, 64:96], KR2[:].unsqueeze(1).broadcast_to([128, 4, 32]))
                if own:
                    dve.tensor_copy(Qb[:], Qf[:])
                act.copy(Kb[:].rearrange("p h e -> p (h e)"), Kf[:].rearrange("p h e -> p (h e)"))
                if (not isctx or ctxout) and own:
                    act.activation(sq384[:], Qf[:].rearrange("p h e -> p (h e)"), AF.Square)
                    dve.tensor_reduce(st4y[:, 4:8], sq384[:].rearrange("p (h e) -> p h e", h=4), AX.X, ALU.add)
                    dve.tensor_tensor(QMAX[:], QMAX[:], st4y[:, 4:8], ALU.max)
                act.activation(sq384[:], Kf[:].rearrange("p h e -> p (h e)"), AF.Square)
                dve.tensor_reduce(st4y[:, 8:12], sq384[:].rearrange("p (h e) -> p h e", h=4), AX.X, ALU.add)
                dve.tensor_tensor(KMAX[:], KMAX[:], st4y[:, 8:12], ALU.max)
                if own:
                    for h in range(4):
                        pe.transpose(wbb[0:96, h * 128:(h + 1) * 128], Qb[:, h, :], identb[:])
                for h in range(4):
                    pe.transpose(wbb[0:96, 512 + h * 128:512 + (h + 1) * 128], Kb[:, h, :], identb[:])
                if own:
                    act.copy(QKT[0:96, :], wbb[0:96, :])
                else:
                    act.copy(QKT[0:96, 512:1024], wbb[0:96, 512:1024])
                for h in range(4):
                    if own:
                        k.dma(out=QTD[h, :, c0:c0 + 128], in_=QKT[:, h * 128:(h + 1) * 128], wkey=(i, h))
                    k.dma(out=KTD[h, :, c0:c0 + 128], in_=QKT[:, 512 + h * 128:512 + (h + 1) * 128], wkey=(i, h))
                k.dma(out=VAD[c0:c0 + 128, :], in_=VA[:].rearrange("p h e -> p (h e)"), wkey=i)

            load(0)
            for t_ in range(ntile + 1):
                if t_ + 1 < ntile:
                    load(t_ + 1)
                if t_ < ntile:
                    X(t_)
                if t_ >= 1:
                    Y(t_ - 1)
        k.barrier()

    def phase2(l, ctxout, last=False):
        OWNM = (H + 512) if (last and split) else nx
        nown = 2 + OWNM // 128
        with ExitStack() as ph:
            S = sbuf(ph, "S", [64, 256], F32)
            Sb = sbuf(ph, "Sb", [64, 256], BF16)
            gln = sbuf(ph, "gln", [128, 256], F32)
            qit = [sbuf(ph, "qit%d" % i, [64, 512], BF16) for i in range(2)]
            kvd = [sbuf(ph, "kvd%d" % i, [64, 260], F32) for i in range(2)]
            oac = [sbuf(ph, "oac%d" % i, [128, 256], F32) for i in range(2)]
            rsl = [sbuf(ph, "rsl%d" % i, [128, 256], F32) for i in range(2)]
            osum = sbuf(ph, "osum", [128, 256], F32)
            t1 = sbuf(ph, "t1", [128, 256], F32)
            t2 = sbuf(ph, "t2", [128, 256], F32)
            st4 = sbuf(ph, "st4", [128, 16], F32)
            obb = sbuf(ph, "obb", [128, 256], BF16)
            mxT = sbuf(ph, "mxT", [128, 2, 128], BF16)
            ops_ = [psum(ph, "ops%d" % i, [128, 512], F32) for i in range(2)]
            wbb = psum(ph, "wbb", [128, 1024], BF16)
            k.dma(out=gln[:], in_=I["glan"][l, :, :])
            for d in range(2):
                order = list(range(min(ntile, nown))) if d == 0 else [1, 0] + list(range(ntile - 1, 1, -1))
                dve.memset(S[:], 0.0)

                def load(n):
                    i = order[n]
                    k.dma(out=qit[n % 2][:], in_=QIT[d, i, :, :], rkey=(d, i))
                    k.dma(out=kvd[n % 2][:], in_=KVD[d, i, :, :], rkey=(d, i))
                    k.dma(out=oac[n % 2][:], in_=OACC[i * 128:(i + 1) * 128, :], rkey=i)
                    if d == 1:
                        k.dma(out=rsl[n % 2][:], in_=RSD[i * 128:(i + 1) * 128, :], rkey=i)
                load(0)
                for n, i in enumerate(order):
                    if n + 1 < len(order):
                        load(n + 1)
                    need_o = ((i >= 2) or ctxout) and i < nown
                    q_, kv_, oa_ = qit[n % 2], kvd[n % 2], oac[n % 2]
                    if need_o:
                        act.copy(Sb[:], S[:])
                        op = ops_[n % 2]
                        for h in range(4):
                            pe.matmul(op[:, h * 64:(h + 1) * 64], lhsT=q_[:, h * 128:(h + 1) * 128], rhs=Sb[:, h * 64:(h + 1) * 64],
                                      start=True, stop=True)
                        dve.tensor_tensor(osum[:], op[:, 0:256], oa_[:], ALU.add)
                        if d == 0:
                            k.dma(out=OACC[i * 128:(i + 1) * 128, :], in_=osum[:], wkey=i)
                        else:
                            act.activation(t1[:], osum[:], AF.Square)
                            dve.tensor_reduce(st4[:, 0:4], t1[:].rearrange("p (h e) -> p h e", h=4), AX.X, ALU.add)
                            rstd_from_ssq(st4[:, 0:4], st4[:, 8:12], 64, st4[:, 4:8])
                            dve.tensor_tensor(t2[:].rearrange("p (h e) -> p h e", h=4), osum[:].rearrange("p (h e) -> p h e", h=4),
                                              st4[:, 8:12].unsqueeze(2).broadcast_to([128, 4, 64]), ALU.mult)
                            dve.tensor_tensor(t1[:], t2[:], gln[:], ALU.mult)
                            dve.tensor_tensor(obb[:], t1[:], rsl[n % 2][:], ALU.mult)
                            for c in range(2):
                                pe.transpose(wbb[:, c * 128:(c + 1) * 128], obb[:, c * 128:(c + 1) * 128], identb[:])
                            act.copy(mxT[:].rearrange("p c t -> p (c t)"), wbb[:, 0:256])
                            for c in range(2):
                                k.dma(out=MIXT[256 + c * 128:256 + (c + 1) * 128, i * 128:(i + 1) * 128], in_=mxT[:, c, :], wkey=("b", i, c))
                    dve.tensor_tensor(S[:].rearrange("p (h e) -> p h e", h=4), S[:].rearrange("p (h e) -> p h e", h=4),
                                      kv_[:, 256:260].unsqueeze(2).broadcast_to([64, 4, 64]), ALU.mult)
                    dve.tensor_tensor(S[:], S[:], kv_[:, 0:256], ALU.add)
                k.barrier()

    def phase3(l, ctxout, last=False):
        LC = 512
        OWNM = (H + 512) if (last and split) else nx
        with ExitStack() as ph:
            pr = sbuf(ph, "pr", [128, 2, 3, 8], F32)
            bre = sbuf(ph, "bre", [128, 2, 8, 16], F32)
            bim = sbuf(ph, "bim", [128, 2, 8, 16], F32)
            cre = sbuf(ph, "cre", [128, 2, 8, 16], F32)
            cim = sbuf(ph, "cim", [128, 2, 8, 16], F32)
            dT = sbuf(ph, "dT", [128, 2], F32)
            bgl = sbuf(ph, "bgl", [128, 2], F32)
            wgl_f = sbuf(ph, "wgl_f", [128, 2, 256], F32)
            wgl = sbuf(ph, "wgl", [128, 2, 256], BF16)
            for d in range(2):
                k.dma(out=pr[:, d, 0, :], in_=I["s5are"][l, d, :, :])
                k.dma(out=pr[:, d, 1, :], in_=I["s5aim"][l, d, :, :])
                k.dma(out=pr[:, d, 2, :], in_=I["s5ldt"][l, d, :, :])
                k.dma(out=bre[:, d], in_=I["s5bre"][l, d, :, :, :])
                k.dma(out=bim[:, d], in_=I["s5bim"][l, d, :, :, :])
                k.dma(out=cre[:, d], in_=I["s5cre"][l, d, :, :, :])
                k.dma(out=cim[:, d], in_=I["s5cim"][l, d, :, :, :])
            k.dma(out=dT[:], in_=I["s5dT"][l, :, :])
            k.dma(out=bgl[:], in_=I["s5bgluT"][l, :, :])
            k.dma(out=wgl_f[:], in_=I["s5wglu"][l, :, :].rearrange("(c p) n -> p c n", p=128))
            dve.tensor_copy(wgl[:], wgl_f[:])
            e = sbuf(ph, "e", [128, 2, 16, 8], F32)
            dve.memset(e[:], 0.0)
            A_RE, A_IM, LDT = pr[:, :, 0, :], pr[:, :, 1, :], pr[:, :, 2, :]
            DT, AR, TH, RM, S16, S8, C8, T0, T1, LR, LI = [e[:, :, n, :] for n in range(11)]
            act.activation(DT, LDT, AF.Exp)
            dve.tensor_tensor(AR, A_RE, DT, ALU.mult)
            dve.tensor_tensor(TH, A_IM, DT, ALU.mult)
            act.activation(RM, AR, AF.Exp)
            act.activation(S16, TH, AF.Sin, scale=1.0 / 16)
            act.activation(S8, TH, AF.Sin, scale=1.0 / 8)
            dve.tensor_tensor(T0, S16, S16, ALU.mult)
            dve.tensor_scalar(C8, T0, -2.0, 1.0, ALU.mult, ALU.add)
            cc, ss = C8, S8
            for it in range(3):
                dve.tensor_tensor(T0, cc, cc, ALU.mult)
                dve.tensor_tensor(T1, ss, ss, ALU.mult)
                dve.tensor_tensor(LI, cc, ss, ALU.mult)
                dve.tensor_tensor(T0, T0, T1, ALU.subtract)
                dve.tensor_scalar(S8, LI, 2.0, None, ALU.mult)
                dve.tensor_copy(C8, T0)
                cc, ss = C8, S8
            CT, ST = C8, S8
            dve.tensor_tensor(LR, RM, CT, ALU.mult)
            dve.tensor_tensor(LI, RM, ST, ALU.mult)
            NR, DEN, CR, CI, T2 = [e[:, :, n, :] for n in range(11, 16)]
            dve.tensor_scalar(NR, LR, -1.0, None, ALU.add)
            dve.tensor_tensor(T0, A_RE, A_RE, ALU.mult)
            dve.tensor_tensor(T1, A_IM, A_IM, ALU.mult)
            dve.tensor_tensor(DEN, T0, T1, ALU.add)
            dve.reciprocal(DEN, DEN)
            dve.tensor_tensor(T0, NR, A_RE, ALU.mult)
            dve.tensor_tensor(T1, LI, A_IM, ALU.mult)
            dve.tensor_tensor(T0, T0, T1, ALU.add)
            dve.tensor_tensor(CR, T0, DEN, ALU.mult)
            dve.tensor_tensor(T0, LI, A_RE, ALU.mult)
            dve.tensor_tensor(T1, NR, A_IM, ALU.mult)
            dve.tensor_tensor(T0, T0, T1, ALU.subtract)
            dve.tensor_tensor(CI, T0, DEN, ALU.mult)
            bbr = sbuf(ph, "bbr", [128, 2, 8, 16], F32)
            bbi = sbuf(ph, "bbi", [128, 2, 8, 16], F32)
            tb = sbuf(ph, "tb", [128, 2, 8, 16], F32)
            for d in range(2):
                crb = e[:, d, 13, :].unsqueeze(2).broadcast_to([128, 8, 16])
                cib = e[:, d, 14, :].unsqueeze(2).broadcast_to([128, 8, 16])
                dve.tensor_tensor(bbr[:, d], bre[:, d], crb, ALU.mult)
                dve.tensor_tensor(tb[:, d], bim[:, d], cib, ALU.mult)
                dve.tensor_tensor(bbr[:, d], bbr[:, d], tb[:, d], ALU.subtract)
                dve.tensor_tensor(bbi[:, d], bim[:, d], crb, ALU.mult)
                dve.tensor_tensor(tb[:, d], bre[:, d], cib, ALU.mult)
                dve.tensor_tensor(bbi[:, d], bbi[:, d], tb[:, d], ALU.add)
            BT = sbuf(ph, "BT", [128, 2, 2, 8, 128], BF16)
            CP = sbuf(ph, "CP", [128, 2, 2, 8, 128], BF16)
            Z = [sbuf(ph, "Z%d" % i, [128, 128], F32) for i in range(2)]
            pz = [psum(ph, "pz%d" % i, [128, 512], F32) for i in range(2)]
            dve.memset(CP[:], 0.0)
            dve.memset(Z[0][:], 0.0)
            dve.memset(Z[1][:], 0.0)
            n = 0
            for d in range(2):
                for ri, src in enumerate((bbr, bbi)):
                    for j in range(8):
                        jj = j % 4
                        z = Z[n % 2]
                        if jj != (j - 1) % 4 or True:
                            pool.memset(z[:], 0.0)
                        pool.tensor_copy(z[0:64, 32 * jj:32 * jj + 16], src[0:64, d, j, :])
                        pool.tensor_copy(z[64:128, 32 * jj + 16:32 * jj + 32], src[64:128, d, j, :])
                        pe.transpose(pz[n % 2][:, 0:128], z[:], identf[:])
                        act.copy(BT[:, d, ri, j, :], pz[n % 2][:, 0:128])
                        n += 1
                for j in range(8):
                    jj = j % 4
                    dve.tensor_copy(CP[0:64, d, 0, j, 32 * jj:32 * jj + 16], cre[0:64, d, j, :])
                    dve.tensor_copy(CP[64:128, d, 0, j, 32 * jj + 16:32 * jj + 32], cre[64:128, d, j, :])
                    dve.tensor_scalar(CP[0:64, d, 1, j, 32 * jj:32 * jj + 16], cim[0:64, d, j, :], -1.0, None, ALU.mult)
                    dve.tensor_scalar(CP[64:128, d, 1, j, 32 * jj + 16:32 * jj + 32], cim[64:128, d, j, :], -1.0, None, ALU.mult)
            RR = sbuf(ph, "RR", [128, 2, 8, LC], F32)
            RI = sbuf(ph, "RI", [128, 2, 8, LC], F32)
            tt = sbuf(ph, "tt", [128, LC], F32)
            for d in range(2):
                for j in range(8):
                    dve.tensor_copy(RR[:, d, j, 0:1], e[:, d, 6, j:j + 1])
                    dve.tensor_copy(RI[:, d, j, 0:1], e[:, d, 5, j:j + 1])
                    wdt = 1
                    while wdt < LC:
                        cw = RR[:, d, j, wdt - 1:wdt]
                        sw = RI[:, d, j, wdt - 1:wdt]
                        a_r, a_i = RR[:, d, j, 0:wdt], RI[:, d, j, 0:wdt]
                        dve.tensor_scalar(tt[:, 0:wdt], a_i, sw, None, ALU.mult)
                        dve.scalar_tensor_tensor(RR[:, d, j, wdt:2 * wdt], a_r, cw, tt[:, 0:wdt], ALU.mult, ALU.subtract)
                        dve.tensor_scalar(tt[:, 0:wdt], a_i, cw, None, ALU.mult)
                        dve.scalar_tensor_tensor(RI[:, d, j, wdt:2 * wdt], a_r, sw, tt[:, 0:wdt], ALU.mult, ALU.add)
                        wdt *= 2
            uTf = [sbuf(ph, "uTf%d" % i, [128, 2, LC], F32) for i in range(2)]
            uTb = sbuf(ph, "uTb", [128, 2, LC], BF16)
            yfl = [sbuf(ph, "yfl%d" % i, [128, 2, LC], F32) for i in range(2)]
            ta_ = [sbuf(ph, "ta%d" % i, [128, LC], F32) for i in range(2)]
            tb2_ = [sbuf(ph, "tb2%d" % i, [128, LC], F32) for i in range(2)]
            tc_ = [sbuf(ph, "tc%d" % i, [128, LC], F32) for i in range(2)]
            td_ = [sbuf(ph, "td%d" % i, [128, LC], F32) for i in range(2)]
            btr_ = [sbuf(ph, "btr%d" % i, [128, LC], F32) for i in range(2)]
            bti_ = [sbuf(ph, "bti%d" % i, [128, LC], F32) for i in range(2)]
            wr_ = [sbuf(ph, "wr%d" % i, [128, LC], F32) for i in range(2)]
            wi_ = [sbuf(ph, "wi%d" % i, [128, LC], F32) for i in range(2)]
            PR = [sbuf(ph, "PR0", [128, 4, 4, LC], BF16)] * 2
            h0 = sbuf(ph, "h0", [128, 2, 8], F32)
            hc_ = [sbuf(ph, "hc%d" % i, [128, 4], F32) for i in range(2)]
            ysum = sbuf(ph, "ysum", [128, 2, LC], F32)
            ygf = sbuf(ph, "ygf", [128, 2, LC], F32)
            ygb = sbuf(ph, "ygb", [128, 2, LC], BF16)
            sg = sbuf(ph, "sg", [128, LC], F32)
            ocb = sbuf(ph, "ocb", [128, 2, LC], BF16)
            pbr_ = pz
            pbi_ = [psum(ph, "pbi%d" % i, [128, 512], F32) for i in range(2)]
            py = [psum(ph, "py%d" % i, [128, 512], F32) for i in range(2)]
            pg = psum(ph, "pg", [128, 512], F32)
            chunks = [(0, NCTX)] + [(NCTX + c * LC, min(LC, nx - c * LC)) for c in range((nx + LC - 1) // LC)]
            for d in range(2):
                nownc = 1 + min(len(chunks) - 1, OWNM // LC)
                order = list(range(nownc)) if d == 0 else [0] + list(range(len(chunks) - 1, 0, -1))
                dve.memset(h0[:], 0.0)

                def load(n):
                    t0, Lc = chunks[order[n]]
                    k.dma(out=uTf[n % 2][:, :, 0:Lc], in_=UT[:, t0:t0 + Lc].rearrange("(c p) t -> p c t", p=128), rkey=None)
                    if d == 1:
                        k.dma(out=yfl[n % 2][:, :, 0:Lc], in_=YF[:, t0:t0 + Lc].rearrange("(c p) t -> p c t", p=128), rkey=None)
                load(0)
                for n, ci in enumerate(order):
                    if n + 1 < len(order):
                        load(n + 1)
                    t0, Lc = chunks[ci]
                    need_y = ((ci > 0) or ctxout) and ci < nownc
                    uf = uTf[n % 2]
                    for c_ in range(2):
                        act.copy(uTb[:, c_, 0:Lc], uf[:, c_, 0:Lc])

                    def tv(ap):
                        return ap if d == 0 else ap[:, ::-1]
                    for j in range(8):
                        jj = j % 4
                        jb = j % 2
                        ta, tb2, tc, td, btr, bti, wr, wi, hc, pbr, pbi = (ta_[jb], tb2_[jb], tc_[jb], td_[jb], btr_[jb], bti_[jb],
                                                                       wr_[jb], wi_[jb], hc_[jb], pbr_[jb], pbi_[jb])
                        Rr = tv(RR[:, d, j, 0:Lc])
                        Ri = tv(RI[:, d, j, 0:Lc])
                        pe.matmul(pbr[:, 0:Lc], lhsT=BT[:, d, 0, j, :], rhs=uTb[:, j // 4, 0:Lc], start=True, stop=True)
                        pe.matmul(pbi[:, 0:Lc], lhsT=BT[:, d, 1, j, :], rhs=uTb[:, j // 4, 0:Lc], start=True, stop=True)
                        dve.tensor_tensor(ta[:, 0:Lc], pbr[:, 0:Lc], Rr, ALU.mult)
                        dve.tensor_tensor(tb2[:, 0:Lc], pbi[:, 0:Lc], Ri, ALU.mult)
                        dve.tensor_tensor(btr[:, 0:Lc], ta[:, 0:Lc], tb2[:, 0:Lc], ALU.add)
                        dve.tensor_tensor(tc[:, 0:Lc], pbi[:, 0:Lc], Rr, ALU.mult)
                        dve.tensor_tensor(td[:, 0:Lc], pbr[:, 0:Lc], Ri, ALU.mult)
                        dve.tensor_tensor(bti[:, 0:Lc], tc[:, 0:Lc], td[:, 0:Lc], ALU.subtract)
                        rm = e[:, d, 3, j:j + 1].broadcast_to([128, Lc])
                        dve.tensor_tensor_scan(tv(wr[:, 0:Lc]), rm, tv(btr[:, 0:Lc]), h0[:, 0, j:j + 1], ALU.mult, ALU.add)
                        dve.tensor_tensor_scan(tv(wi[:, 0:Lc]), rm, tv(bti[:, 0:Lc]), h0[:, 1, j:j + 1], ALU.mult, ALU.add)
                        tl = Lc - 1 if d == 0 else 0
                        rl = RR[:, d, j, Lc - 1:Lc]
                        il = RI[:, d, j, Lc - 1:Lc]
                        dve.tensor_scalar(hc[:, 0:1], wi[:, tl:tl + 1], il, None, ALU.mult)
                        dve.tensor_scalar(hc[:, 1:2], wr[:, tl:tl + 1], il, None, ALU.mult)
                        dve.scalar_tensor_tensor(h0[:, 0, j:j + 1], wr[:, tl:tl + 1], rl, hc[:, 0:1], ALU.mult, ALU.subtract)
                        dve.scalar_tensor_tensor(h0[:, 1, j:j + 1], wi[:, tl:tl + 1], rl, hc[:, 1:2], ALU.mult, ALU.add)
                        if need_y:
                            pp = PR[(j // 4) % 2]
                            pool.tensor_tensor(pp[:, jj, 0, 0:Lc], wr[:, 0:Lc], Rr, ALU.mult)
                            dve.scalar_tensor_tensor(pp[:, jj, 1, 0:Lc], wi[:, 0:Lc], -1.0, Ri, ALU.mult, ALU.mult)
                            dve.tensor_tensor(pp[:, jj, 2, 0:Lc], wi[:, 0:Lc], Rr, ALU.mult)
                            dve.tensor_tensor(pp[:, jj, 3, 0:Lc], wr[:, 0:Lc], Ri, ALU.mult)
                            if jj == 3:
                                c = j // 4
                                for j2 in range(4):
                                    for pi in range(4):
                                        pe.matmul(py[c][:, 0:Lc], lhsT=CP[:, d, pi // 2, 4 * c + j2, :], rhs=pp[:, j2, pi, 0:Lc],
                                                  start=(j2 == 0 and pi == 0), stop=(j2 == 3 and pi == 3))
                    if need_y:
                        if d == 0:
                            for c in range(2):
                                act.copy(ysum[:, c, 0:Lc], py[c][:, 0:Lc])
                            for c in range(2):
                                k.dma(out=YF[c * 128:(c + 1) * 128, t0:t0 + Lc], in_=ysum[:, c, 0:Lc], wkey=(ci, c))
                        else:
                            for c in range(2):
                                dve.tensor_tensor(ysum[:, c, 0:Lc], py[c][:, 0:Lc], yfl[n % 2][:, c, 0:Lc], ALU.add)
                                dve.scalar_tensor_tensor(ysum[:, c, 0:Lc], uf[:, c, 0:Lc], dT[:, c:c + 1], ysum[:, c, 0:Lc], ALU.mult, ALU.add)
                                act.activation(ygf[:, c, 0:Lc], ysum[:, c, 0:Lc], AF.Gelu_apprx_tanh)
                                act.copy(ygb[:, c, 0:Lc], ygf[:, c, 0:Lc])
                            for c2 in range(2):
                                for c in range(2):
                                    pe.matmul(pg[:, 0:Lc], lhsT=wgl[:, c, c2 * 128:(c2 + 1) * 128], rhs=ygb[:, c, 0:Lc],
                                              start=(c == 0), stop=(c == 1))
                                act.activation(sg[:, 0:Lc], pg[:, 0:Lc], AF.Sigmoid, bias=bgl[:, c2:c2 + 1])
                                dve.tensor_tensor(ocb[:, c2, 0:Lc], ygf[:, c2, 0:Lc], sg[:, 0:Lc], ALU.mult)
                            for c in range(2):
                                k.dma(out=MIXT[512 + c * 128:512 + (c + 1) * 128, t0:t0 + Lc], in_=ocb[:, c, 0:Lc], wkey=("c", ci, c))
                k.barrier()

    def phase4(l, ctxout, last=False):
        OWNM = (H + 512) if (last and split) else nx
        scale = 96.0 ** -0.5
        nkc = ntile
        with ExitStack() as ph:
            KT = sbuf(ph, "KT", [128, 4, NT], BF16)
            VAs = sbuf(ph, "VAs", [128, nkc, 264], BF16)
            QTb = [sbuf(ph, "QTb%d" % i, [128, 512], BF16) for i in range(2)]
            Pe = [sbuf(ph, "Pe%d" % i, [128, 1024], BF16) for i in range(2)]
            negb = sbuf(ph, "negb", [128, 1], F32)
            m2 = sbuf(ph, "m2", [128, 2], F32)
            m2t = sbuf(ph, "m2t", [2, 2], F32)
            rrow = sbuf(ph, "rrow", [128, 512], F32)
            bcs = sbuf(ph, "bcs", [64, 512], F32)
            odb = [sbuf(ph, "odb%d" % i, [64, 512], BF16) for i in range(2)]
            sp_ = [psum(ph, "sps%d" % i, [128, 1024], F32) for i in range(2)]
            acc = [psum(ph, "acc%d" % i, [128, 512], F32) for i in range(2)]
            pbc = psum(ph, "pbc", [128, 512], F32)
            for h in range(4):
                k.dma(out=KT[:, h, :], in_=KTD[h, :, :])
            k.dma(out=VAs[:], in_=VAD[:, :].rearrange("(kc p) c -> p kc c", p=128))
            dve.tensor_reduce(m2[:, 0:1], QMAX[:], AX.X, ALU.max)
            dve.tensor_reduce(m2[:, 1:2], KMAX[:], AX.X, ALU.max)
            pe.transpose(pbc[0:2, 0:128], m2[:], identf[:])
            dve.tensor_reduce(m2t[:, 0:1], pbc[0:2, 0:128], AX.X, ALU.max)
            act.activation(m2t[:, 1:2], m2t[:, 0:1], AF.Ln)
            pe.matmul(pbc[:, 256:257], lhsT=onesf[0:2, :], rhs=m2t[:, 1:2], start=True, stop=True)
            act.activation(negb[:], pbc[:, 256:257], AF.Exp, scale=0.5)
            dve.tensor_scalar(negb[:], negb[:], -scale, None, ALU.mult)
            jobs = []
            for h in range(4):
                if ctxout:
                    jobs.append((h, 0, NCTX, [0, 1]))
                for qb in range(min(nx, OWNM) // 512):
                    jobs.append((h, NCTX + qb * 512, 512, list(range(nkc))))

            def load(n):
                h, q0, qw, _ = jobs[n]
                k.dma(out=QTb[n % 2][:, 0:qw], in_=QTD[h, :, q0:q0 + qw])
            load(0)
            for n, (h, q0, qw, kcs) in enumerate(jobs):
                if n + 1 < len(jobs):
                    load(n + 1)
                Q = QTb[n % 2]
                ac = acc[n % 2]

                def scores(pi):
                    s__ = sp_[(pi // 2) % 2]
                    for u in range(2):
                        kc = kcs[pi + u]
                        pe.matmul(s__[:, u * 512:u * 512 + qw], lhsT=KT[:, h, kc * 128:(kc + 1) * 128], rhs=Q[:, 0:qw], start=True, stop=True)
                scores(0)
                for pi in range(0, len(kcs), 2):
                    s_ = sp_[(pi // 2) % 2]
                    P_ = Pe[(pi // 2) % 2]
                    if pi + 2 < len(kcs):
                        scores(pi + 2)
                    if qw == 512:
                        act.activation(P_[:], s_[:], AF.Exp, bias=negb[:, 0:1], scale=scale)
                    else:
                        for u in range(2):
                            act.activation(P_[:, u * 512:u * 512 + qw], s_[:, u * 512:u * 512 + qw], AF.Exp, bias=negb[:, 0:1], scale=scale)
                    for u in range(2):
                        kc = kcs[pi + u]
                        pe.matmul(ac[0:65, 0:qw], lhsT=VAs[:, kc, h * 66:h * 66 + 65], rhs=P_[:, u * 512:u * 512 + qw],
                                  start=(pi == 0 and u == 0), stop=(pi + u == len(kcs) - 1))
                dve.reciprocal(rrow[64:65, 0:qw], ac[64:65, 0:qw])
                pe.matmul(pbc[0:64, 0:qw], lhsT=onesf[64:65, 0:64], rhs=rrow[64:65, 0:qw], start=True, stop=True)
                act.copy(bcs[:, 0:qw], pbc[0:64, 0:qw])
                dve.tensor_tensor(odb[n % 2][:, 0:qw], ac[0:64, 0:qw], bcs[:, 0:qw], ALU.mult)
                k.dma(out=MIXT[768 + 64 * h:768 + 64 * (h + 1), q0:q0 + qw], in_=odb[n % 2][:, 0:qw], wkey=("d", n))
        k.barrier()

    def epilogue(po, x_, gb, xo, st4, junk, tmpf):
        act.activation(junk[:], po[:], AF.Square, accum_out=st4[:, 0:1])
        rstd_from_ssq(st4[:, 0:1], st4[:, 2:3], D, st4[:, 1:2])
        dve.scalar_tensor_tensor(tmpf[:], po[:], st4[:, 2:3], gb[:], ALU.mult, ALU.mult)
        dve.tensor_tensor(xo[:], tmpf[:], x_[:], ALU.add)

    def load_cast_rows(dst, src_rows, nk, ncols, stg):
        for kc in range(nk):
            s_ = stg[kc % 2]
            k.dma(out=s_[:, 0:ncols], in_=src_rows[kc * 128:(kc + 1) * 128, :])
            (pool if kc % 2 else dve).tensor_copy(dst[:, kc, :], s_[:, 0:ncols])

    def phase5(l, ctxout, last=False):
        OWN5 = (H + 128) if (last and split) else nx
        with ExitStack() as ph:
            wo = sbuf(ph, "wo", [128, 8, D], BF16)
            stg = [sbuf(ph, "stg%d" % i, [128, D], F32) for i in range(2)]
            load_cast_rows(wo, I["w_out"][l], 8, D, stg)
            mx = [sbuf(ph, "mx%d" % i, [128, 8, 128], BF16) for i in range(2)]
            xt = [sbuf(ph, "xt%d" % i, [128, D], F32) for i in range(2)]
            xo = [sbuf(ph, "xo%d" % i, [128, D], F32) for i in range(2)]
            junk = sbuf(ph, "junk", [128, D], BF16)
            tmpf = sbuf(ph, "tmpf", [128, D], F32)
            st4 = sbuf(ph, "st4", [128, 4], F32)
            po = [psum(ph, "po%d" % i, [128, 1024], F32) for i in range(2)]
            tiles = list(range(0 if ctxout else 2, min(ntile, 2 + OWN5 // 128)))

            def load(n):
                i = tiles[n]
                k.dma(out=mx[n % 2][:], in_=MIXT[:, i * 128:(i + 1) * 128].rearrange("(c p) t -> p c t", p=128))
                k.dma(out=xt[n % 2][:], in_=XS[i * 128:(i + 1) * 128, :], rkey=i)
            load(0)
            for n, i in enumerate(tiles):
                if n + 1 < len(tiles):
                    load(n + 1)
                p_ = po[n % 2]
                for nn in range(2):
                    for kc in range(8):
                        pe.matmul(p_[:, nn * 512:(nn + 1) * 512], lhsT=mx[n % 2][:, kc, :], rhs=wo[:, kc, nn * 512:(nn + 1) * 512],
                                  start=(kc == 0), stop=(kc == 7))
                epilogue(p_, xt[n % 2], GB[0][1 if i < 2 else 0], xo[n % 2], st4, junk, tmpf)
                k.dma(out=XS[i * 128:(i + 1) * 128, :], in_=xo[n % 2][:], wkey=i)
        k.barrier()

    def phase6(l, ctxout, last):
        OWN = 510
        with ExitStack() as ph:
            wdn = sbuf(ph, "wdn", [128, 22, D], BF16)
            cw = sbuf(ph, "cw", [128, 44, 3], F32)
            cb = sbuf(ph, "cb", [128, 44], F32)
            k.dma(out=cw[:], in_=I["convwT"][l, :, :, :])
            k.dma(out=cb[:], in_=I["convbT"][l, :, :])
            with ExitStack() as pp_:
                stg = [sbuf(pp_, "stg%d" % i, [128, DFF], F32) for i in range(2)]
                cbf = [sbuf(pp_, "cbf%d" % i, [128, DFF], BF16) for i in range(2)]
                load_cast_rows(wdn, I["w_dn"][l], 22, D, stg)
                n = 0
                for kc in range(8):
                    for half in range(2):
                        s_, c_ = stg[n % 2], cbf[n % 2]
                        k.dma(out=s_[:], in_=I["w_up"][l, kc * 128:(kc + 1) * 128, half * DFF:(half + 1) * DFF])
                        (pool if n % 2 else dve).tensor_copy(c_[:], s_[:])
                        for i_ in range(22):
                            k.dma(out=WUPS[i_, :, kc * 256 + half * 128:kc * 256 + (half + 1) * 128], in_=c_[:, i_ * 128:(i_ + 1) * 128],
                                  wkey=(kc, half, i_))
                        n += 1
                k.barrier()
            NW = 4
            wst = [sbuf(ph, "wst%d" % i, [128, 8, 256], BF16) for i in range(NW)]
            xb = [sbuf(ph, "xb%d" % i, [128, 4, D], F32) for i in range(2)]
            xsb = sbuf(ph, "xsb", [128, D], BF16)
            junk = sbuf(ph, "junk", [128, D], BF16)
            hxT = sbuf(ph, "hxT", [128, 8, 512], BF16)
            hid = sbuf(ph, "hid", [128, 22, 512], BF16)
            ca = sbuf(ph, "ca", [128, 512], F32)
            cg = sbuf(ph, "cg", [128, 512], F32)
            ga = sbuf(ph, "ga", [128, 512], F32)
            st4 = sbuf(ph, "st4", [128, 4], F32)
            tmpf = sbuf(ph, "tmpf", [128, D], F32)
            xo = [sbuf(ph, "xo%d" % i, [128, D], F32) for i in range(2)]
            tpb = psum(ph, "tpb", [128, 1024], BF16)
            pz = [psum(ph, "pz%d" % i, [128, 512], F32) for i in range(4)]
            po = psum(ph, "po", [128, 1024], F32)
            pool.memset(hid[:], 0.0)
            xown = H if (last and split) else nx
            xval = min(nx, xown + 1)
            segs = ([(0, NCTX, NCTX, 1)] if ctxout else []) + [(NCTX, xown, xval, 0)]
            blocks = []
            for (s0, so, sl, w) in segs:
                for b0 in range(0, so, OWN):
                    blocks.append((s0, sl, w, b0, min(OWN, so - b0)))
            nwl = [0]
            total_w = len(blocks) * 22

            def loadw():
                if nwl[0] < total_w:
                    i_ = nwl[0] % 22
                    k.dma(out=wst[nwl[0] % NW][:].rearrange("p k c -> p (k c)"), in_=WUPS[i_, :, :])
                    nwl[0] += 1

            def load(n):
                s0, sl, w, b0, own = blocks[n]
                X = xb[n % 2]
                lo, hi = b0 - 1, b0 - 1 + 512
                vlo, vhi = max(lo, 0), min(hi, sl)
                for s in range(4):
                    a, b_ = lo + 128 * s, lo + 128 * (s + 1)
                    va, vb_ = max(a, vlo), min(b_, vhi)
                    if va >= vb_:
                        pool.memset(X[:, s, :], 0.0)
                        continue
                    if va > a or vb_ < b_:
                        pool.memset(X[:, s, :], 0.0)
                    k.dma(out=X[va - a:vb_ - a, s, :], in_=XS[s0 + va:s0 + vb_, :])
            load(0)
            for _ in range(NW - 1):
                loadw()
            nuse = 0
            for n, (s0, sl, w, b0, own) in enumerate(blocks):
                if n + 1 < len(blocks):
                    load(n + 1)
                X = xb[n % 2]
                lo = b0 - 1
                vlo, vhi = max(lo, 0), min(lo + 512, sl)
                for s in range(4):
                    act.activation(junk[:], X[:, s, :], AF.Square, accum_out=st4[:, 0:1])
                    rstd_from_ssq(st4[:, 0:1], st4[:, 2:3], D, st4[:, 1:2])
                    dve.tensor_scalar(xsb[:], X[:, s, :], st4[:, 2:3], None, ALU.mult)
                    for c in range(8):
                        pe.transpose(tpb[:, c * 128:(c + 1) * 128], xsb[:, c * 128:(c + 1) * 128], identb[:])
                    for c in range(8):
                        if c % 2 == 0:
                            dve.tensor_scalar(hxT[:, c, s * 128:(s + 1) * 128], tpb[:, c * 128:(c + 1) * 128], GS[:, 1, c, w:w + 1],
                                              SH[:, 1, c, w:w + 1], ALU.mult, ALU.add)
                        else:
                            act.activation(hxT[:, c, s * 128:(s + 1) * 128], tpb[:, c * 128:(c + 1) * 128], AF.Identity,
                                           bias=SH[:, 1, c, w:w + 1], scale=GS[:, 1, c, w:w + 1])
                if vlo - lo > 0:
                    pool.memset(hxT[:, :, 0:vlo - lo], 0.0)
                if vhi - lo < 512:
                    pool.memset(hxT[:, :, vhi - lo:512], 0.0)
                for i in range(22):
                    loadw()
                    wt = wst[nuse % NW]
                    nuse += 1
                    pa, pg_ = pz[(2 * i) % 4], pz[(2 * i + 1) % 4]
                    for (pp, hf) in ((pa, 0), (pg_, 1)):
                        for kc in range(8):
                            pe.matmul(pp[:], lhsT=wt[:, kc, hf * 128:(hf + 1) * 128], rhs=hxT[:, kc, :], start=(kc == 0), stop=(kc == 7))
                    for (pp, f, cc_) in ((pa, i, ca), (pg_, 22 + i, cg)):
                        act.activation(cc_[:, 1:511], pp[:, 1:511], AF.Identity, bias=cb[:, f:f + 1], scale=cw[:, f, 1:2])
                        dve.scalar_tensor_tensor(cc_[:, 1:511], pp[:, 0:510], cw[:, f, 0:1], cc_[:, 1:511], ALU.mult, ALU.add)
                        dve.scalar_tensor_tensor(cc_[:, 1:511], pp[:, 2:512], cw[:, f, 2:3], cc_[:, 1:511], ALU.mult, ALU.add)
                    act.activation(ga[:, 1:511], ca[:, 1:511], AF.Gelu_apprx_tanh)
                    dve.tensor_tensor(hid[:, i, 1:511], ga[:, 1:511], cg[:, 1:511], ALU.mult)
                for s in range(4):
                    ca_, cb_ = max(1, 128 * s), min(own + 1, 128 * (s + 1))
                    if ca_ >= cb_:
                        continue
                    for nn in range(2):
                        for i in range(22):
                            pe.matmul(po[:, nn * 512:(nn + 1) * 512], lhsT=hid[:, i, s * 128:(s + 1) * 128], rhs=wdn[:, i, nn * 512:(nn + 1) * 512],
                                      start=(i == 0), stop=(i == 21))
                    xo_ = xo[s % 2]
                    epilogue(po, X[:, s, :], GB[1][w], xo_, st4, junk, tmpf)
                    r0 = lo + ca_
                    p0, p1 = ca_ - 128 * s, cb_ - 128 * s
                    if last and w == 0:
                        k.dma(out=yout[r0:r0 + (p1 - p0), :], in_=xo_[p0:p1, :], wkey=("y", n, s))
                    else:
                        k.dma(out=XS[s0 + r0:s0 + r0 + (p1 - p0), :], in_=xo_[p0:p1, :])
        k.barrier()

    phases = []
    for l in range(depth):
        ctxout = l < depth - 1
        last = l == depth - 1
        phases += [("p0", lambda l=l: phase0(l)), ("p1", lambda l=l, c=ctxout, la=last: phase1(l, c, la)),
                   ("p2", lambda l=l, c=ctxout, la=last: phase2(l, c, la)), ("p3", lambda l=l, c=ctxout, la=last: phase3(l, c, la)),
                   ("p4", lambda l=l, c=ctxout, la=last: phase4(l, c, la)), ("p5", lambda l=l, c=ctxout, la=last: phase5(l, c, la)),
                   ("p6", lambda l=l, c=ctxout, la=last: phase6(l, c, la))]
    for n, (nm, fn) in enumerate(phases):
        fn()
        if stop_after is not None and n + 1 >= stop_after:
            break
    k.finish()
    top.close()
    return nc, k


_CACHE = {}


def _shapes(m):
    return {k_: (v.shape, "bf16" if v.dtype == ml_dtypes.bfloat16 else "f32") for k_, v in m.items()}


def run(inputs, nx, batches, stop_after=None, dbg=False, depth=DEPTH):
    inputs = {k_: np.asarray(v) for k_, v in inputs.items()}
    maps = [_layout_inputs(inputs, b, nx, fl) for (b, fl) in batches]
    key = (nx, stop_after, dbg, depth)
    if key not in _CACHE:
        _CACHE[key] = build(nx, _shapes(maps[0]), depth=depth, stop_after=stop_after, dbg=dbg)
    nc, kb = _CACHE[key]
    res = run_bass_kernel_spmd(nc, maps, core_ids=list(range(len(maps))))
    return res


def kernel(**inputs):
    nx = inputs["x"].shape[1]
    nb = inputs["x"].shape[0]
    batches = [(i // 2, bool(i % 2)) for i in range(2 * nb)]
    res = run(inputs, nx, batches)
    out = np.empty((nb, nx, D), np.float32)
    h = nx // 2
    for b in range(nb):
        out[b, :h] = np.asarray(res.results[2 * b]["y"], dtype=np.float32)
        out[b, h:] = np.asarray(res.results[2 * b + 1]["y"], dtype=np.float32)[::-1]
    return out
```

```python
import math, os
CUT = int(os.environ.get('K_CUT', '99'))
ASUB = int(os.environ.get('K_ASUB', '99'))
DSUB = int(os.environ.get('K_DSUB', '99'))
DX = int(os.environ.get('K_DX', '99'))
from contextlib import ExitStack
import numpy as np
import ml_dtypes
import concourse.bass as bass
import concourse.mybir as mybir
from concourse.bass_utils import run_bass_kernel_spmd

F32 = mybir.dt.float32
BF16 = mybir.dt.bfloat16
AF = mybir.ActivationFunctionType
ALU = mybir.AluOpType
AX = mybir.AxisListType
AP = bass.AP

D = 1024
NCTX = 256
DEPTH = 2
DFF = 2816
EPS = 1e-6


class _Buf:
    __slots__ = ("w", "r", "ep")

    def __init__(self):
        self.w = None
        self.r = {}
        self.ep = -1


class _Eng:
    def __init__(self, k, name, e, sem):
        self.k, self.name, self.e, self.sem = k, name, e, sem
        self.cnt = 0
        self.seen = {}
        self.prog = []

    def __getattr__(self, op):
        f = getattr(self.e, op)

        def call(*a, **kw):
            if self.k.rec is not None:
                self.k.rec.append((self, f, a, kw, False, None, None))
                return None
            return self.k._emit(self, f, a, kw)
        return call


class KB:
    NDMA = 40

    def __init__(self, nc, stack):
        self.nc = nc
        mk = lambda n: stack.enter_context(nc.semaphore(n))
        self.pe = _Eng(self, "pe", nc.tensor, mk("s_pe"))
        self.dve = _Eng(self, "dve", nc.vector, mk("s_dve"))
        self.act = _Eng(self, "act", nc.scalar, mk("s_act"))
        self.pool = _Eng(self, "pool", nc.gpsimd, mk("s_pool"))
        self.sp = _Eng(self, "sp", nc.sync, mk("s_sp"))
        self.engs = [self.pe, self.dve, self.act, self.pool, self.sp]
        self.dsem = [mk("s_d%d" % i) for i in range(self.NDMA)]
        self.dval = [0] * self.NDMA
        self.dnext = 0
        self.bufs = {}
        self.epoch = 0
        self.ninst = 0
        self.rec = None

    def record(self):
        self.rec = []
        return self.rec

    def stop(self):
        self.rec = None

    def replay(self, lists):
        items = []
        for li, lst in enumerate(lists):
            n = len(lst)
            for j, it in enumerate(lst):
                items.append(((j + 0.5) / n, li, j, it))
        items.sort(key=lambda z: (z[0], z[1], z[2]))
        for _, _, _, (eng, f, a, kw, dma, wkey, rkey) in items:
            self._emit(eng, f, a, kw, dma=dma, wkey=wkey, rkey=rkey)

    def _buf(self, key):
        b = self.bufs.get(key)
        if b is None:
            b = self.bufs[key] = _Buf()
        if b.ep != self.epoch:
            b.w, b.r, b.ep = None, {}, self.epoch
        return b

    def _need(self, eng, ev, waits):
        sem, val, owner = ev
        if owner is eng and eng is self.pe:
            return
        if eng.seen.get(id(sem), 0) >= val:
            return
        eng.seen[id(sem)] = val
        waits.append((sem, val))

    def _emit(self, eng, f, a, kw, dma=False, wkey=None, rkey=None):
        wk, rk = [], []
        for i, x in enumerate(a):
            if isinstance(x, AP):
                (wk if i == 0 else rk).append(x)
        for n, x in kw.items():
            if isinstance(x, AP):
                (wk if n in ("out", "accum_out") else rk).append(x)

        def key(x, dk):
            if dk is not None and str(x.space) == "DRAM":
                return (x.name, dk)
            return x.name
        wb = [self._buf(key(x, wkey)) for x in wk]
        rb = [self._buf(key(x, rkey)) for x in rk]
        waits = []
        for b in rb:
            if b.w is not None:
                self._need(eng, b.w, waits)
        for b in wb:
            if b.w is not None:
                self._need(eng, b.w, waits)
            for ev in b.r.values():
                self._need(eng, ev, waits)
        if dma:
            i = self.dnext
            self.dnext = (i + 1) % self.NDMA
            if self.dval[i] > 0:
                self._need(eng, (self.dsem[i], self.dval[i], None), waits)
            self.dval[i] += 16
            ev = (self.dsem[i], self.dval[i], None)
            inc = (self.dsem[i], 16)
        else:
            eng.cnt += 1
            ev = (eng.sem, eng.cnt, eng)
            inc = (eng.sem, 1)
        eng.prog.append((waits, f, a, kw, inc))
        self.ninst += 1
        for b in wb:
            b.w = ev
            b.r = {}
        for b in rb:
            if b.w is not ev:
                b.r[id(ev[0])] = ev
        return ev

    def dma(self, out, in_, wkey=None, rkey=None):
        q = self.sp
        if self.rec is not None:
            self.rec.append((q, q.e.dma_start, (), {"out": out, "in_": in_}, True, wkey, rkey))
            return None
        return self._emit(q, q.e.dma_start, (), {"out": out, "in_": in_}, dma=True, wkey=wkey, rkey=rkey)

    def barrier(self):
        for e in self.engs:
            waits = []
            for o in self.engs:
                if o.cnt > 0 and not (o is e and e is self.pe):
                    self._need(e, (o.sem, o.cnt, o), waits)
            for i in range(self.NDMA):
                if self.dval[i] > 0:
                    self._need(e, (self.dsem[i], self.dval[i], None), waits)
            if waits:
                e.prog.append((waits, None, (), {}, None))
        self.epoch += 1

    def finish(self):
        self.barrier()
        with self.nc.Block() as block:
            for e, deco in ((self.sp, block.sync), (self.pe, block.tensor), (self.dve, block.vector),
                            (self.act, block.scalar), (self.pool, block.gpsimd)):
                def body(_x, e=e):
                    for waits, f, a, kw, inc in e.prog:
                        for sem, val in waits:
                            e.e.wait_ge(sem, val)
                        if f is not None:
                            f(*a, **kw).then_inc(inc[0], inc[1])
                deco(body)


def _bf(a):
    return np.ascontiguousarray(a).astype(ml_dtypes.bfloat16)


def _rep(row, n=128):
    return np.ascontiguousarray(np.broadcast_to(np.asarray(row, np.float32).reshape(1, -1), (n, row.size)))


def _consts(nx):
    c = {}
    c["identb"] = _bf(np.eye(128, dtype=np.float32))
    c["identf"] = np.eye(128, dtype=np.float32)
    r = np.arange(128)[:, None]
    t = np.arange(128)[None, :]
    s = np.float32(-1.0 / 16.0)
    tri = np.stack([(r <= t), (r > t), (r >= t), (r < t)]).astype(np.float32) * s
    c["tri"] = np.ascontiguousarray(tri.transpose(1, 0, 2))
    mf = (r <= t).astype(np.float32)
    mb = (r >= t).astype(np.float32)
    c["maskf"] = np.ascontiguousarray(np.tile(mf, (1, 4)))
    c["maskb"] = np.ascontiguousarray(np.tile(mb, (1, 4)))
    pos = np.arange(nx)
    row = (pos // 64).astype(np.float32)
    col = (pos % 64).astype(np.float32)
    inv = (10000.0 ** (-np.arange(8, dtype=np.float32) / 8)).astype(np.float32)
    ar = row[:, None] * inv[None, :]
    ac = col[:, None] * inv[None, :]
    cosf = np.concatenate([np.cos(ar), np.cos(ar), np.cos(ac), np.cos(ac)], axis=1).astype(np.float32)
    sinf = np.concatenate([-np.sin(ar), np.sin(ar), -np.sin(ac), np.sin(ac)], axis=1).astype(np.float32)
    c["ropec"] = np.ascontiguousarray(np.tile(cosf, (1, 4)))
    c["ropes"] = np.ascontiguousarray(np.tile(sinf, (1, 4)))
    return c


def _layout_inputs(inp, b, nx, flip=False):
    L = DEPTH
    m = {}
    if flip:
        inp = dict(inp)
        inp["x"] = inp["x"][:, ::-1]
        inp["ctx"] = inp["ctx"][:, ::-1]
        wi = inp["w_in"].copy()
        wi[:, :, 1536:1552] = inp["w_in"][:, :, 1552:1568]
        wi[:, :, 1552:1568] = inp["w_in"][:, :, 1536:1552]
        inp["w_in"] = wi
        inp["gla_w_gate"] = inp["gla_w_gate"][:, ::-1]
        inp["gla_b_gate"] = inp["gla_b_gate"][:, ::-1]
        inp["sgu_w"] = inp["sgu_w"][:, :, ::-1, ::-1]
        inp["sgu_b"] = inp["sgu_b"][:, :, ::-1]
        for n_ in ("s5_a_re", "s5_a_im", "s5_log_dt", "s5_b_re", "s5_b_im", "s5_c_re", "s5_c_im"):
            inp[n_] = np.ascontiguousarray(inp[n_][:, ::-1])
        inp["ffn_conv_w"] = inp["ffn_conv_w"][:, ::-1]
    m["xin"] = np.ascontiguousarray(inp["x"][b])
    m["cin"] = np.ascontiguousarray(inp["ctx"][b])
    cT = np.stack([inp["c"][b].reshape(8, 128).T, inp["c_ctx"].reshape(8, 128).T], axis=-1)
    m["cT"] = np.ascontiguousarray(cT.astype(np.float32))
    m["w_mod"] = inp["w_mod"]
    m["bmodT"] = np.ascontiguousarray(inp["b_mod"].reshape(L, 48, 128).transpose(0, 2, 1))
    gv = np.stack([inp[n].reshape(L, 8, 128).transpose(0, 2, 1) for n in
                   ("g_pre_mix", "g_post_mix", "g_pre_ffn", "g_post_ffn")], axis=2)
    m["gvT"] = np.ascontiguousarray(gv)
    m["w_in"] = inp["w_in"]
    m["sguwT"] = np.ascontiguousarray(inp["sgu_w"].transpose(0, 3, 1, 2))
    m["sgub"] = np.ascontiguousarray(inp["sgu_b"].transpose(0, 2, 1))
    m["sgun"] = np.stack([_rep(inp["sgu_norm"][l].reshape(-1)) for l in range(L)])
    wg = np.zeros((L, 32, 512), np.float32)
    wg[:, 0:16, 0:256] = inp["gla_w_gate"][:, 0]
    wg[:, 16:32, 256:512] = inp["gla_w_gate"][:, 1]
    m["wg"] = wg
    m["bg"] = np.stack([_rep(inp["gla_b_gate"][l].reshape(-1)) for l in range(L)])
    m["glan"] = np.stack([_rep(inp["gla_norm"][l].reshape(-1)) for l in range(L)])
    def st(a):
        return np.ascontiguousarray(a.reshape(L, 2, 8, 128).transpose(0, 1, 3, 2))
    m["s5are"] = st(inp["s5_a_re"])
    m["s5aim"] = st(inp["s5_a_im"])
    m["s5ldt"] = st(np.ascontiguousarray(np.broadcast_to(inp["s5_log_dt"][..., None], (L, 2, 16, 64))))
    def stb(a):
        return np.ascontiguousarray(a.reshape(L, 2, 8, 128, 16).transpose(0, 1, 3, 2, 4))
    m["s5bre"] = stb(inp["s5_b_re"])
    m["s5bim"] = stb(inp["s5_b_im"])
    def stc(a):
        return np.ascontiguousarray(a.reshape(L, 2, 8, 2, 16, 64).transpose(0, 1, 3, 5, 2, 4).reshape(L, 2, 128, 8, 16))
    m["s5cre"] = stc(inp["s5_c_re"])
    m["s5cim"] = stc(inp["s5_c_im"])
    m["s5dT"] = np.ascontiguousarray(inp["s5_d"].reshape(L, 2, 128).transpose(0, 2, 1))
    m["s5wglu"] = inp["s5_w_glu"]
    m["s5bgluT"] = np.ascontiguousarray(inp["s5_b_glu"].reshape(L, 2, 128).transpose(0, 2, 1))
    m["qn"] = np.stack([_rep(inp["mla_q_norm"][l]) for l in range(L)])
    m["kvn"] = np.stack([_rep(inp["mla_kv_norm"][l]) for l in range(L)])
    m["wuq"] = inp["mla_w_uq"]
    m["wukv"] = inp["mla_w_ukv"]
    m["w_out"] = inp["w_out"]
    m["w_up"] = inp["ffn_w_up"]
    m["convwT"] = np.ascontiguousarray(inp["ffn_conv_w"].reshape(L, 3, 44, 128).transpose(0, 3, 2, 1))
    m["convbT"] = np.ascontiguousarray(inp["ffn_conv_b"].reshape(L, 44, 128).transpose(0, 2, 1))
    m["w_dn"] = inp["ffn_w_down"]
    m.update(_consts(nx))
    if flip:
        m["ropec"] = np.ascontiguousarray(m["ropec"][::-1])
        m["ropes"] = np.ascontiguousarray(m["ropes"][::-1])
    return {k: np.ascontiguousarray(v) for k, v in m.items()}


def build(nx, in_shapes, depth=DEPTH, stop_after=None, dbg=False, split=True):
    nc = bass.Bass("TRN2", target_bir_lowering=False)
    NT = NCTX + nx
    ntile = NT // 128
    I = {}
    for name, (shape, dt) in in_shapes.items():
        I[name] = nc.dram_tensor(name, list(shape), BF16 if dt == "bf16" else F32, kind="ExternalInput").ap()
    H = nx // 2 if split else nx
    yout = nc.dram_tensor("y", [H, D], F32, kind="ExternalOutput").ap()
    okind = "ExternalOutput" if dbg else "Internal"

    def scratch(name, shape, dt):
        return nc.dram_tensor(name, shape, dt, kind=okind).ap()
    XS = scratch("XS", [NT, D], F32)
    MIXT = scratch("MIXT", [D, NT], BF16)
    QIT = scratch("QIT", [2, ntile, 64, 512], BF16)
    KVD = scratch("KVD", [2, ntile, 64, 260], F32)
    OACC = scratch("OACC", [NT, 256], F32)
    RSD = scratch("RSD", [NT, 256], F32)
    UT = scratch("UT", [256, NT], F32)
    YF = scratch("YF", [256, NT], F32)
    QTD = scratch("QTD", [4, 128, NT], BF16)
    KTD = scratch("KTD", [4, 128, NT], BF16)
    VAD = scratch("VAD", [NT, 264], BF16)
    WUPS = scratch("WUPS", [22, 128, 2048], BF16)

    top = ExitStack()
    k = KB(nc, top)
    pe, dve, act, pool = k.pe, k.dve, k.act, k.pool
    uid = [0]

    def sbuf(st, name, shape, dt):
        uid[0] += 1
        return st.enter_context(nc.sbuf_tensor("%s_%d" % (name, uid[0]), list(shape), dt))

    def psum(st, name, shape, dt):
        uid[0] += 1
        return st.enter_context(nc.psum_tensor("%s_%d" % (name, uid[0]), list(shape), dt))

    identb = sbuf(top, "identb", [128, 128], BF16)
    identf = sbuf(top, "identf", [128, 128], F32)
    onesf = sbuf(top, "onesf", [128, 128], F32)
    epsc = sbuf(top, "epsc", [128, 1], F32)
    onec = sbuf(top, "onec", [128, 1], F32)
    n16c = sbuf(top, "n16c", [128, 1], F32)
    cT = sbuf(top, "cT", [128, 8, 2], F32)
    scT = sbuf(top, "scT", [128, 8, 2], F32)
    QMAX = sbuf(top, "QMAX", [128, 4], F32)
    KMAX = sbuf(top, "KMAX", [128, 4], F32)
    GS = sbuf(top, "GS", [128, 2, 8, 2], F32)
    SH = sbuf(top, "SH", [128, 2, 8, 2], F32)
    GTT = sbuf(top, "GTT", [128, 2, 8, 2], F32)
    GB = [[sbuf(top, "GB%d%d" % (s, w), [128, D], F32) for w in range(2)] for s in range(2)]

    k.dma(out=identb[:], in_=I["identb"][:, :])
    k.dma(out=identf[:], in_=I["identf"][:, :])
    k.dma(out=cT[:], in_=I["cT"][:, :, :])
    dve.memset(onesf[:], 1.0)
    dve.memset(epsc[:], EPS)
    dve.memset(onec[:], 1.0)
    dve.memset(n16c[:], -1.0 / 16.0)
    act.activation(scT[:], cT[:], AF.Silu)
    k.dma(out=XS[0:NCTX, :], in_=I["cin"][:, :])
    step = max(128, nx // 8)
    for r0 in range(0, nx, step):
        k.dma(out=XS[NCTX + r0:NCTX + r0 + step, :], in_=I["xin"][r0:r0 + step, :])
    k.barrier()

    def rstd_from_ssq(ssq, out, n, tmp):
        act.activation(tmp, ssq, AF.Sqrt, scale=1.0 / n, bias=epsc[0:tmp.shape[0], 0:1])
        dve.reciprocal(out, tmp)

    def phase0(l):
        with ExitStack() as ph:
            wm = [sbuf(ph, "wm%d" % i, [128, 8, 512], F32) for i in range(2)]
            bmT = sbuf(ph, "bmT", [128, 48], F32)
            gvT = sbuf(ph, "gvT", [128, 4, 8], F32)
            modT = sbuf(ph, "modT", [128, 48, 2], F32)
            tmp1 = sbuf(ph, "tmp1", [128, 8, 2], F32)
            dg = [sbuf(ph, "dg%d" % i, [128, 128], F32) for i in range(2)]
            pm = psum(ph, "pm", [128, 512], F32)
            pbk = [psum(ph, "pbk%d" % i, [128, 512], F32) for i in range(2)]
            k.dma(out=bmT[:], in_=I["bmodT"][l, :, :])
            k.dma(out=gvT[:], in_=I["gvT"][l, :, :, :])
            for blk in range(12):
                w = wm[blk % 2]
                k.dma(out=w[:], in_=I["w_mod"][l, :, blk * 512:(blk + 1) * 512].rearrange("(kc p) n -> p kc n", p=128))
                for j4 in range(4):
                    j = blk * 4 + j4
                    for kc in range(8):
                        pe.matmul(pm[:, 2 * j:2 * j + 2], lhsT=w[:, kc, j4 * 128:(j4 + 1) * 128], rhs=scT[:, kc, :],
                                  start=(kc == 0), stop=(kc == 7))
            dve.tensor_tensor(modT[:], pm[:, 0:96].rearrange("p (j w) -> p j w", w=2),
                              bmT[:].unsqueeze(2).broadcast_to([128, 48, 2]), ALU.add)
            for s in range(2):
                o = 24 * s
                dve.tensor_copy(SH[:, s], modT[:, o:o + 8, :])
                dve.tensor_scalar(tmp1[:], modT[:, o + 8:o + 16, :], 1.0, None, ALU.add)
                dve.tensor_tensor(GS[:, s], tmp1[:], gvT[:, 2 * s, :].unsqueeze(2).broadcast_to([128, 8, 2]), ALU.mult)
                dve.tensor_tensor(GTT[:, s], modT[:, o + 16:o + 24, :],
                                  gvT[:, 2 * s + 1, :].unsqueeze(2).broadcast_to([128, 8, 2]), ALU.mult)
            n = 0
            for s in range(2):
                for w in range(2):
                    for c in range(8):
                        d_ = dg[n % 2]
                        pb = pbk[(c // 4) % 2]
                        dve.tensor_scalar(d_[:], identf[:], GTT[:, s, c, w:w + 1], None, ALU.mult)
                        pe.matmul(pb[:, (c % 4) * 128:(c % 4 + 1) * 128], lhsT=onesf[:], rhs=d_[:], start=True, stop=True)
                        if c % 4 == 3:
                            act.copy(GB[s][w][:, (c // 4) * 512:(c // 4 + 1) * 512], pb[:])
                        n += 1
        k.barrier()

    def phase1(l, ctxout, last=False):
        OWNM = (H + 512) if (last and split) else nx
        with ExitStack() as ph:
            win = sbuf(ph, "win", [128, 8, 2176], BF16)
            stg = [sbuf(ph, "stg%d" % i, [128, 2176], F32) for i in range(2)]
            for kc in range(8):
                k.dma(out=stg[kc % 2][:], in_=I["w_in"][l, kc * 128:(kc + 1) * 128, :])
                (pool if kc % 2 else dve).tensor_copy(win[:, kc, :], stg[kc % 2][:])
            sgw_f = sbuf(ph, "sgw_f", [128, 4, 128], F32)
            sgw = sbuf(ph, "sgw", [128, 4, 128], BF16)
            sgb = sbuf(ph, "sgb", [128, 4], F32)
            sgn = sbuf(ph, "sgn", [128, 256], F32)
            wg_f = sbuf(ph, "wg_f", [32, 512], F32)
            wgb = sbuf(ph, "wgb", [32, 512], BF16)
            bg = sbuf(ph, "bg", [128, 512], F32)
            tri = sbuf(ph, "tri", [128, 4, 128], F32)
            maskf = sbuf(ph, "maskf", [128, 512], F32)
            maskb = sbuf(ph, "maskb", [128, 512], F32)
            qn = sbuf(ph, "qn", [128, 224], F32)
            kvn = sbuf(ph, "kvn", [128, 96], F32)
            wuq_f = sbuf(ph, "wuq_f", [128, 2, 384], F32)
            wuq = sbuf(ph, "wuq", [128, 2, 384], BF16)
            wukv_f = sbuf(ph, "wukv_f", [128, 512], F32)
            wukv = sbuf(ph, "wukv", [128, 512], BF16)
            k.dma(out=sgw_f[:], in_=I["sguwT"][l, :, :, :])
            k.dma(out=sgb[:], in_=I["sgub"][l, :, :])
            k.dma(out=sgn[:], in_=I["sgun"][l, :, :])
            k.dma(out=wg_f[:], in_=I["wg"][l, :, :])
            k.dma(out=bg[:], in_=I["bg"][l, :, :])
            k.dma(out=tri[:], in_=I["tri"][:, :, :])
            k.dma(out=maskf[:], in_=I["maskf"][:, :])
            k.dma(out=maskb[:], in_=I["maskb"][:, :])
            k.dma(out=qn[:], in_=I["qn"][l, :, :])
            k.dma(out=kvn[:], in_=I["kvn"][l, :, :])
            dve.memset(wuq_f[:], 0.0)
            dve.memset(wukv_f[:], 0.0)
            k.dma(out=wuq_f[:, 0, :], in_=I["wuq"][l, 0:128, :])
            k.dma(out=wuq_f[0:96, 1, :], in_=I["wuq"][l, 128:224, :])
            k.dma(out=wukv_f[0:96, :], in_=I["wukv"][l, :, :])
            dve.tensor_copy(sgw[:], sgw_f[:])
            dve.tensor_copy(wgb[:], wg_f[:])
            dve.tensor_copy(wuq[:], wuq_f[:])
            dve.tensor_copy(wukv[:], wukv_f[:])
            dve.memset(QMAX[:], 0.0)
            dve.memset(KMAX[:], 0.0)

            xt = [sbuf(ph, "xt%d" % i, [128, D], F32) for i in range(2)]
            rc = [sbuf(ph, "rc%d" % i, [128, 128], F32) for i in range(3)]
            rs_ = [sbuf(ph, "rs%d" % i, [128, 128], F32) for i in range(3)]
            junk = sbuf(ph, "junk", [128, D], BF16)
            st4 = sbuf(ph, "st4", [128, 16], F32)
            st4y = sbuf(ph, "st4y", [128, 16], F32)
            junky = sbuf(ph, "junky", [128, 512], BF16)
            xsb = sbuf(ph, "xsb", [128, D], BF16)
            hxT = sbuf(ph, "hxT", [128, 8, 128], BF16)
            P2 = [sbuf(ph, "P%d" % i, [128, 2176], F32) for i in range(2)]
            GA = sbuf(ph, "GA", [128, 512], F32)
            t256a = sbuf(ph, "t256a", [128, 256], F32)
            t256b = sbuf(ph, "t256b", [128, 256], F32)
            vnb = sbuf(ph, "vnb", [128, 256], BF16)
            oab = sbuf(ph, "oab", [128, 256], BF16)
            mxT = sbuf(ph, "mxT", [128, 2, 128], BF16)
            glb = sbuf(ph, "glb", [128, 32], BF16)
            glT = sbuf(ph, "glT", [32, 128], BF16)
            zb = sbuf(ph, "zb", [128, 512], F32)
            Lg = sbuf(ph, "Lg", [128, 512], F32)
            E1 = sbuf(ph, "E1", [128, 256], F32)
            E2 = sbuf(ph, "E2", [128, 256], F32)
            E3 = sbuf(ph, "E3", [128, 256], F32)
            qin = sbuf(ph, "qin", [128, 256], BF16)
            kin = sbuf(ph, "kin", [128, 256], BF16)
            kend = sbuf(ph, "kend", [128, 256], BF16)
            vb = sbuf(ph, "vb", [128, 256], BF16)
            qkT = [sbuf(ph, "qkT%d" % i, [64, 1024], BF16) for i in range(2)]
            kvd = [sbuf(ph, "kvd%d" % i, [64, 260], F32) for i in range(2)]
            scA = sbuf(ph, "scA", [128, 512], F32)
            scB = sbuf(ph, "scB", [128, 512], F32)
            scS = sbuf(ph, "scS", [128, 512], BF16)
            oin = sbuf(ph, "oin", [128, 256], F32)
            rsl = sbuf(ph, "rsl", [128, 256], F32)
            uTs = sbuf(ph, "uTs", [128, 2, 128], F32)
            cqn = sbuf(ph, "cqn", [128, 320], BF16)
            cTt = sbuf(ph, "cTt", [128, 384], BF16)
            QR = sbuf(ph, "QR", [128, 128], F32)
            KR = sbuf(ph, "KR", [128, 32], F32)
            KR2 = sbuf(ph, "KR2", [128, 32], F32)
            rA = sbuf(ph, "rA", [128, 128], F32)
            rB = sbuf(ph, "rB", [128, 128], F32)
            Qf = sbuf(ph, "Qf", [128, 4, 96], F32)
            Kf = sbuf(ph, "Kf", [128, 4, 96], F32)
            Qb = sbuf(ph, "Qb", [128, 4, 96], BF16)
            Kb = sbuf(ph, "Kb", [128, 4, 96], BF16)
            sq384 = sbuf(ph, "sq384", [128, 384], F32)
            VA = sbuf(ph, "VA", [128, 4, 66], BF16)
            QKT = sbuf(ph, "QKT", [128, 1024], BF16)
            dve.memset(VA[:], 1.0)
            dve.memset(cTt[:], 0.0)
            dve.memset(QKT[:], 0.0)

            tpb = psum(ph, "tpb", [128, 1024], BF16)
            wbb = psum(ph, "wbb", [128, 1024], BF16)
            pin = [psum(ph, "pin%d" % i, [128, 512], F32) for i in range(2)]
            w0 = psum(ph, "w0", [128, 512], F32)
            w1 = psum(ph, "w1", [128, 512], F32)
            w2 = psum(ph, "w2", [128, 512], F32)
            wz = psum(ph, "wz", [128, 512], F32)

            def load(i):
                r0 = i * 128
                k.dma(out=xt[i % 2][:], in_=XS[r0:r0 + 128, :], rkey=i)
                if i >= 2:
                    p0 = r0 - NCTX
                    k.dma(out=rc[i % 3][:], in_=I["ropec"][p0:p0 + 128, :])
                    k.dma(out=rs_[i % 3][:], in_=I["ropes"][p0:p0 + 128, :])

            def X(i):
                isctx = i < 2
                w = 1 if isctx else 0
                c0 = i * 128
                own = isctx or ((i - 2) * 128 < OWNM)
                P = P2[i % 2]
                x_ = xt[i % 2]
                act.activation(junk[:], x_[:], AF.Square, accum_out=st4[:, 0:1])
                rstd_from_ssq(st4[:, 0:1], st4[:, 2:3], D, st4[:, 1:2])
                dve.tensor_scalar(xsb[:], x_[:], st4[:, 2:3], None, ALU.mult)
                for c in range(8):
                    pe.transpose(tpb[:, c * 128:(c + 1) * 128], xsb[:, c * 128:(c + 1) * 128], identb[:])
                for c in range(8):
                    if c % 2 == 0:
                        dve.tensor_scalar(hxT[:, c, :], tpb[:, c * 128:(c + 1) * 128], GS[:, 0, c, w:w + 1],
                                          SH[:, 0, c, w:w + 1], ALU.mult, ALU.add)
                    else:
                        act.activation(hxT[:, c, :], tpb[:, c * 128:(c + 1) * 128], AF.Identity,
                                       bias=SH[:, 0, c, w:w + 1], scale=GS[:, 0, c, w:w + 1])
                for n in range(5):
                    n0 = n * 512
                    nw = min(512, 2176 - n0)
                    pb = pin[n % 2]
                    if n == 0 and not own:
                        continue
                    for kc in range(8):
                        pe.matmul(pb[:, 0:nw], lhsT=hxT[:, kc, :], rhs=win[:, kc, n0:n0 + nw], start=(kc == 0), stop=(kc == 7))
                    if n % 2 == 0:
                        act.copy(P[:, n0:n0 + nw], pb[:, 0:nw])
                    else:
                        dve.tensor_copy(P[:, n0:n0 + nw], pb[:, 0:nw])
                if ((not isctx) or ctxout) and own:
                    act.activation(GA[:], P[:, 0:512], AF.Gelu_apprx_tanh)
                    act.activation(t256a[:], GA[:, 256:512], AF.Square)
                    dve.tensor_reduce(st4[:, 4:8], t256a[:].rearrange("p (h d) -> p h d", h=4), AX.X, ALU.add)
                    rstd_from_ssq(st4[:, 4:8], st4[:, 12:16], 64, st4[:, 8:12])
                    dve.tensor_tensor(t256b[:].rearrange("p (h d) -> p h d", h=4), GA[:, 256:512].rearrange("p (h d) -> p h d", h=4),
                                      st4[:, 12:16].unsqueeze(2).broadcast_to([128, 4, 64]), ALU.mult)
                    dve.tensor_tensor(vnb[:], t256b[:], sgn[:], ALU.mult)
                    for h in range(4):
                        pe.matmul(w0[:, h * 64:(h + 1) * 64], lhsT=sgw[:, h, :], rhs=vnb[:, h * 64:(h + 1) * 64], start=True, stop=True)
                    dve.tensor_tensor(t256a[:].rearrange("p (h d) -> p h d", h=4), w0[:, 0:256].rearrange("p (h d) -> p h d", h=4),
                                      sgb[:].unsqueeze(2).broadcast_to([128, 4, 64]), ALU.add)
                    dve.tensor_tensor(oab[:], t256a[:], GA[:, 0:256], ALU.mult)
                    for c in range(2):
                        pe.transpose(tpb[:, c * 128:(c + 1) * 128], oab[:, c * 128:(c + 1) * 128], identb[:])
                    act.copy(mxT[:].rearrange("p c t -> p (c t)"), tpb[:, 0:256])
                    for c in range(2):
                        k.dma(out=MIXT[c * 128:(c + 1) * 128, c0:c0 + 128], in_=mxT[:, c, :], wkey=("a", i, c))
                for c in range(2):
                    pe.transpose(w0[:, 256 + c * 128:256 + (c + 1) * 128], P[:, 1568 + c * 128:1568 + (c + 1) * 128], identf[:])
                dve.tensor_copy(uTs[:].rearrange("p c t -> p (c t)"), w0[:, 256:512])
                for c in range(2):
                    k.dma(out=UT[c * 128:(c + 1) * 128, c0:c0 + 128], in_=uTs[:, c, :], wkey=(i, c))

            def Y(i):
                isctx = i < 2
                w = 1 if isctx else 0
                c0 = i * 128
                own = isctx or ((i - 2) * 128 < OWNM)
                P = P2[i % 2]
                PB = P[:, 512:1568]
                dve.tensor_copy(glb[:], PB[:, 1024:1056])
                pe.transpose(wbb[0:32, 256:384], glb[:], identb[:])
                act.copy(glT[:], wbb[0:32, 256:384])
                pe.matmul(wz[:], lhsT=glT[:], rhs=wgb[:], start=True, stop=True)
                dve.tensor_tensor(zb[:], wz[:], bg[:], ALU.add)
                act.activation(zb[:], zb[:], AF.Exp, scale=-1.0)
                act.activation(Lg[:], zb[:], AF.Ln, bias=onec[:, 0:1])
                act.copy(vb[:], PB[:, 512:768])
                if own:
                    act.activation(rsl[:], PB[:, 768:1024], AF.Silu)
                    k.dma(out=RSD[c0:c0 + 128, :], in_=rsl[:], wkey=i)
                for d in range(2):
                    if not own and d == 0:
                        continue
                    Ld = Lg[:, d * 256:(d + 1) * 256]
                    pe.matmul(w1[:, 0:256], lhsT=tri[:, 2 * d, :], rhs=Ld, start=True, stop=True)
                    pe.matmul(w1[:, 256:512], lhsT=tri[:, 2 * d + 1, :], rhs=Ld, start=True, stop=True)
                    act.activation(E3[:], w1[:, 256:512], AF.Exp)
                    dve.tensor_tensor(kend[:], PB[:, 256:512], E3[:], ALU.mult)
                    if own:
                        act.activation(E1[:], w1[:, 0:256], AF.Exp)
                        act.activation(E2[:], w1[:, 0:256], AF.Exp, scale=-1.0)
                        dve.scalar_tensor_tensor(qin[:], PB[:, 0:256], 0.125, E1[:], ALU.mult, ALU.mult)
                        dve.tensor_tensor(kin[:], PB[:, 256:512], E2[:], ALU.mult)
                        qk = qkT[d]
                        for h in range(4):
                            pe.transpose(wbb[0:64, h * 128:(h + 1) * 128], qin[:, h * 64:(h + 1) * 64], identb[:])
                        for h in range(4):
                            pe.transpose(wbb[0:64, 512 + h * 128:512 + (h + 1) * 128], kin[:, h * 64:(h + 1) * 64], identb[:])
                        dve.tensor_copy(qk[:], wbb[0:64, :])
                        k.dma(out=QIT[d, i, :, :], in_=qk[:, 0:512], wkey=(d, i))
                        for h in range(4):
                            pe.matmul(w2[:, h * 128:(h + 1) * 128], lhsT=qk[:, 512 + h * 128:512 + (h + 1) * 128],
                                      rhs=qk[:, h * 128:(h + 1) * 128], start=True, stop=True)
                        dve.tensor_tensor((scA if d == 0 else scB)[:], w2[:], (maskf if d == 0 else maskb)[:], ALU.mult)
                    for h in range(4):
                        pe.matmul(wz[0:64, h * 64:(h + 1) * 64], lhsT=kend[:, h * 64:(h + 1) * 64], rhs=vb[:, h * 64:(h + 1) * 64],
                                  start=True, stop=True)
                    for h in range(4):
                        pe.matmul(wz[0:64, 256 + h:257 + h], lhsT=Lg[:, d * 256 + h * 64:d * 256 + (h + 1) * 64], rhs=n16c[:, 0:1],
                                  start=True, stop=True)
                    dve.tensor_copy(kvd[d][:, 0:256], wz[0:64, 0:256])
                    act.activation(kvd[d][:, 256:260], wz[0:64, 256:260], AF.Exp)
                    k.dma(out=KVD[d, i, :, :], in_=kvd[d][:], wkey=(d, i))
                if own:
                    dve.tensor_tensor(scS[:], scA[:], scB[:], ALU.add)
                    for h in range(4):
                        pe.matmul(w1[:, h * 64:(h + 1) * 64], lhsT=scS[:, h * 128:(h + 1) * 128], rhs=vb[:, h * 64:(h + 1) * 64],
                                  start=True, stop=True)
                    act.copy(oin[:], w1[:, 0:256])
                    k.dma(out=OACC[c0:c0 + 128, :], in_=oin[:], wkey=i)
                PD = P[:, 1824:2176]
                act.activation(junky[:, 0:224], PD[:, 0:224], AF.Square, accum_out=st4y[:, 4:5])
                act.activation(junky[:, 256:352], PD[:, 224:320], AF.Square, accum_out=st4y[:, 5:6])
                act.activation(st4y[:, 8:9], st4y[:, 4:5], AF.Sqrt, scale=1.0 / 224, bias=epsc[:, 0:1])
                act.activation(st4y[:, 9:10], st4y[:, 5:6], AF.Sqrt, scale=1.0 / 96, bias=epsc[:, 0:1])
                dve.reciprocal(st4y[:, 12:14], st4y[:, 8:10])
                if own:
                    dve.scalar_tensor_tensor(cqn[:, 0:224], PD[:, 0:224], st4y[:, 12:13], qn[:], ALU.mult, ALU.mult)
                dve.scalar_tensor_tensor(cqn[:, 224:320], PD[:, 224:320], st4y[:, 13:14], kvn[:], ALU.mult, ALU.mult)
                if own:
                    pe.transpose(wbb[:, 0:128], cqn[:, 0:128], identb[:])
                    pe.transpose(wbb[0:96, 128:256], cqn[:, 128:224], identb[:])
                pe.transpose(wbb[0:96, 256:384], cqn[:, 224:320], identb[:])
                if own:
                    act.copy(cTt[:, 0:128], wbb[:, 0:128])
                    act.copy(cTt[0:96, 128:384], wbb[0:96, 128:384])
                else:
                    act.copy(cTt[0:96, 256:384], wbb[0:96, 256:384])
                if own:
                    pe.matmul(w1[:, 0:384], lhsT=cTt[:, 0:128], rhs=wuq[:, 0, :], start=True, stop=False)
                    pe.matmul(w1[:, 0:384], lhsT=cTt[:, 128:256], rhs=wuq[:, 1, :], start=False, stop=True)
                pe.matmul(w2[:], lhsT=cTt[:, 256:384], rhs=wukv[:], start=True, stop=True)
                q3 = w1[:, 0:384].rearrange("p (h e) -> p h e", h=4)
                kv3 = w2[:].rearrange("p (h e) -> p h e", h=4)
                if own:
                    dve.tensor_copy(Qf[:, :, 0:64], q3[:, :, 0:64])
                    dve.tensor_copy(QR[:].rearrange("p (h e) -> p h e", h=4), q3[:, :, 64:96])
                dve.tensor_copy(Kf[:, :, 0:64], kv3[:, :, 0:64])
                dve.tensor_copy(VA[:, :, 0:64], kv3[:, :, 64:128])
                if isctx:
                    pool.tensor_copy(Qf[:, :, 64:96], QR[:].rearrange("p (h e) -> p h e", h=4))
                    pool.tensor_copy(Kf[:, :, 64:96], PD[:, 320:352].unsqueeze(1).broadcast_to([128, 4, 32]))
                else:
                    cs, sn = rc[i % 3], rs_[i % 3]
                    if own:
                        dve.tensor_tensor(rA[:], QR[:], cs[:], ALU.mult)
                        QRv = QR[:].rearrange("p (g a j) -> p g a j", g=8, a=2)
                        snv = sn[:].rearrange("p (g a j) -> p g a j", g=8, a=2)
                        rBv = rB[:].rearrange("p (g a j) -> p g a j", g=8, a=2)
                        pool.tensor_tensor(rBv[:, :, 0, :], QRv[:, :, 1, :], snv[:, :, 0, :], ALU.mult)
                        pool.tensor_tensor(rBv[:, :, 1, :], QRv[:, :, 0, :], snv[:, :, 1, :], ALU.mult)
                        dve.tensor_tensor(Qf[:, :, 64:96], rA[:].rearrange("p (h e) -> p h e", h=4),
                                          rB[:].rearrange("p (h e) -> p h e", h=4), ALU.add)
                    pool.tensor_copy(KR[:], PD[:, 320:352])
                    dve.tensor_tensor(rA[:, 0:32], KR[:], cs[:, 0:32], ALU.mult)
                    KRv = KR[:].rearrange("p (g a j) -> p g a j", g=2, a=2)
                    sn2 = sn[:, 0:32].rearrange("p (g a j) -> p g a j", g=2, a=2)
                    rB2 = rB[:, 0:32].rearrange("p (g a j) -> p g a j", g=2, a=2)
                    pool.tensor_tensor(rB2[:, :, 0, :], KRv[:, :, 1, :], sn2[:, :, 0, :], ALU.mult)
                    pool.tensor_tensor(rB2[:, :, 1, :], KRv[:, :, 0, :], sn2[:, :, 1, :], ALU.mult)
                    dve.tensor_tensor(KR2[:], rA[:, 0:32], rB[:, 0:32], ALU.add)
                    pool.tensor_copy(Kf[:, :, 64:96], KR2[:].unsqueeze(1).broadcast_to([128, 4, 32]))
                if own:
                    dve.tensor_copy(Qb[:], Qf[:])
                act.copy(Kb[:].rearrange("p h e -> p (h e)"), Kf[:].rearrange("p h e -> p (h e)"))
                if (not isctx or ctxout) and own:
                    act.activation(sq384[:], Qf[:].rearrange("p h e -> p (h e)"), AF.Square)
                    dve.tensor_reduce(st4y[:, 4:8], sq384[:].rearrange("p (h e) -> p h e", h=4), AX.X, ALU.add)
                    dve.tensor_tensor(QMAX[:], QMAX[:], st4y[:, 4:8], ALU.max)
                act.activation(sq384[:], Kf[:].rearrange("p h e -> p (h e)"), AF.Square)
                dve.tensor_reduce(st4y[:, 8:12], sq384[:].rearrange("p (h e) -> p h e", h=4), AX.X, ALU.add)
                dve.tensor_tensor(KMAX[:], KMAX[:], st4y[:, 8:12], ALU.max)
                if own:
                    for h in range(4):
                        pe.transpose(wbb[0:96, h * 128:(h + 1) * 128], Qb[:, h, :], identb[:])
                for h in range(4):
                    pe.transpose(wbb[0:96, 512 + h * 128:512 + (h + 1) * 128], Kb[:, h, :], identb[:])
                if own:
                    act.copy(QKT[0:96, :], wbb[0:96, :])
                else:
                    act.copy(QKT[0:96, 512:1024], wbb[0:96, 512:1024])
                for h in range(4):
                    if own:
                        k.dma(out=QTD[h, :, c0:c0 + 128], in_=QKT[:, h * 128:(h + 1) * 128], wkey=(i, h))
                    k.dma(out=KTD[h, :, c0:c0 + 128], in_=QKT[:, 512 + h * 128:512 + (h + 1) * 128], wkey=(i, h))
                k.dma(out=VAD[c0:c0 + 128, :], in_=VA[:].rearrange("p h e -> p (h e)"), wkey=i)

            load(0)
            for t_ in range(ntile + 1):
                if t_ + 1 < ntile:
                    load(t_ + 1)
                chains = []
                if t_ < ntile:
                    chains.append(k.record())
                    X(t_)
                    k.stop()
                if t_ >= 1:
                    chains.append(k.record())
                    Y(t_ - 1)
                    k.stop()
                k.replay(chains)
        k.barrier()

    def phase2(l, ctxout, last=False):
        OWNM = (H + 512) if (last and split) else nx
        nown = 2 + OWNM // 128
        with ExitStack() as ph:
            S = sbuf(ph, "S", [64, 256], F32)
            Sb = sbuf(ph, "Sb", [64, 256], BF16)
            gln = sbuf(ph, "gln", [128, 256], F32)
            qit = [sbuf(ph, "qit%d" % i, [64, 512], BF16) for i in range(2)]
            kvd = [sbuf(ph, "kvd%d" % i, [64, 260], F32) for i in range(2)]
            oac = [sbuf(ph, "oac%d" % i, [128, 256], F32) for i in range(2)]
            rsl = [sbuf(ph, "rsl%d" % i, [128, 256], F32) for i in range(2)]
            osum = sbuf(ph, "osum", [128, 256], F32)
            t1 = sbuf(ph, "t1", [128, 256], F32)
            t2 = sbuf(ph, "t2", [128, 256], F32)
            st4 = sbuf(ph, "st4", [128, 16], F32)
            obb = sbuf(ph, "obb", [128, 256], BF16)
            mxT = sbuf(ph, "mxT", [128, 2, 128], BF16)
            ops_ = [psum(ph, "ops%d" % i, [128, 512], F32) for i in range(2)]
            wbb = psum(ph, "wbb", [128, 1024], BF16)
            k.dma(out=gln[:], in_=I["glan"][l, :, :])
            for d in range(2):
                order = list(range(min(ntile, nown))) if d == 0 else [1, 0] + list(range(ntile - 1, 1, -1))
                dve.memset(S[:], 0.0)

                def load(n):
                    i = order[n]
                    k.dma(out=qit[n % 2][:], in_=QIT[d, i, :, :], rkey=(d, i))
                    k.dma(out=kvd[n % 2][:], in_=KVD[d, i, :, :], rkey=(d, i))
                    k.dma(out=oac[n % 2][:], in_=OACC[i * 128:(i + 1) * 128, :], rkey=i)
                    if d == 1:
                        k.dma(out=rsl[n % 2][:], in_=RSD[i * 128:(i + 1) * 128, :], rkey=i)
                load(0)
                for n, i in enumerate(order):
                    if n + 1 < len(order):
                        load(n + 1)
                    need_o = ((i >= 2) or ctxout) and i < nown
                    q_, kv_, oa_ = qit[n % 2], kvd[n % 2], oac[n % 2]
                    if need_o:
                        act.copy(Sb[:], S[:])
                        op = ops_[n % 2]
                        for h in range(4):
                            pe.matmul(op[:, h * 64:(h + 1) * 64], lhsT=q_[:, h * 128:(h + 1) * 128], rhs=Sb[:, h * 64:(h + 1) * 64],
                                      start=True, stop=True)
                        dve.tensor_tensor(osum[:], op[:, 0:256], oa_[:], ALU.add)
                        if d == 0:
                            k.dma(out=OACC[i * 128:(i + 1) * 128, :], in_=osum[:], wkey=i)
                        else:
                            act.activation(t1[:], osum[:], AF.Square)
                            dve.tensor_reduce(st4[:, 0:4], t1[:].rearrange("p (h e) -> p h e", h=4), AX.X, ALU.add)
                            rstd_from_ssq(st4[:, 0:4], st4[:, 8:12], 64, st4[:, 4:8])
                            dve.tensor_tensor(t2[:].rearrange("p (h e) -> p h e", h=4), osum[:].rearrange("p (h e) -> p h e", h=4),
                                              st4[:, 8:12].unsqueeze(2).broadcast_to([128, 4, 64]), ALU.mult)
                            dve.tensor_tensor(t1[:], t2[:], gln[:], ALU.mult)
                            dve.tensor_tensor(obb[:], t1[:], rsl[n % 2][:], ALU.mult)
                            for c in range(2):
                                pe.transpose(wbb[:, c * 128:(c + 1) * 128], obb[:, c * 128:(c + 1) * 128], identb[:])
                            act.copy(mxT[:].rearrange("p c t -> p (c t)"), wbb[:, 0:256])
                            for c in range(2):
                                k.dma(out=MIXT[256 + c * 128:256 + (c + 1) * 128, i * 128:(i + 1) * 128], in_=mxT[:, c, :], wkey=("b", i, c))
                    dve.tensor_tensor(S[:].rearrange("p (h e) -> p h e", h=4), S[:].rearrange("p (h e) -> p h e", h=4),
                                      kv_[:, 256:260].unsqueeze(2).broadcast_to([64, 4, 64]), ALU.mult)
                    dve.tensor_tensor(S[:], S[:], kv_[:, 0:256], ALU.add)
                k.barrier()

    def phase3(l, ctxout, last=False):
        LC = 512
        OWNM = (H + 512) if (last and split) else nx
        with ExitStack() as ph:
            pr = sbuf(ph, "pr", [128, 2, 3, 8], F32)
            bre = sbuf(ph, "bre", [128, 2, 8, 16], F32)
            bim = sbuf(ph, "bim", [128, 2, 8, 16], F32)
            cre = sbuf(ph, "cre", [128, 2, 8, 16], F32)
            cim = sbuf(ph, "cim", [128, 2, 8, 16], F32)
            dT = sbuf(ph, "dT", [128, 2], F32)
            bgl = sbuf(ph, "bgl", [128, 2], F32)
            wgl_f = sbuf(ph, "wgl_f", [128, 2, 256], F32)
            wgl = sbuf(ph, "wgl", [128, 2, 256], BF16)
            for d in range(2):
                k.dma(out=pr[:, d, 0, :], in_=I["s5are"][l, d, :, :])
                k.dma(out=pr[:, d, 1, :], in_=I["s5aim"][l, d, :, :])
                k.dma(out=pr[:, d, 2, :], in_=I["s5ldt"][l, d, :, :])
                k.dma(out=bre[:, d], in_=I["s5bre"][l, d, :, :, :])
                k.dma(out=bim[:, d], in_=I["s5bim"][l, d, :, :, :])
                k.dma(out=cre[:, d], in_=I["s5cre"][l, d, :, :, :])
                k.dma(out=cim[:, d], in_=I["s5cim"][l, d, :, :, :])
            k.dma(out=dT[:], in_=I["s5dT"][l, :, :])
            k.dma(out=bgl[:], in_=I["s5bgluT"][l, :, :])
            k.dma(out=wgl_f[:], in_=I["s5wglu"][l, :, :].rearrange("(c p) n -> p c n", p=128))
            dve.tensor_copy(wgl[:], wgl_f[:])
            e = sbuf(ph, "e", [128, 2, 16, 8], F32)
            dve.memset(e[:], 0.0)
            A_RE, A_IM, LDT = pr[:, :, 0, :], pr[:, :, 1, :], pr[:, :, 2, :]
            DT, AR, TH, RM, S16, S8, C8, T0, T1, LR, LI = [e[:, :, n, :] for n in range(11)]
            act.activation(DT, LDT, AF.Exp)
            dve.tensor_tensor(AR, A_RE, DT, ALU.mult)
            dve.tensor_tensor(TH, A_IM, DT, ALU.mult)
            act.activation(RM, AR, AF.Exp)
            act.activation(S16, TH, AF.Sin, scale=1.0 / 16)
            act.activation(S8, TH, AF.Sin, scale=1.0 / 8)
            dve.tensor_tensor(T0, S16, S16, ALU.mult)
            dve.tensor_scalar(C8, T0, -2.0, 1.0, ALU.mult, ALU.add)
            cc, ss = C8, S8
            for it in range(3):
                dve.tensor_tensor(T0, cc, cc, ALU.mult)
                dve.tensor_tensor(T1, ss, ss, ALU.mult)
                dve.tensor_tensor(LI, cc, ss, ALU.mult)
                dve.tensor_tensor(T0, T0, T1, ALU.subtract)
                dve.tensor_scalar(S8, LI, 2.0, None, ALU.mult)
                dve.tensor_copy(C8, T0)
                cc, ss = C8, S8
            CT, ST = C8, S8
            dve.tensor_tensor(LR, RM, CT, ALU.mult)
            dve.tensor_tensor(LI, RM, ST, ALU.mult)
            NR, DEN, CR, CI, T2 = [e[:, :, n, :] for n in range(11, 16)]
            dve.tensor_scalar(NR, LR, -1.0, None, ALU.add)
            dve.tensor_tensor(T0, A_RE, A_RE, ALU.mult)
            dve.tensor_tensor(T1, A_IM, A_IM, ALU.mult)
            dve.tensor_tensor(DEN, T0, T1, ALU.add)
            dve.reciprocal(DEN, DEN)
            dve.tensor_tensor(T0, NR, A_RE, ALU.mult)
            dve.tensor_tensor(T1, LI, A_IM, ALU.mult)
            dve.tensor_tensor(T0, T0, T1, ALU.add)
            dve.tensor_tensor(CR, T0, DEN, ALU.mult)
            dve.tensor_tensor(T0, LI, A_RE, ALU.mult)
            dve.tensor_tensor(T1, NR, A_IM, ALU.mult)
            dve.tensor_tensor(T0, T0, T1, ALU.subtract)
            dve.tensor_tensor(CI, T0, DEN, ALU.mult)
            bbr = sbuf(ph, "bbr", [128, 2, 8, 16], F32)
            bbi = sbuf(ph, "bbi", [128, 2, 8, 16], F32)
            tb = sbuf(ph, "tb", [128, 2, 8, 16], F32)
            for d in range(2):
                crb = e[:, d, 13, :].unsqueeze(2).broadcast_to([128, 8, 16])
                cib = e[:, d, 14, :].unsqueeze(2).broadcast_to([128, 8, 16])
                dve.tensor_tensor(bbr[:, d], bre[:, d], crb, ALU.mult)
                dve.tensor_tensor(tb[:, d], bim[:, d], cib, ALU.mult)
                dve.tensor_tensor(bbr[:, d], bbr[:, d], tb[:, d], ALU.subtract)
                dve.tensor_tensor(bbi[:, d], bim[:, d], crb, ALU.mult)
                dve.tensor_tensor(tb[:, d], bre[:, d], cib, ALU.mult)
                dve.tensor_tensor(bbi[:, d], bbi[:, d], tb[:, d], ALU.add)
            BT = sbuf(ph, "BT", [128, 2, 2, 8, 128], BF16)
            CP = sbuf(ph, "CP", [128, 2, 2, 8, 128], BF16)
            Z = [sbuf(ph, "Z%d" % i, [128, 128], F32) for i in range(2)]
            pz = [psum(ph, "pz%d" % i, [128, 512], F32) for i in range(2)]
            dve.memset(CP[:], 0.0)
            dve.memset(Z[0][:], 0.0)
            dve.memset(Z[1][:], 0.0)
            n = 0
            for d in range(2):
                for ri, src in enumerate((bbr, bbi)):
                    for j in range(8):
                        jj = j % 4
                        z = Z[n % 2]
                        if jj != (j - 1) % 4 or True:
                            pool.memset(z[:], 0.0)
                        pool.tensor_copy(z[0:64, 32 * jj:32 * jj + 16], src[0:64, d, j, :])
                        pool.tensor_copy(z[64:128, 32 * jj + 16:32 * jj + 32], src[64:128, d, j, :])
                        pe.transpose(pz[n % 2][:, 0:128], z[:], identf[:])
                        act.copy(BT[:, d, ri, j, :], pz[n % 2][:, 0:128])
                        n += 1
                for j in range(8):
                    jj = j % 4
                    dve.tensor_copy(CP[0:64, d, 0, j, 32 * jj:32 * jj + 16], cre[0:64, d, j, :])
                    dve.tensor_copy(CP[64:128, d, 0, j, 32 * jj + 16:32 * jj + 32], cre[64:128, d, j, :])
                    dve.tensor_scalar(CP[0:64, d, 1, j, 32 * jj:32 * jj + 16], cim[0:64, d, j, :], -1.0, None, ALU.mult)
                    dve.tensor_scalar(CP[64:128, d, 1, j, 32 * jj + 16:32 * jj + 32], cim[64:128, d, j, :], -1.0, None, ALU.mult)
            RR = sbuf(ph, "RR", [128, 2, 8, LC], F32)
            RI = sbuf(ph, "RI", [128, 2, 8, LC], F32)
            tt = sbuf(ph, "tt", [128, LC], F32)
            for d in range(2):
                for j in range(8):
                    dve.tensor_copy(RR[:, d, j, 0:1], e[:, d, 6, j:j + 1])
                    dve.tensor_copy(RI[:, d, j, 0:1], e[:, d, 5, j:j + 1])
                    wdt = 1
                    while wdt < LC:
                        cw = RR[:, d, j, wdt - 1:wdt]
                        sw = RI[:, d, j, wdt - 1:wdt]
                        a_r, a_i = RR[:, d, j, 0:wdt], RI[:, d, j, 0:wdt]
                        dve.tensor_scalar(tt[:, 0:wdt], a_i, sw, None, ALU.mult)
                        dve.scalar_tensor_tensor(RR[:, d, j, wdt:2 * wdt], a_r, cw, tt[:, 0:wdt], ALU.mult, ALU.subtract)
                        dve.tensor_scalar(tt[:, 0:wdt], a_i, cw, None, ALU.mult)
                        dve.scalar_tensor_tensor(RI[:, d, j, wdt:2 * wdt], a_r, sw, tt[:, 0:wdt], ALU.mult, ALU.add)
                        wdt *= 2
            uTf = [sbuf(ph, "uTf%d" % i, [128, 2, LC], F32) for i in range(2)]
            uTb = sbuf(ph, "uTb", [128, 2, LC], BF16)
            yfl = [sbuf(ph, "yfl%d" % i, [128, 2, LC], F32) for i in range(2)]
            ta_ = [sbuf(ph, "ta%d" % i, [128, LC], F32) for i in range(2)]
            tb2_ = [sbuf(ph, "tb2%d" % i, [128, LC], F32) for i in range(2)]
            tc_ = [sbuf(ph, "tc%d" % i, [128, LC], F32) for i in range(2)]
            td_ = [sbuf(ph, "td%d" % i, [128, LC], F32) for i in range(2)]
            btr_ = [sbuf(ph, "btr%d" % i, [128, LC], F32) for i in range(2)]
            bti_ = [sbuf(ph, "bti%d" % i, [128, LC], F32) for i in range(2)]
            wr_ = [sbuf(ph, "wr%d" % i, [128, LC], F32) for i in range(2)]
            wi_ = [sbuf(ph, "wi%d" % i, [128, LC], F32) for i in range(2)]
            PR = [sbuf(ph, "PR0", [128, 4, 4, LC], BF16)] * 2
            h0 = sbuf(ph, "h0", [128, 2, 8], F32)
            hc_ = [sbuf(ph, "hc%d" % i, [128, 4], F32) for i in range(2)]
            ysum = sbuf(ph, "ysum", [128, 2, LC], F32)
            ygf = sbuf(ph, "ygf", [128, 2, LC], F32)
            ygb = sbuf(ph, "ygb", [128, 2, LC], BF16)
            sg = sbuf(ph, "sg", [128, LC], F32)
            ocb = sbuf(ph, "ocb", [128, 2, LC], BF16)
            pbr_ = pz
            pbi_ = [psum(ph, "pbi%d" % i, [128, 512], F32) for i in range(2)]
            py = [psum(ph, "py%d" % i, [128, 512], F32) for i in range(2)]
            pg = psum(ph, "pg", [128, 512], F32)
            chunks = [(0, NCTX)] + [(NCTX + c * LC, min(LC, nx - c * LC)) for c in range((nx + LC - 1) // LC)]
            for d in range(2):
                nownc = 1 + min(len(chunks) - 1, OWNM // LC)
                order = list(range(nownc)) if d == 0 else [0] + list(range(len(chunks) - 1, 0, -1))
                dve.memset(h0[:], 0.0)

                def load(n):
                    t0, Lc = chunks[order[n]]
                    k.dma(out=uTf[n % 2][:, :, 0:Lc], in_=UT[:, t0:t0 + Lc].rearrange("(c p) t -> p c t", p=128), rkey=None)
                    if d == 1:
                        k.dma(out=yfl[n % 2][:, :, 0:Lc], in_=YF[:, t0:t0 + Lc].rearrange("(c p) t -> p c t", p=128), rkey=None)
                load(0)
                for n, ci in enumerate(order):
                    if n + 1 < len(order):
                        load(n + 1)
                    t0, Lc = chunks[ci]
                    need_y = ((ci > 0) or ctxout) and ci < nownc
                    uf = uTf[n % 2]
                    for c_ in range(2):
                        act.copy(uTb[:, c_, 0:Lc], uf[:, c_, 0:Lc])

                    def tv(ap):
                        return ap if d == 0 else ap[:, ::-1]
                    def ymm(c):
                        pp = PR[0]
                        for j2 in range(4):
                            for pi in range(4):
                                pe.matmul(py[c][:, 0:Lc], lhsT=CP[:, d, pi // 2, 4 * c + j2, :], rhs=pp[:, j2, pi, 0:Lc],
                                          start=(j2 == 0 and pi == 0), stop=(j2 == 3 and pi == 3))
                    chains = []
                    for j in range(8):
                        chains.append(k.record())
                        jj = j % 4
                        jb = j % 2
                        ta, tb2, tc, td, btr, bti, wr, wi, hc, pbr, pbi = (ta_[jb], tb2_[jb], tc_[jb], td_[jb], btr_[jb], bti_[jb],
                                                                       wr_[jb], wi_[jb], hc_[jb], pbr_[jb], pbi_[jb])
                        Rr = tv(RR[:, d, j, 0:Lc])
                        Ri = tv(RI[:, d, j, 0:Lc])
                        pe.matmul(pbr[:, 0:Lc], lhsT=BT[:, d, 0, j, :], rhs=uTb[:, j // 4, 0:Lc], start=True, stop=True)
                        pe.matmul(pbi[:, 0:Lc], lhsT=BT[:, d, 1, j, :], rhs=uTb[:, j // 4, 0:Lc], start=True, stop=True)
                        dve.tensor_tensor(ta[:, 0:Lc], pbr[:, 0:Lc], Rr, ALU.mult)
                        dve.tensor_tensor(tb2[:, 0:Lc], pbi[:, 0:Lc], Ri, ALU.mult)
                        dve.tensor_tensor(btr[:, 0:Lc], ta[:, 0:Lc], tb2[:, 0:Lc], ALU.add)
                        dve.tensor_tensor(tc[:, 0:Lc], pbi[:, 0:Lc], Rr, ALU.mult)
                        dve.tensor_tensor(td[:, 0:Lc], pbr[:, 0:Lc], Ri, ALU.mult)
                        dve.tensor_tensor(bti[:, 0:Lc], tc[:, 0:Lc], td[:, 0:Lc], ALU.subtract)
                        rm = e[:, d, 3, j:j + 1].broadcast_to([128, Lc])
                        dve.tensor_tensor_scan(tv(wr[:, 0:Lc]), rm, tv(btr[:, 0:Lc]), h0[:, 0, j:j + 1], ALU.mult, ALU.add)
                        dve.tensor_tensor_scan(tv(wi[:, 0:Lc]), rm, tv(bti[:, 0:Lc]), h0[:, 1, j:j + 1], ALU.mult, ALU.add)
                        tl = Lc - 1 if d == 0 else 0
                        rl = RR[:, d, j, Lc - 1:Lc]
                        il = RI[:, d, j, Lc - 1:Lc]
                        dve.tensor_scalar(hc[:, 0:1], wi[:, tl:tl + 1], il, None, ALU.mult)
                        dve.tensor_scalar(hc[:, 1:2], wr[:, tl:tl + 1], il, None, ALU.mult)
                        dve.scalar_tensor_tensor(h0[:, 0, j:j + 1], wr[:, tl:tl + 1], rl, hc[:, 0:1], ALU.mult, ALU.subtract)
                        dve.scalar_tensor_tensor(h0[:, 1, j:j + 1], wi[:, tl:tl + 1], rl, hc[:, 1:2], ALU.mult, ALU.add)
                        if need_y:
                            pp = PR[(j // 4) % 2]
                            pool.tensor_tensor(pp[:, jj, 0, 0:Lc], wr[:, 0:Lc], Rr, ALU.mult)
                            dve.scalar_tensor_tensor(pp[:, jj, 1, 0:Lc], wi[:, 0:Lc], -1.0, Ri, ALU.mult, ALU.mult)
                            dve.tensor_tensor(pp[:, jj, 2, 0:Lc], wi[:, 0:Lc], Rr, ALU.mult)
                            dve.tensor_tensor(pp[:, jj, 3, 0:Lc], wr[:, 0:Lc], Ri, ALU.mult)
                        k.stop()
                        if j % 2 == 1:
                            k.replay(chains)
                            chains = []
                            if need_y and jj == 3:
                                ymm(j // 4)
                    if need_y:
                        if d == 0:
                            for c in range(2):
                                act.copy(ysum[:, c, 0:Lc], py[c][:, 0:Lc])
                            for c in range(2):
                                k.dma(out=YF[c * 128:(c + 1) * 128, t0:t0 + Lc], in_=ysum[:, c, 0:Lc], wkey=(ci, c))
                        else:
                            for c in range(2):
                                dve.tensor_tensor(ysum[:, c, 0:Lc], py[c][:, 0:Lc], yfl[n % 2][:, c, 0:Lc], ALU.add)
                                dve.scalar_tensor_tensor(ysum[:, c, 0:Lc], uf[:, c, 0:Lc], dT[:, c:c + 1], ysum[:, c, 0:Lc], ALU.mult, ALU.add)
                                act.activation(ygf[:, c, 0:Lc], ysum[:, c, 0:Lc], AF.Gelu_apprx_tanh)
                                act.copy(ygb[:, c, 0:Lc], ygf[:, c, 0:Lc])
                            for c2 in range(2):
                                for c in range(2):
                                    pe.matmul(pg[:, 0:Lc], lhsT=wgl[:, c, c2 * 128:(c2 + 1) * 128], rhs=ygb[:, c, 0:Lc],
                                              start=(c == 0), stop=(c == 1))
                                act.activation(sg[:, 0:Lc], pg[:, 0:Lc], AF.Sigmoid, bias=bgl[:, c2:c2 + 1])
                                dve.tensor_tensor(ocb[:, c2, 0:Lc], ygf[:, c2, 0:Lc], sg[:, 0:Lc], ALU.mult)
                            for c in range(2):
                                k.dma(out=MIXT[512 + c * 128:512 + (c + 1) * 128, t0:t0 + Lc], in_=ocb[:, c, 0:Lc], wkey=("c", ci, c))
                k.barrier()

    def phase4(l, ctxout, last=False):
        OWNM = (H + 512) if (last and split) else nx
        scale = 96.0 ** -0.5
        nkc = ntile
        with ExitStack() as ph:
            KT = sbuf(ph, "KT", [128, 4, NT], BF16)
            VAs = sbuf(ph, "VAs", [128, nkc, 264], BF16)
            QTb = [sbuf(ph, "QTb%d" % i, [128, 512], BF16) for i in range(2)]
            Pe = [sbuf(ph, "Pe%d" % i, [128, 1024], BF16) for i in range(2)]
            negb = sbuf(ph, "negb", [128, 1], F32)
            m2 = sbuf(ph, "m2", [128, 2], F32)
            m2t = sbuf(ph, "m2t", [2, 2], F32)
            rrow = sbuf(ph, "rrow", [128, 512], F32)
            bcs = sbuf(ph, "bcs", [64, 512], F32)
            odb = [sbuf(ph, "odb%d" % i, [64, 512], BF16) for i in range(2)]
            sp_ = [psum(ph, "sps%d" % i, [128, 1024], F32) for i in range(2)]
            acc = [psum(ph, "acc%d" % i, [128, 512], F32) for i in range(2)]
            pbc = psum(ph, "pbc", [128, 512], F32)
            for h in range(4):
                k.dma(out=KT[:, h, :], in_=KTD[h, :, :])
            k.dma(out=VAs[:], in_=VAD[:, :].rearrange("(kc p) c -> p kc c", p=128))
            dve.tensor_reduce(m2[:, 0:1], QMAX[:], AX.X, ALU.max)
            dve.tensor_reduce(m2[:, 1:2], KMAX[:], AX.X, ALU.max)
            pe.transpose(pbc[0:2, 0:128], m2[:], identf[:])
            dve.tensor_reduce(m2t[:, 0:1], pbc[0:2, 0:128], AX.X, ALU.max)
            act.activation(m2t[:, 1:2], m2t[:, 0:1], AF.Ln)
            pe.matmul(pbc[:, 256:257], lhsT=onesf[0:2, :], rhs=m2t[:, 1:2], start=True, stop=True)
            act.activation(negb[:], pbc[:, 256:257], AF.Exp, scale=0.5)
            dve.tensor_scalar(negb[:], negb[:], -scale, None, ALU.mult)
            jobs = []
            for h in range(4):
                if ctxout:
                    jobs.append((h, 0, NCTX, [0, 1]))
                for qb in range(min(nx, OWNM) // 512):
                    jobs.append((h, NCTX + qb * 512, 512, list(range(nkc))))

            def load(n):
                h, q0, qw, _ = jobs[n]
                k.dma(out=QTb[n % 2][:, 0:qw], in_=QTD[h, :, q0:q0 + qw])
            load(0)
            for n, (h, q0, qw, kcs) in enumerate(jobs):
                if n + 1 < len(jobs):
                    load(n + 1)
                Q = QTb[n % 2]
                ac = acc[n % 2]

                def scores(pi):
                    s__ = sp_[(pi // 2) % 2]
                    for u in range(2):
                        kc = kcs[pi + u]
                        pe.matmul(s__[:, u * 512:u * 512 + qw], lhsT=KT[:, h, kc * 128:(kc + 1) * 128], rhs=Q[:, 0:qw], start=True, stop=True)
                scores(0)
                for pi in range(0, len(kcs), 2):
                    s_ = sp_[(pi // 2) % 2]
                    P_ = Pe[(pi // 2) % 2]
                    if pi + 2 < len(kcs):
                        scores(pi + 2)
                    if qw == 512:
                        act.activation(P_[:], s_[:], AF.Exp, bias=negb[:, 0:1], scale=scale)
                    else:
                        for u in range(2):
                            act.activation(P_[:, u * 512:u * 512 + qw], s_[:, u * 512:u * 512 + qw], AF.Exp, bias=negb[:, 0:1], scale=scale)
                    for u in range(2):
                        kc = kcs[pi + u]
                        pe.matmul(ac[0:65, 0:qw], lhsT=VAs[:, kc, h * 66:h * 66 + 65], rhs=P_[:, u * 512:u * 512 + qw],
                                  start=(pi == 0 and u == 0), stop=(pi + u == len(kcs) - 1))
                dve.reciprocal(rrow[64:65, 0:qw], ac[64:65, 0:qw])
                pe.matmul(pbc[0:64, 0:qw], lhsT=onesf[64:65, 0:64], rhs=rrow[64:65, 0:qw], start=True, stop=True)
                act.copy(bcs[:, 0:qw], pbc[0:64, 0:qw])
                dve.tensor_tensor(odb[n % 2][:, 0:qw], ac[0:64, 0:qw], bcs[:, 0:qw], ALU.mult)
                k.dma(out=MIXT[768 + 64 * h:768 + 64 * (h + 1), q0:q0 + qw], in_=odb[n % 2][:, 0:qw], wkey=("d", n))
        k.barrier()

    def epilogue(po, x_, gb, xo, st4, junk, tmpf):
        act.activation(junk[:], po[:], AF.Square, accum_out=st4[:, 0:1])
        rstd_from_ssq(st4[:, 0:1], st4[:, 2:3], D, st4[:, 1:2])
        dve.scalar_tensor_tensor(tmpf[:], po[:], st4[:, 2:3], gb[:], ALU.mult, ALU.mult)
        dve.tensor_tensor(xo[:], tmpf[:], x_[:], ALU.add)

    def load_cast_rows(dst, src_rows, nk, ncols, stg):
        for kc in range(nk):
            s_ = stg[kc % 2]
            k.dma(out=s_[:, 0:ncols], in_=src_rows[kc * 128:(kc + 1) * 128, :])
            (pool if kc % 2 else dve).tensor_copy(dst[:, kc, :], s_[:, 0:ncols])

    def phase5(l, ctxout, last=False):
        OWN5 = (H + 128) if (last and split) else nx
        with ExitStack() as ph:
            wo = sbuf(ph, "wo", [128, 8, D], BF16)
            stg = [sbuf(ph, "stg%d" % i, [128, D], F32) for i in range(2)]
            load_cast_rows(wo, I["w_out"][l], 8, D, stg)
            mx = [sbuf(ph, "mx%d" % i, [128, 8, 128], BF16) for i in range(2)]
            xt = [sbuf(ph, "xt%d" % i, [128, D], F32) for i in range(2)]
            xo = [sbuf(ph, "xo%d" % i, [128, D], F32) for i in range(2)]
            junk = sbuf(ph, "junk", [128, D], BF16)
            tmpf = sbuf(ph, "tmpf", [128, D], F32)
            st4 = sbuf(ph, "st4", [128, 4], F32)
            po = [psum(ph, "po%d" % i, [128, 1024], F32) for i in range(2)]
            tiles = list(range(0 if ctxout else 2, min(ntile, 2 + OWN5 // 128)))

            def load(n):
                i = tiles[n]
                k.dma(out=mx[n % 2][:], in_=MIXT[:, i * 128:(i + 1) * 128].rearrange("(c p) t -> p c t", p=128))
                k.dma(out=xt[n % 2][:], in_=XS[i * 128:(i + 1) * 128, :], rkey=i)
            load(0)
            for n, i in enumerate(tiles):
                if n + 1 < len(tiles):
                    load(n + 1)
                p_ = po[n % 2]
                for nn in range(2):
                    for kc in range(8):
                        pe.matmul(p_[:, nn * 512:(nn + 1) * 512], lhsT=mx[n % 2][:, kc, :], rhs=wo[:, kc, nn * 512:(nn + 1) * 512],
                                  start=(kc == 0), stop=(kc == 7))
                epilogue(p_, xt[n % 2], GB[0][1 if i < 2 else 0], xo[n % 2], st4, junk, tmpf)
                k.dma(out=XS[i * 128:(i + 1) * 128, :], in_=xo[n % 2][:], wkey=i)
        k.barrier()

    def phase6(l, ctxout, last):
        OWN = 510
        with ExitStack() as ph:
            wdn = sbuf(ph, "wdn", [128, 22, D], BF16)
            cw = sbuf(ph, "cw", [128, 44, 3], F32)
            cb = sbuf(ph, "cb", [128, 44], F32)
            k.dma(out=cw[:], in_=I["convwT"][l, :, :, :])
            k.dma(out=cb[:], in_=I["convbT"][l, :, :])
            with ExitStack() as pp_:
                stg = [sbuf(pp_, "stg%d" % i, [128, DFF], F32) for i in range(2)]
                cbf = [sbuf(pp_, "cbf%d" % i, [128, DFF], BF16) for i in range(2)]
                load_cast_rows(wdn, I["w_dn"][l], 22, D, stg)
                n = 0
                for kc in range(8):
                    for half in range(2):
                        s_, c_ = stg[n % 2], cbf[n % 2]
                        k.dma(out=s_[:], in_=I["w_up"][l, kc * 128:(kc + 1) * 128, half * DFF:(half + 1) * DFF])
                        (pool if n % 2 else dve).tensor_copy(c_[:], s_[:])
                        for i_ in range(22):
                            k.dma(out=WUPS[i_, :, kc * 256 + half * 128:kc * 256 + (half + 1) * 128], in_=c_[:, i_ * 128:(i_ + 1) * 128],
                                  wkey=(kc, half, i_))
                        n += 1
                k.barrier()
            NW = 6
            wst = [sbuf(ph, "wst%d" % i, [128, 8, 256], BF16) for i in range(NW)]
            xb = [sbuf(ph, "xb%d" % i, [128, 4, D], F32) for i in range(2)]
            xsb = sbuf(ph, "xsb", [128, D], BF16)
            junk = sbuf(ph, "junk", [128, D], BF16)
            hxT = sbuf(ph, "hxT", [128, 8, 512], BF16)
            hid = sbuf(ph, "hid", [128, 22, 512], BF16)
            ca2 = [sbuf(ph, "ca%d" % i, [128, 512], F32) for i in range(2)]
            cg2 = [sbuf(ph, "cg%d" % i, [128, 512], F32) for i in range(2)]
            ga2 = [sbuf(ph, "ga%d" % i, [128, 512], F32) for i in range(2)]
            st4 = sbuf(ph, "st4", [128, 4], F32)
            tmpf = sbuf(ph, "tmpf", [128, D], F32)
            xo = [sbuf(ph, "xo%d" % i, [128, D], F32) for i in range(2)]
            tpb = psum(ph, "tpb", [128, 1024], BF16)
            pz = [psum(ph, "pz%d" % i, [128, 512], F32) for i in range(4)]
            po = psum(ph, "po", [128, 1024], F32)
            pool.memset(hid[:], 0.0)
            xown = H if (last and split) else nx
            xval = min(nx, xown + 1)
            segs = ([(0, NCTX, NCTX, 1)] if ctxout else []) + [(NCTX, xown, xval, 0)]
            blocks = []
            for (s0, so, sl, w) in segs:
                for b0 in range(0, so, OWN):
                    blocks.append((s0, sl, w, b0, min(OWN, so - b0)))
            nwl = [0]
            total_w = len(blocks) * 22

            def loadw():
                if nwl[0] < total_w:
                    i_ = nwl[0] % 22
                    k.dma(out=wst[nwl[0] % NW][:].rearrange("p k c -> p (k c)"), in_=WUPS[i_, :, :])
                    nwl[0] += 1

            def load(n):
                s0, sl, w, b0, own = blocks[n]
                X = xb[n % 2]
                lo, hi = b0 - 1, b0 - 1 + 512
                vlo, vhi = max(lo, 0), min(hi, sl)
                for s in range(4):
                    a, b_ = lo + 128 * s, lo + 128 * (s + 1)
                    va, vb_ = max(a, vlo), min(b_, vhi)
                    if va >= vb_:
                        pool.memset(X[:, s, :], 0.0)
                        continue
                    if va > a or vb_ < b_:
                        pool.memset(X[:, s, :], 0.0)
                    k.dma(out=X[va - a:vb_ - a, s, :], in_=XS[s0 + va:s0 + vb_, :])
            load(0)
            for _ in range(3):
                loadw()
            nuse = 0
            for n, (s0, sl, w, b0, own) in enumerate(blocks):
                if n + 1 < len(blocks):
                    load(n + 1)
                X = xb[n % 2]
                lo = b0 - 1
                vlo, vhi = max(lo, 0), min(lo + 512, sl)
                for s in range(4):
                    act.activation(junk[:], X[:, s, :], AF.Square, accum_out=st4[:, 0:1])
                    rstd_from_ssq(st4[:, 0:1], st4[:, 2:3], D, st4[:, 1:2])
                    dve.tensor_scalar(xsb[:], X[:, s, :], st4[:, 2:3], None, ALU.mult)
                    for c in range(8):
                        pe.transpose(tpb[:, c * 128:(c + 1) * 128], xsb[:, c * 128:(c + 1) * 128], identb[:])
                    for c in range(8):
                        if c % 2 == 0:
                            dve.tensor_scalar(hxT[:, c, s * 128:(s + 1) * 128], tpb[:, c * 128:(c + 1) * 128], GS[:, 1, c, w:w + 1],
                                              SH[:, 1, c, w:w + 1], ALU.mult, ALU.add)
                        else:
                            act.activation(hxT[:, c, s * 128:(s + 1) * 128], tpb[:, c * 128:(c + 1) * 128], AF.Identity,
                                           bias=SH[:, 1, c, w:w + 1], scale=GS[:, 1, c, w:w + 1])
                if vlo - lo > 0:
                    pool.memset(hxT[:, :, 0:vlo - lo], 0.0)
                if vhi - lo < 512:
                    pool.memset(hxT[:, :, vhi - lo:512], 0.0)
                chains = []
                for i in range(22):
                    ca, cg, ga = ca2[i % 2], cg2[i % 2], ga2[i % 2]
                    chains.append(k.record())
                    loadw()
                    wt = wst[nuse % NW]
                    nuse += 1
                    pa, pg_ = pz[(2 * i) % 4], pz[(2 * i + 1) % 4]
                    for (pp, hf) in ((pa, 0), (pg_, 1)):
                        for kc in range(8):
                            pe.matmul(pp[:], lhsT=wt[:, kc, hf * 128:(hf + 1) * 128], rhs=hxT[:, kc, :], start=(kc == 0), stop=(kc == 7))
                    for (pp, f, cc_) in ((pa, i, ca), (pg_, 22 + i, cg)):
                        act.activation(cc_[:, 1:511], pp[:, 1:511], AF.Identity, bias=cb[:, f:f + 1], scale=cw[:, f, 1:2])
                        dve.scalar_tensor_tensor(cc_[:, 1:511], pp[:, 0:510], cw[:, f, 0:1], cc_[:, 1:511], ALU.mult, ALU.add)
                        dve.scalar_tensor_tensor(cc_[:, 1:511], pp[:, 2:512], cw[:, f, 2:3], cc_[:, 1:511], ALU.mult, ALU.add)
                    act.activation(ga[:, 1:511], ca[:, 1:511], AF.Gelu_apprx_tanh)
                    dve.tensor_tensor(hid[:, i, 1:511], ga[:, 1:511], cg[:, 1:511], ALU.mult)
                    k.stop()
                    if i % 2 == 1:
                        k.replay(chains)
                        chains = []
                for s in range(4):
                    ca_, cb_ = max(1, 128 * s), min(own + 1, 128 * (s + 1))
                    if ca_ >= cb_:
                        continue
                    for nn in range(2):
                        for i in range(22):
                            pe.matmul(po[:, nn * 512:(nn + 1) * 512], lhsT=hid[:, i, s * 128:(s + 1) * 128], rhs=wdn[:, i, nn * 512:(nn + 1) * 512],
                                      start=(i == 0), stop=(i == 21))
                    xo_ = xo[s % 2]
                    epilogue(po, X[:, s, :], GB[1][w], xo_, st4, junk, tmpf)
                    r0 = lo + ca_
                    p0, p1 = ca_ - 128 * s, cb_ - 128 * s
                    if last and w == 0:
                        k.dma(out=yout[r0:r0 + (p1 - p0), :], in_=xo_[p0:p1, :], wkey=("y", n, s))
                    else:
                        k.dma(out=XS[s0 + r0:s0 + r0 + (p1 - p0), :], in_=xo_[p0:p1, :])
        k.barrier()

    phases = []
    for l in range(depth):
        ctxout = l < depth - 1
        last = l == depth - 1
        phases += [("p0", lambda l=l: phase0(l)), ("p1", lambda l=l, c=ctxout, la=last: phase1(l, c, la)),
                   ("p2", lambda l=l, c=ctxout, la=last: phase2(l, c, la)), ("p3", lambda l=l, c=ctxout, la=last: phase3(l, c, la)),
                   ("p4", lambda l=l, c=ctxout, la=last: phase4(l, c, la)), ("p5", lambda l=l, c=ctxout, la=last: phase5(l, c, la)),
                   ("p6", lambda l=l, c=ctxout, la=last: phase6(l, c, la))]
    for n, (nm, fn) in enumerate(phases):
        fn()
        if stop_after is not None and n + 1 >= stop_after:
            break
    k.finish()
    top.close()
    return nc, k


_CACHE = {}


def _shapes(m):
    return {k_: (v.shape, "bf16" if v.dtype == ml_dtypes.bfloat16 else "f32") for k_, v in m.items()}


def run(inputs, nx, batches, stop_after=None, dbg=False, depth=DEPTH):
    inputs = {k_: np.asarray(v) for k_, v in inputs.items()}
    maps = [_layout_inputs(inputs, b, nx, fl) for (b, fl) in batches]
    key = (nx, stop_after, dbg, depth)
    if key not in _CACHE:
        _CACHE[key] = build(nx, _shapes(maps[0]), depth=depth, stop_after=stop_after, dbg=dbg)
    nc, kb = _CACHE[key]
    res = run_bass_kernel_spmd(nc, maps, core_ids=list(range(len(maps))))
    return res


def kernel(**inputs):
    nx = inputs["x"].shape[1]
    nb = inputs["x"].shape[0]
    batches = [(i // 2, bool(i % 2)) for i in range(2 * nb)]
    res = run(inputs, nx, batches)
    out = np.empty((nb, nx, D), np.float32)
    h = nx // 2
    for b in range(nb):
        out[b, :h] = np.asarray(res.results[2 * b]["y"], dtype=np.float32)
        out[b, h:] = np.asarray(res.results[2 * b + 1]["y"], dtype=np.float32)[::-1]
    return out
```

```python
import math, os
CUT = int(os.environ.get('K_CUT', '99'))
ASUB = int(os.environ.get('K_ASUB', '99'))
DSUB = int(os.environ.get('K_DSUB', '99'))
DX = int(os.environ.get('K_DX', '99'))
from contextlib import ExitStack
import numpy as np
import ml_dtypes
import concourse.bass as bass
import concourse.mybir as mybir
from concourse.bass_utils import run_bass_kernel_spmd

F32 = mybir.dt.float32
BF16 = mybir.dt.bfloat16
AF = mybir.ActivationFunctionType
ALU = mybir.AluOpType
AX = mybir.AxisListType
AP = bass.AP

D = 1024
NCTX = 256
DEPTH = 2
DFF = 2816
EPS = 1e-6


class _Buf:
    __slots__ = ("w", "r", "ep")

    def __init__(self):
        self.w = None
        self.r = {}
        self.ep = -1


class _Eng:
    def __init__(self, k, name, e, sem):
        self.k, self.name, self.e, self.sem = k, name, e, sem
        self.cnt = 0
        self.seen = {}
        self.prog = []

    def __getattr__(self, op):
        f = getattr(self.e, op)

        def call(*a, **kw):
            if self.k.rec is not None:
                self.k.rec.append((self, f, a, kw, False, None, None))
                return None
            return self.k._emit(self, f, a, kw)
        return call


class KB:
    NDMA = 40

    def __init__(self, nc, stack):
        self.nc = nc
        mk = lambda n: stack.enter_context(nc.semaphore(n))
        self.pe = _Eng(self, "pe", nc.tensor, mk("s_pe"))
        self.dve = _Eng(self, "dve", nc.vector, mk("s_dve"))
        self.act = _Eng(self, "act", nc.scalar, mk("s_act"))
        self.pool = _Eng(self, "pool", nc.gpsimd, mk("s_pool"))
        self.sp = _Eng(self, "sp", nc.sync, mk("s_sp"))
        self.engs = [self.pe, self.dve, self.act, self.pool, self.sp]
        self.dsem = [mk("s_d%d" % i) for i in range(self.NDMA)]
        self.dval = [0] * self.NDMA
        self.dnext = 0
        self.bufs = {}
        self.epoch = 0
        self.ninst = 0
        self.rec = None

    def record(self):
        self.rec = []
        return self.rec

    def stop(self):
        self.rec = None

    def replay(self, lists):
        items = []
        for li, lst in enumerate(lists):
            n = len(lst)
            for j, it in enumerate(lst):
                items.append(((j + 0.5) / n, li, j, it))
        items.sort(key=lambda z: (z[0], z[1], z[2]))
        for _, _, _, (eng, f, a, kw, dma, wkey, rkey) in items:
            self._emit(eng, f, a, kw, dma=dma, wkey=wkey, rkey=rkey)

    def _buf(self, key):
        b = self.bufs.get(key)
        if b is None:
            b = self.bufs[key] = _Buf()
        if b.ep != self.epoch:
            b.w, b.r, b.ep = None, {}, self.epoch
        return b

    def _need(self, eng, ev, waits):
        sem, val, owner = ev
        if owner is eng and eng is self.pe:
            return
        if eng.seen.get(id(sem), 0) >= val:
            return
        eng.seen[id(sem)] = val
        waits.append((sem, val))

    def _emit(self, eng, f, a, kw, dma=False, wkey=None, rkey=None):
        wk, rk = [], []
        for i, x in enumerate(a):
            if isinstance(x, AP):
                (wk if i == 0 else rk).append(x)
        for n, x in kw.items():
            if isinstance(x, AP):
                (wk if n in ("out", "accum_out") else rk).append(x)

        def key(x, dk):
            if dk is not None and str(x.space) == "DRAM":
                return (x.name, dk)
            return x.name
        wb = [self._buf(key(x, wkey)) for x in wk]
        rb = [self._buf(key(x, rkey)) for x in rk]
        waits = []
        for b in rb:
            if b.w is not None:
                self._need(eng, b.w, waits)
        for b in wb:
            if b.w is not None:
                self._need(eng, b.w, waits)
            for ev in b.r.values():
                self._need(eng, ev, waits)
        if dma:
            i = self.dnext
            self.dnext = (i + 1) % self.NDMA
            if self.dval[i] > 0:
                self._need(eng, (self.dsem[i], self.dval[i], None), waits)
            self.dval[i] += 16
            ev = (self.dsem[i], self.dval[i], None)
            inc = (self.dsem[i], 16)
        else:
            eng.cnt += 1
            ev = (eng.sem, eng.cnt, eng)
            inc = (eng.sem, 1)
        eng.prog.append((waits, f, a, kw, inc))
        self.ninst += 1
        for b in wb:
            b.w = ev
            b.r = {}
        for b in rb:
            if b.w is not ev:
                b.r[id(ev[0])] = ev
        return ev

    def dma(self, out, in_, wkey=None, rkey=None):
        q = self.sp
        if self.rec is not None:
            self.rec.append((q, q.e.dma_start, (), {"out": out, "in_": in_}, True, wkey, rkey))
            return None
        return self._emit(q, q.e.dma_start, (), {"out": out, "in_": in_}, dma=True, wkey=wkey, rkey=rkey)

    def barrier(self):
        for e in self.engs:
            waits = []
            for o in self.engs:
                if o.cnt > 0 and not (o is e and e is self.pe):
                    self._need(e, (o.sem, o.cnt, o), waits)
            for i in range(self.NDMA):
                if self.dval[i] > 0:
                    self._need(e, (self.dsem[i], self.dval[i], None), waits)
            if waits:
                e.prog.append((waits, None, (), {}, None))
        self.epoch += 1

    def finish(self):
        self.barrier()
        with self.nc.Block() as block:
            for e, deco in ((self.sp, block.sync), (self.pe, block.tensor), (self.dve, block.vector),
                            (self.act, block.scalar), (self.pool, block.gpsimd)):
                def body(_x, e=e):
                    for waits, f, a, kw, inc in e.prog:
                        for sem, val in waits:
                            e.e.wait_ge(sem, val)
                        if f is not None:
                            f(*a, **kw).then_inc(inc[0], inc[1])
                deco(body)


def _bf(a):
    return np.ascontiguousarray(a).astype(ml_dtypes.bfloat16)


def _rep(row, n=128):
    return np.ascontiguousarray(np.broadcast_to(np.asarray(row, np.float32).reshape(1, -1), (n, row.size)))


def _consts(nx):
    c = {}
    c["identb"] = _bf(np.eye(128, dtype=np.float32))
    c["identf"] = np.eye(128, dtype=np.float32)
    r = np.arange(128)[:, None]
    t = np.arange(128)[None, :]
    s = np.float32(-1.0 / 16.0)
    tri = np.stack([(r <= t), (r > t), (r >= t), (r < t)]).astype(np.float32) * s
    c["tri"] = np.ascontiguousarray(tri.transpose(1, 0, 2))
    mf = (r <= t).astype(np.float32)
    mb = (r >= t).astype(np.float32)
    c["maskf"] = np.ascontiguousarray(np.tile(mf, (1, 4)))
    c["maskb"] = np.ascontiguousarray(np.tile(mb, (1, 4)))
    pos = np.arange(nx)
    row = (pos // 64).astype(np.float32)
    col = (pos % 64).astype(np.float32)
    inv = (10000.0 ** (-np.arange(8, dtype=np.float32) / 8)).astype(np.float32)
    ar = row[:, None] * inv[None, :]
    ac = col[:, None] * inv[None, :]
    cosf = np.concatenate([np.cos(ar), np.cos(ar), np.cos(ac), np.cos(ac)], axis=1).astype(np.float32)
    sinf = np.concatenate([-np.sin(ar), np.sin(ar), -np.sin(ac), np.sin(ac)], axis=1).astype(np.float32)
    c["ropec"] = np.ascontiguousarray(np.tile(cosf, (1, 4)))
    c["ropes"] = np.ascontiguousarray(np.tile(sinf, (1, 4)))
    return c


def _layout_inputs(inp, b, nx, flip=False):
    L = DEPTH
    m = {}
    if flip:
        inp = dict(inp)
        inp["x"] = inp["x"][:, ::-1]
        inp["ctx"] = inp["ctx"][:, ::-1]
        wi = inp["w_in"].copy()
        wi[:, :, 1536:1552] = inp["w_in"][:, :, 1552:1568]
        wi[:, :, 1552:1568] = inp["w_in"][:, :, 1536:1552]
        inp["w_in"] = wi
        inp["gla_w_gate"] = inp["gla_w_gate"][:, ::-1]
        inp["gla_b_gate"] = inp["gla_b_gate"][:, ::-1]
        inp["sgu_w"] = inp["sgu_w"][:, :, ::-1, ::-1]
        inp["sgu_b"] = inp["sgu_b"][:, :, ::-1]
        for n_ in ("s5_a_re", "s5_a_im", "s5_log_dt", "s5_b_re", "s5_b_im", "s5_c_re", "s5_c_im"):
            inp[n_] = np.ascontiguousarray(inp[n_][:, ::-1])
        inp["ffn_conv_w"] = inp["ffn_conv_w"][:, ::-1]
    m["xin"] = np.ascontiguousarray(inp["x"][b])
    m["cin"] = np.ascontiguousarray(inp["ctx"][b])
    cT = np.stack([inp["c"][b].reshape(8, 128).T, inp["c_ctx"].reshape(8, 128).T], axis=-1)
    m["cT"] = np.ascontiguousarray(cT.astype(np.float32))
    m["w_mod"] = inp["w_mod"]
    m["bmodT"] = np.ascontiguousarray(inp["b_mod"].reshape(L, 48, 128).transpose(0, 2, 1))
    gv = np.stack([inp[n].reshape(L, 8, 128).transpose(0, 2, 1) for n in
                   ("g_pre_mix", "g_post_mix", "g_pre_ffn", "g_post_ffn")], axis=2)
    m["gvT"] = np.ascontiguousarray(gv)
    m["w_in"] = inp["w_in"]
    m["sguwT"] = np.ascontiguousarray(inp["sgu_w"].transpose(0, 3, 1, 2))
    m["sgub"] = np.ascontiguousarray(inp["sgu_b"].transpose(0, 2, 1))
    m["sgun"] = np.stack([_rep(inp["sgu_norm"][l].reshape(-1)) for l in range(L)])
    wg = np.zeros((L, 32, 512), np.float32)
    wg[:, 0:16, 0:256] = inp["gla_w_gate"][:, 0]
    wg[:, 16:32, 256:512] = inp["gla_w_gate"][:, 1]
    m["wg"] = wg
    m["bg"] = np.stack([_rep(inp["gla_b_gate"][l].reshape(-1)) for l in range(L)])
    m["glan"] = np.stack([_rep(inp["gla_norm"][l].reshape(-1)) for l in range(L)])
    def st(a):
        return np.ascontiguousarray(a.reshape(L, 2, 8, 128).transpose(0, 1, 3, 2))
    m["s5are"] = st(inp["s5_a_re"])
    m["s5aim"] = st(inp["s5_a_im"])
    m["s5ldt"] = st(np.ascontiguousarray(np.broadcast_to(inp["s5_log_dt"][..., None], (L, 2, 16, 64))))
    def stb(a):
        return np.ascontiguousarray(a.reshape(L, 2, 8, 128, 16).transpose(0, 1, 3, 2, 4))
    m["s5bre"] = stb(inp["s5_b_re"])
    m["s5bim"] = stb(inp["s5_b_im"])
    def stc(a):
        return np.ascontiguousarray(a.reshape(L, 2, 8, 2, 16, 64).transpose(0, 1, 3, 5, 2, 4).reshape(L, 2, 128, 8, 16))
    m["s5cre"] = stc(inp["s5_c_re"])
    m["s5cim"] = stc(inp["s5_c_im"])
    m["s5dT"] = np.ascontiguousarray(inp["s5_d"].reshape(L, 2, 128).transpose(0, 2, 1))
    m["s5wglu"] = inp["s5_w_glu"]
    m["s5bgluT"] = np.ascontiguousarray(inp["s5_b_glu"].reshape(L, 2, 128).transpose(0, 2, 1))
    m["qn"] = np.stack([_rep(inp["mla_q_norm"][l]) for l in range(L)])
    m["kvn"] = np.stack([_rep(inp["mla_kv_norm"][l]) for l in range(L)])
    m["wuq"] = inp["mla_w_uq"]
    m["wukv"] = inp["mla_w_ukv"]
    m["w_out"] = inp["w_out"]
    m["w_up"] = inp["ffn_w_up"]
    m["convwT"] = np.ascontiguousarray(inp["ffn_conv_w"].reshape(L, 3, 44, 128).transpose(0, 3, 2, 1))
    m["convbT"] = np.ascontiguousarray(inp["ffn_conv_b"].reshape(L, 44, 128).transpose(0, 2, 1))
    m["w_dn"] = inp["ffn_w_down"]
    m.update(_consts(nx))
    if flip:
        m["ropec"] = np.ascontiguousarray(m["ropec"][::-1])
        m["ropes"] = np.ascontiguousarray(m["ropes"][::-1])
    return {k: np.ascontiguousarray(v) for k, v in m.items()}


def build(nx, in_shapes, depth=DEPTH, stop_after=None, dbg=False, split=True):
    nc = bass.Bass("TRN2", target_bir_lowering=False)
    NT = NCTX + nx
    ntile = NT // 128
    I = {}
    for name, (shape, dt) in in_shapes.items():
        I[name] = nc.dram_tensor(name, list(shape), BF16 if dt == "bf16" else F32, kind="ExternalInput").ap()
    H = nx // 2 if split else nx
    yout = nc.dram_tensor("y", [H, D], F32, kind="ExternalOutput").ap()
    okind = "ExternalOutput" if dbg else "Internal"

    def scratch(name, shape, dt):
        return nc.dram_tensor(name, shape, dt, kind=okind).ap()
    XS = scratch("XS", [NT, D], F32)
    MIXT = scratch("MIXT", [D, NT], BF16)
    QIT = scratch("QIT", [2, ntile, 64, 512], BF16)
    KVD = scratch("KVD", [2, ntile, 64, 260], F32)
    OACC = scratch("OACC", [NT, 256], F32)
    RSD = scratch("RSD", [NT, 256], F32)
    UT = scratch("UT", [256, NT], F32)
    YF = scratch("YF", [256, NT], F32)
    QTD = scratch("QTD", [4, 128, NT], BF16)
    KTD = scratch("KTD", [4, 128, NT], BF16)
    VAD = scratch("VAD", [NT, 264], BF16)
    WUPS = scratch("WUPS", [22, 128, 2048], BF16)

    top = ExitStack()
    k = KB(nc, top)
    pe, dve, act, pool = k.pe, k.dve, k.act, k.pool
    uid = [0]

    def sbuf(st, name, shape, dt):
        uid[0] += 1
        return st.enter_context(nc.sbuf_tensor("%s_%d" % (name, uid[0]), list(shape), dt))

    def psum(st, name, shape, dt):
        uid[0] += 1
        return st.enter_context(nc.psum_tensor("%s_%d" % (name, uid[0]), list(shape), dt))

    identb = sbuf(top, "identb", [128, 128], BF16)
    identf = sbuf(top, "identf", [128, 128], F32)
    onesf = sbuf(top, "onesf", [128, 128], F32)
    epsc = sbuf(top, "epsc", [128, 1], F32)
    onec = sbuf(top, "onec", [128, 1], F32)
    n16c = sbuf(top, "n16c", [128, 1], F32)
    cT = sbuf(top, "cT", [128, 8, 2], F32)
    scT = sbuf(top, "scT", [128, 8, 2], F32)
    QMAX = sbuf(top, "QMAX", [128, 4], F32)
    KMAX = sbuf(top, "KMAX", [128, 4], F32)
    GS = sbuf(top, "GS", [128, 2, 8, 2], F32)
    SH = sbuf(top, "SH", [128, 2, 8, 2], F32)
    GTT = sbuf(top, "GTT", [128, 2, 8, 2], F32)
    GB = [[sbuf(top, "GB%d%d" % (s, w), [128, D], F32) for w in range(2)] for s in range(2)]

    k.dma(out=identb[:], in_=I["identb"][:, :])
    k.dma(out=identf[:], in_=I["identf"][:, :])
    k.dma(out=cT[:], in_=I["cT"][:, :, :])
    dve.memset(onesf[:], 1.0)
    dve.memset(epsc[:], EPS)
    dve.memset(onec[:], 1.0)
    dve.memset(n16c[:], -1.0 / 16.0)
    act.activation(scT[:], cT[:], AF.Silu)
    k.dma(out=XS[0:NCTX, :], in_=I["cin"][:, :])
    step = max(128, nx // 8)
    for r0 in range(0, nx, step):
        k.dma(out=XS[NCTX + r0:NCTX + r0 + step, :], in_=I["xin"][r0:r0 + step, :])
    k.barrier()

    def rstd_from_ssq(ssq, out, n, tmp):
        act.activation(tmp, ssq, AF.Sqrt, scale=1.0 / n, bias=epsc[0:tmp.shape[0], 0:1])
        dve.reciprocal(out, tmp)

    def phase0(l):
        with ExitStack() as ph:
            wm = [sbuf(ph, "wm%d" % i, [128, 8, 512], F32) for i in range(2)]
            bmT = sbuf(ph, "bmT", [128, 48], F32)
            gvT = sbuf(ph, "gvT", [128, 4, 8], F32)
            modT = sbuf(ph, "modT", [128, 48, 2], F32)
            tmp1 = sbuf(ph, "tmp1", [128, 8, 2], F32)
            dg = [sbuf(ph, "dg%d" % i, [128, 128], F32) for i in range(2)]
            pm = psum(ph, "pm", [128, 512], F32)
            pbk = [psum(ph, "pbk%d" % i, [128, 512], F32) for i in range(2)]
            k.dma(out=bmT[:], in_=I["bmodT"][l, :, :])
            k.dma(out=gvT[:], in_=I["gvT"][l, :, :, :])
            for blk in range(12):
                w = wm[blk % 2]
                k.dma(out=w[:], in_=I["w_mod"][l, :, blk * 512:(blk + 1) * 512].rearrange("(kc p) n -> p kc n", p=128))
                for j4 in range(4):
                    j = blk * 4 + j4
                    for kc in range(8):
                        pe.matmul(pm[:, 2 * j:2 * j + 2], lhsT=w[:, kc, j4 * 128:(j4 + 1) * 128], rhs=scT[:, kc, :],
                                  start=(kc == 0), stop=(kc == 7))
            dve.tensor_tensor(modT[:], pm[:, 0:96].rearrange("p (j w) -> p j w", w=2),
                              bmT[:].unsqueeze(2).broadcast_to([128, 48, 2]), ALU.add)
            for s in range(2):
                o = 24 * s
                dve.tensor_copy(SH[:, s], modT[:, o:o + 8, :])
                dve.tensor_scalar(tmp1[:], modT[:, o + 8:o + 16, :], 1.0, None, ALU.add)
                dve.tensor_tensor(GS[:, s], tmp1[:], gvT[:, 2 * s, :].unsqueeze(2).broadcast_to([128, 8, 2]), ALU.mult)
                dve.tensor_tensor(GTT[:, s], modT[:, o + 16:o + 24, :],
                                  gvT[:, 2 * s + 1, :].unsqueeze(2).broadcast_to([128, 8, 2]), ALU.mult)
            n = 0
            for s in range(2):
                for w in range(2):
                    for c in range(8):
                        d_ = dg[n % 2]
                        pb = pbk[(c // 4) % 2]
                        dve.tensor_scalar(d_[:], identf[:], GTT[:, s, c, w:w + 1], None, ALU.mult)
                        pe.matmul(pb[:, (c % 4) * 128:(c % 4 + 1) * 128], lhsT=onesf[:], rhs=d_[:], start=True, stop=True)
                        if c % 4 == 3:
                            act.copy(GB[s][w][:, (c // 4) * 512:(c // 4 + 1) * 512], pb[:])
                        n += 1
        k.barrier()

    def phase1(l, ctxout, last=False):
        OWNM = (H + 512) if (last and split) else nx
        with ExitStack() as ph:
            win = sbuf(ph, "win", [128, 8, 2176], BF16)
            stg = [sbuf(ph, "stg%d" % i, [128, 2176], F32) for i in range(2)]
            for kc in range(8):
                k.dma(out=stg[kc % 2][:], in_=I["w_in"][l, kc * 128:(kc + 1) * 128, :])
                (pool if kc % 2 else dve).tensor_copy(win[:, kc, :], stg[kc % 2][:])
            sgw_f = sbuf(ph, "sgw_f", [128, 4, 128], F32)
            sgw = sbuf(ph, "sgw", [128, 4, 128], BF16)
            sgb = sbuf(ph, "sgb", [128, 4], F32)
            sgn = sbuf(ph, "sgn", [128, 256], F32)
            wg_f = sbuf(ph, "wg_f", [32, 512], F32)
            wgb = sbuf(ph, "wgb", [32, 512], BF16)
            bg = sbuf(ph, "bg", [128, 512], F32)
            tri = sbuf(ph, "tri", [128, 4, 128], F32)
            maskf = sbuf(ph, "maskf", [128, 512], F32)
            maskb = sbuf(ph, "maskb", [128, 512], F32)
            qn = sbuf(ph, "qn", [128, 224], F32)
            kvn = sbuf(ph, "kvn", [128, 96], F32)
            wuq_f = sbuf(ph, "wuq_f", [128, 2, 384], F32)
            wuq = sbuf(ph, "wuq", [128, 2, 384], BF16)
            wukv_f = sbuf(ph, "wukv_f", [128, 512], F32)
            wukv = sbuf(ph, "wukv", [128, 512], BF16)
            k.dma(out=sgw_f[:], in_=I["sguwT"][l, :, :, :])
            k.dma(out=sgb[:], in_=I["sgub"][l, :, :])
            k.dma(out=sgn[:], in_=I["sgun"][l, :, :])
            k.dma(out=wg_f[:], in_=I["wg"][l, :, :])
            k.dma(out=bg[:], in_=I["bg"][l, :, :])
            k.dma(out=tri[:], in_=I["tri"][:, :, :])
            k.dma(out=maskf[:], in_=I["maskf"][:, :])
            k.dma(out=maskb[:], in_=I["maskb"][:, :])
            k.dma(out=qn[:], in_=I["qn"][l, :, :])
            k.dma(out=kvn[:], in_=I["kvn"][l, :, :])
            dve.memset(wuq_f[:], 0.0)
            dve.memset(wukv_f[:], 0.0)
            k.dma(out=wuq_f[:, 0, :], in_=I["wuq"][l, 0:128, :])
            k.dma(out=wuq_f[0:96, 1, :], in_=I["wuq"][l, 128:224, :])
            k.dma(out=wukv_f[0:96, :], in_=I["wukv"][l, :, :])
            dve.tensor_copy(sgw[:], sgw_f[:])
            dve.tensor_copy(wgb[:], wg_f[:])
            dve.tensor_copy(wuq[:], wuq_f[:])
            dve.tensor_copy(wukv[:], wukv_f[:])
            dve.memset(QMAX[:], 0.0)
            dve.memset(KMAX[:], 0.0)

            xt = [sbuf(ph, "xt%d" % i, [128, D], F32) for i in range(2)]
            rc = [sbuf(ph, "rc%d" % i, [128, 128], F32) for i in range(3)]
            rs_ = [sbuf(ph, "rs%d" % i, [128, 128], F32) for i in range(3)]
            junk = sbuf(ph, "junk", [128, D], BF16)
            st4 = sbuf(ph, "st4", [128, 16], F32)
            st4y = sbuf(ph, "st4y", [128, 16], F32)
            junky = sbuf(ph, "junky", [128, 512], BF16)
            xsb = sbuf(ph, "xsb", [128, D], BF16)
            hxT = sbuf(ph, "hxT", [128, 8, 128], BF16)
            P2 = [sbuf(ph, "P%d" % i, [128, 2176], F32) for i in range(2)]
            GA = sbuf(ph, "GA", [128, 512], F32)
            t256a = sbuf(ph, "t256a", [128, 256], F32)
            t256b = sbuf(ph, "t256b", [128, 256], F32)
            vnb = sbuf(ph, "vnb", [128, 256], BF16)
            oab = sbuf(ph, "oab", [128, 256], BF16)
            mxT = sbuf(ph, "mxT", [128, 2, 128], BF16)
            glb = sbuf(ph, "glb", [128, 32], BF16)
            glT = sbuf(ph, "glT", [32, 128], BF16)
            zb = sbuf(ph, "zb", [128, 512], F32)
            Lg = sbuf(ph, "Lg", [128, 512], F32)
            E1 = sbuf(ph, "E1", [128, 256], F32)
            E2 = sbuf(ph, "E2", [128, 256], F32)
            E3 = sbuf(ph, "E3", [128, 256], F32)
            qin = sbuf(ph, "qin", [128, 256], BF16)
            kin = sbuf(ph, "kin", [128, 256], BF16)
            kend = sbuf(ph, "kend", [128, 256], BF16)
            vb = sbuf(ph, "vb", [128, 256], BF16)
            qkT = [sbuf(ph, "qkT%d" % i, [64, 1024], BF16) for i in range(2)]
            kvd = [sbuf(ph, "kvd%d" % i, [64, 260], F32) for i in range(2)]
            scA = sbuf(ph, "scA", [128, 512], F32)
            scB = sbuf(ph, "scB", [128, 512], F32)
            scS = sbuf(ph, "scS", [128, 512], BF16)
            oin = sbuf(ph, "oin", [128, 256], F32)
            rsl = sbuf(ph, "rsl", [128, 256], F32)
            uTs = sbuf(ph, "uTs", [128, 2, 128], F32)
            cqn = sbuf(ph, "cqn", [128, 320], BF16)
            cTt = sbuf(ph, "cTt", [128, 384], BF16)
            QR = sbuf(ph, "QR", [128, 128], F32)
            KR = sbuf(ph, "KR", [128, 32], F32)
            KR2 = sbuf(ph, "KR2", [128, 32], F32)
            rA = sbuf(ph, "rA", [128, 128], F32)
            rB = sbuf(ph, "rB", [128, 128], F32)
            Qf = sbuf(ph, "Qf", [128, 4, 96], F32)
            Kf = sbuf(ph, "Kf", [128, 4, 96], F32)
            Qb = sbuf(ph, "Qb", [128, 4, 96], BF16)
            Kb = sbuf(ph, "Kb", [128, 4, 96], BF16)
            sq384 = sbuf(ph, "sq384", [128, 384], F32)
            VA = sbuf(ph, "VA", [128, 4, 66], BF16)
            QKT = sbuf(ph, "QKT", [128, 1024], BF16)
            dve.memset(VA[:], 1.0)
            dve.memset(cTt[:], 0.0)
            dve.memset(QKT[:], 0.0)

            tpb = psum(ph, "tpb", [128, 1024], BF16)
            wbb = psum(ph, "wbb", [128, 1024], BF16)
            pin = [psum(ph, "pin%d" % i, [128, 512], F32) for i in range(2)]
            w0 = psum(ph, "w0", [128, 512], F32)
            w1 = psum(ph, "w1", [128, 512], F32)
            w2 = psum(ph, "w2", [128, 512], F32)
            wz = psum(ph, "wz", [128, 512], F32)

            def load(i):
                r0 = i * 128
                k.dma(out=xt[i % 2][:], in_=XS[r0:r0 + 128, :], rkey=i)
                if i >= 2:
                    p0 = r0 - NCTX
                    k.dma(out=rc[i % 3][:], in_=I["ropec"][p0:p0 + 128, :])
                    k.dma(out=rs_[i % 3][:], in_=I["ropes"][p0:p0 + 128, :])

            def X(i):
                isctx = i < 2
                w = 1 if isctx else 0
                c0 = i * 128
                own = isctx or ((i - 2) * 128 < OWNM)
                P = P2[i % 2]
                x_ = xt[i % 2]
                act.activation(junk[:], x_[:], AF.Square, accum_out=st4[:, 0:1])
                rstd_from_ssq(st4[:, 0:1], st4[:, 2:3], D, st4[:, 1:2])
                dve.tensor_scalar(xsb[:], x_[:], st4[:, 2:3], None, ALU.mult)
                for c in range(8):
                    pe.transpose(tpb[:, c * 128:(c + 1) * 128], xsb[:, c * 128:(c + 1) * 128], identb[:])
                for c in range(8):
                    if c % 2 == 0:
                        dve.tensor_scalar(hxT[:, c, :], tpb[:, c * 128:(c + 1) * 128], GS[:, 0, c, w:w + 1],
                                          SH[:, 0, c, w:w + 1], ALU.mult, ALU.add)
                    else:
                        act.activation(hxT[:, c, :], tpb[:, c * 128:(c + 1) * 128], AF.Identity,
                                       bias=SH[:, 0, c, w:w + 1], scale=GS[:, 0, c, w:w + 1])
                for n in range(5):
                    n0 = n * 512
                    nw = min(512, 2176 - n0)
                    pb = pin[n % 2]
                    if n == 0 and not own:
                        continue
                    for kc in range(8):
                        pe.matmul(pb[:, 0:nw], lhsT=hxT[:, kc, :], rhs=win[:, kc, n0:n0 + nw], start=(kc == 0), stop=(kc == 7))
                    if n % 2 == 0:
                        act.copy(P[:, n0:n0 + nw], pb[:, 0:nw])
                    else:
                        dve.tensor_copy(P[:, n0:n0 + nw], pb[:, 0:nw])
                if ((not isctx) or ctxout) and own:
                    act.activation(GA[:], P[:, 0:512], AF.Gelu_apprx_tanh)
                    act.activation(t256a[:], GA[:, 256:512], AF.Square)
                    dve.tensor_reduce(st4[:, 4:8], t256a[:].rearrange("p (h d) -> p h d", h=4), AX.X, ALU.add)
                    rstd_from_ssq(st4[:, 4:8], st4[:, 12:16], 64, st4[:, 8:12])
                    dve.tensor_tensor(t256b[:].rearrange("p (h d) -> p h d", h=4), GA[:, 256:512].rearrange("p (h d) -> p h d", h=4),
                                      st4[:, 12:16].unsqueeze(2).broadcast_to([128, 4, 64]), ALU.mult)
                    dve.tensor_tensor(vnb[:], t256b[:], sgn[:], ALU.mult)
                    for h in range(4):
                        pe.matmul(w0[:, h * 64:(h + 1) * 64], lhsT=sgw[:, h, :], rhs=vnb[:, h * 64:(h + 1) * 64], start=True, stop=True)
                    dve.tensor_tensor(t256a[:].rearrange("p (h d) -> p h d", h=4), w0[:, 0:256].rearrange("p (h d) -> p h d", h=4),
                                      sgb[:].unsqueeze(2).broadcast_to([128, 4, 64]), ALU.add)
                    dve.tensor_tensor(oab[:], t256a[:], GA[:, 0:256], ALU.mult)
                    for c in range(2):
                        pe.transpose(tpb[:, c * 128:(c + 1) * 128], oab[:, c * 128:(c + 1) * 128], identb[:])
                    act.copy(mxT[:].rearrange("p c t -> p (c t)"), tpb[:, 0:256])
                    for c in range(2):
                        k.dma(out=MIXT[c * 128:(c + 1) * 128, c0:c0 + 128], in_=mxT[:, c, :], wkey=("a", i, c))
                for c in range(2):
                    pe.transpose(w0[:, 256 + c * 128:256 + (c + 1) * 128], P[:, 1568 + c * 128:1568 + (c + 1) * 128], identf[:])
                dve.tensor_copy(uTs[:].rearrange("p c t -> p (c t)"), w0[:, 256:512])
                for c in range(2):
                    k.dma(out=UT[c * 128:(c + 1) * 128, c0:c0 + 128], in_=uTs[:, c, :], wkey=(i, c))

            def Y(i):
                isctx = i < 2
                w = 1 if isctx else 0
                c0 = i * 128
                own = isctx or ((i - 2) * 128 < OWNM)
                P = P2[i % 2]
                PB = P[:, 512:1568]
                dve.tensor_copy(glb[:], PB[:, 1024:1056])
                pe.transpose(wbb[0:32, 256:384], glb[:], identb[:])
                act.copy(glT[:], wbb[0:32, 256:384])
                pe.matmul(wz[:], lhsT=glT[:], rhs=wgb[:], start=True, stop=True)
                dve.tensor_tensor(zb[:], wz[:], bg[:], ALU.add)
                act.activation(zb[:], zb[:], AF.Exp, scale=-1.0)
                act.activation(Lg[:], zb[:], AF.Ln, bias=onec[:, 0:1])
                act.copy(vb[:], PB[:, 512:768])
                if own:
                    act.activation(rsl[:], PB[:, 768:1024], AF.Silu)
                    k.dma(out=RSD[c0:c0 + 128, :], in_=rsl[:], wkey=i)
                for d in range(2):
                    if not own and d == 0:
                        continue
                    Ld = Lg[:, d * 256:(d + 1) * 256]
                    pe.matmul(w1[:, 0:256], lhsT=tri[:, 2 * d, :], rhs=Ld, start=True, stop=True)
                    pe.matmul(w1[:, 256:512], lhsT=tri[:, 2 * d + 1, :], rhs=Ld, start=True, stop=True)
                    act.activation(E3[:], w1[:, 256:512], AF.Exp)
                    dve.tensor_tensor(kend[:], PB[:, 256:512], E3[:], ALU.mult)
                    if own:
                        act.activation(E1[:], w1[:, 0:256], AF.Exp)
                        act.activation(E2[:], w1[:, 0:256], AF.Exp, scale=-1.0)
                        dve.scalar_tensor_tensor(qin[:], PB[:, 0:256], 0.125, E1[:], ALU.mult, ALU.mult)
                        dve.tensor_tensor(kin[:], PB[:, 256:512], E2[:], ALU.mult)
                        qk = qkT[d]
                        for h in range(4):
                            pe.transpose(wbb[0:64, h * 128:(h + 1) * 128], qin[:, h * 64:(h + 1) * 64], identb[:])
                        for h in range(4):
                            pe.transpose(wbb[0:64, 512 + h * 128:512 + (h + 1) * 128], kin[:, h * 64:(h + 1) * 64], identb[:])
                        dve.tensor_copy(qk[:], wbb[0:64, :])
                        k.dma(out=QIT[d, i, :, :], in_=qk[:, 0:512], wkey=(d, i))
                        for h in range(4):
                            pe.matmul(w2[:, h * 128:(h + 1) * 128], lhsT=qk[:, 512 + h * 128:512 + (h + 1) * 128],
                                      rhs=qk[:, h * 128:(h + 1) * 128], start=True, stop=True)
                        dve.tensor_tensor((scA if d == 0 else scB)[:], w2[:], (maskf if d == 0 else maskb)[:], ALU.mult)
                    for h in range(4):
                        pe.matmul(wz[0:64, h * 64:(h + 1) * 64], lhsT=kend[:, h * 64:(h + 1) * 64], rhs=vb[:, h * 64:(h + 1) * 64],
                                  start=True, stop=True)
                    for h in range(4):
                        pe.matmul(wz[0:64, 256 + h:257 + h], lhsT=Lg[:, d * 256 + h * 64:d * 256 + (h + 1) * 64], rhs=n16c[:, 0:1],
                                  start=True, stop=True)
                    dve.tensor_copy(kvd[d][:, 0:256], wz[0:64, 0:256])
                    act.activation(kvd[d][:, 256:260], wz[0:64, 256:260], AF.Exp)
                    k.dma(out=KVD[d, i, :, :], in_=kvd[d][:], wkey=(d, i))
                if own:
                    dve.tensor_tensor(scS[:], scA[:], scB[:], ALU.add)
                    for h in range(4):
                        pe.matmul(w1[:, h * 64:(h + 1) * 64], lhsT=scS[:, h * 128:(h + 1) * 128], rhs=vb[:, h * 64:(h + 1) * 64],
                                  start=True, stop=True)
                    act.copy(oin[:], w1[:, 0:256])
                    k.dma(out=OACC[c0:c0 + 128, :], in_=oin[:], wkey=i)
                PD = P[:, 1824:2176]
                act.activation(junky[:, 0:224], PD[:, 0:224], AF.Square, accum_out=st4y[:, 4:5])
                act.activation(junky[:, 256:352], PD[:, 224:320], AF.Square, accum_out=st4y[:, 5:6])
                act.activation(st4y[:, 8:9], st4y[:, 4:5], AF.Sqrt, scale=1.0 / 224, bias=epsc[:, 0:1])
                act.activation(st4y[:, 9:10], st4y[:, 5:6], AF.Sqrt, scale=1.0 / 96, bias=epsc[:, 0:1])
                dve.reciprocal(st4y[:, 12:14], st4y[:, 8:10])
                if own:
                    dve.scalar_tensor_tensor(cqn[:, 0:224], PD[:, 0:224], st4y[:, 12:13], qn[:], ALU.mult, ALU.mult)
                dve.scalar_tensor_tensor(cqn[:, 224:320], PD[:, 224:320], st4y[:, 13:14], kvn[:], ALU.mult, ALU.mult)
                if own:
                    pe.transpose(wbb[:, 0:128], cqn[:, 0:128], identb[:])
                    pe.transpose(wbb[0:96, 128:256], cqn[:, 128:224], identb[:])
                pe.transpose(wbb[0:96, 256:384], cqn[:, 224:320], identb[:])
                if own:
                    act.copy(cTt[:, 0:128], wbb[:, 0:128])
                    act.copy(cTt[0:96, 128:384], wbb[0:96, 128:384])
                else:
                    act.copy(cTt[0:96, 256:384], wbb[0:96, 256:384])
                if own:
                    pe.matmul(w1[:, 0:384], lhsT=cTt[:, 0:128], rhs=wuq[:, 0, :], start=True, stop=False)
                    pe.matmul(w1[:, 0:384], lhsT=cTt[:, 128:256], rhs=wuq[:, 1, :], start=False, stop=True)
                pe.matmul(w2[:], lhsT=cTt[:, 256:384], rhs=wukv[:], start=True, stop=True)
                q3 = w1[:, 0:384].rearrange("p (h e) -> p h e", h=4)
                kv3 = w2[:].rearrange("p (h e) -> p h e", h=4)
                if own:
                    dve.tensor_copy(Qf[:, :, 0:64], q3[:, :, 0:64])
                    dve.tensor_copy(QR[:].rearrange("p (h e) -> p h e", h=4), q3[:, :, 64:96])
                dve.tensor_copy(Kf[:, :, 0:64], kv3[:, :, 0:64])
                dve.tensor_copy(VA[:, :, 0:64], kv3[:, :, 64:128])
                if isctx:
                    pool.tensor_copy(Qf[:, :, 64:96], QR[:].rearrange("p (h e) -> p h e", h=4))
                    pool.tensor_copy(Kf[:, :, 64:96], PD[:, 320:352].unsqueeze(1).broadcast_to([128, 4, 32]))
                else:
                    cs, sn = rc[i % 3], rs_[i % 3]
                    if own:
                        dve.tensor_tensor(rA[:], QR[:], cs[:], ALU.mult)
                        QRv = QR[:].rearrange("p (g a j) -> p g a j", g=8, a=2)
                        snv = sn[:].rearrange("p (g a j) -> p g a j", g=8, a=2)
                        rBv = rB[:].rearrange("p (g a j) -> p g a j", g=8, a=2)
                        pool.tensor_tensor(rBv[:, :, 0, :], QRv[:, :, 1, :], snv[:, :, 0, :], ALU.mult)
                        pool.tensor_tensor(rBv[:, :, 1, :], QRv[:, :, 0, :], snv[:, :, 1, :], ALU.mult)
                        dve.tensor_tensor(Qf[:, :, 64:96], rA[:].rearrange("p (h e) -> p h e", h=4),
                                          rB[:].rearrange("p (h e) -> p h e", h=4), ALU.add)
                    pool.tensor_copy(KR[:], PD[:, 320:352])
                    dve.tensor_tensor(rA[:, 0:32], KR[:], cs[:, 0:32], ALU.mult)
                    KRv = KR[:].rearrange("p (g a j) -> p g a j", g=2, a=2)
                    sn2 = sn[:, 0:32].rearrange("p (g a j) -> p g a j", g=2, a=2)
                    rB2 = rB[:, 0:32].rearrange("p (g a j) -> p g a j", g=2, a=2)
                    pool.tensor_tensor(rB2[:, :, 0, :], KRv[:, :, 1, :], sn2[:, :, 0, :], ALU.mult)
                    pool.tensor_tensor(rB2[:, :, 1, :], KRv[:, :, 0, :], sn2[:, :, 1, :], ALU.mult)
                    dve.tensor_tensor(KR2[:], rA[:, 0:32], rB[:, 0:32], ALU.add)
                    pool.tensor_copy(Kf[:, :, 64:96], KR2[:].unsqueeze(1).broadcast_to([128, 4, 32]))
                if own:
                    dve.tensor_copy(Qb[:], Qf[:])
                act.copy(Kb[:].rearrange("p h e -> p (h e)"), Kf[:].rearrange("p h e -> p (h e)"))
                if (not isctx or ctxout) and own:
                    act.activation(sq384[:], Qf[:].rearrange("p h e -> p (h e)"), AF.Square)
                    dve.tensor_reduce(st4y[:, 4:8], sq384[:].rearrange("p (h e) -> p h e", h=4), AX.X, ALU.add)
                    dve.tensor_tensor(QMAX[:], QMAX[:], st4y[:, 4:8], ALU.max)
                act.activation(sq384[:], Kf[:].rearrange("p h e -> p (h e)"), AF.Square)
                dve.tensor_reduce(st4y[:, 8:12], sq384[:].rearrange("p (h e) -> p h e", h=4), AX.X, ALU.add)
                dve.tensor_tensor(KMAX[:], KMAX[:], st4y[:, 8:12], ALU.max)
                if own:
                    for h in range(4):
                        pe.transpose(wbb[0:96, h * 128:(h + 1) * 128], Qb[:, h, :], identb[:])
                for h in range(4):
                    pe.transpose(wbb[0:96, 512 + h * 128:512 + (h + 1) * 128], Kb[:, h, :], identb[:])
                if own:
                    act.copy(QKT[0:96, :], wbb[0:96, :])
                else:
                    act.copy(QKT[0:96, 512:1024], wbb[0:96, 512:1024])
                for h in range(4):
                    if own:
                        k.dma(out=QTD[h, :, c0:c0 + 128], in_=QKT[:, h * 128:(h + 1) * 128], wkey=(i, h))
                    k.dma(out=KTD[h, :, c0:c0 + 128], in_=QKT[:, 512 + h * 128:512 + (h + 1) * 128], wkey=(i, h))
                k.dma(out=VAD[c0:c0 + 128, :], in_=VA[:].rearrange("p h e -> p (h e)"), wkey=i)

            load(0)
            for t_ in range(ntile + 1):
                if t_ + 1 < ntile:
                    load(t_ + 1)
                chains = []
                if t_ < ntile:
                    chains.append(k.record())
                    X(t_)
                    k.stop()
                if t_ >= 1:
                    chains.append(k.record())
                    Y(t_ - 1)
                    k.stop()
                k.replay(chains)
        k.barrier()

    def phase2(l, ctxout, last=False):
        OWNM = (H + 512) if (last and split) else nx
        nown = 2 + OWNM // 128
        with ExitStack() as ph:
            S = sbuf(ph, "S", [64, 256], F32)
            Sb = sbuf(ph, "Sb", [64, 256], BF16)
            gln = sbuf(ph, "gln", [128, 256], F32)
            qit = [sbuf(ph, "qit%d" % i, [64, 512], BF16) for i in range(2)]
            kvd = [sbuf(ph, "kvd%d" % i, [64, 260], F32) for i in range(2)]
            oac = [sbuf(ph, "oac%d" % i, [128, 256], F32) for i in range(2)]
            rsl = [sbuf(ph, "rsl%d" % i, [128, 256], F32) for i in range(2)]
            osum = sbuf(ph, "osum", [128, 256], F32)
            t1 = sbuf(ph, "t1", [128, 256], F32)
            t2 = sbuf(ph, "t2", [128, 256], F32)
            st4 = sbuf(ph, "st4", [128, 16], F32)
            obb = sbuf(ph, "obb", [128, 256], BF16)
            mxT = sbuf(ph, "mxT", [128, 2, 128], BF16)
            ops_ = [psum(ph, "ops%d" % i, [128, 512], F32) for i in range(2)]
            wbb = psum(ph, "wbb", [128, 1024], BF16)
            k.dma(out=gln[:], in_=I["glan"][l, :, :])
            for d in range(2):
                order = list(range(min(ntile, nown))) if d == 0 else [1, 0] + list(range(ntile - 1, 1, -1))
                dve.memset(S[:], 0.0)

                def load(n):
                    i = order[n]
                    k.dma(out=qit[n % 2][:], in_=QIT[d, i, :, :], rkey=(d, i))
                    k.dma(out=kvd[n % 2][:], in_=KVD[d, i, :, :], rkey=(d, i))
                    k.dma(out=oac[n % 2][:], in_=OACC[i * 128:(i + 1) * 128, :], rkey=i)
                    if d == 1:
                        k.dma(out=rsl[n % 2][:], in_=RSD[i * 128:(i + 1) * 128, :], rkey=i)
                load(0)
                for n, i in enumerate(order):
                    if n + 1 < len(order):
                        load(n + 1)
                    need_o = ((i >= 2) or ctxout) and i < nown
                    q_, kv_, oa_ = qit[n % 2], kvd[n % 2], oac[n % 2]
                    if need_o:
                        act.copy(Sb[:], S[:])
                        op = ops_[n % 2]
                        for h in range(4):
                            pe.matmul(op[:, h * 64:(h + 1) * 64], lhsT=q_[:, h * 128:(h + 1) * 128], rhs=Sb[:, h * 64:(h + 1) * 64],
                                      start=True, stop=True)
                        dve.tensor_tensor(osum[:], op[:, 0:256], oa_[:], ALU.add)
                        if d == 0:
                            k.dma(out=OACC[i * 128:(i + 1) * 128, :], in_=osum[:], wkey=i)
                        else:
                            act.activation(t1[:], osum[:], AF.Square)
                            dve.tensor_reduce(st4[:, 0:4], t1[:].rearrange("p (h e) -> p h e", h=4), AX.X, ALU.add)
                            rstd_from_ssq(st4[:, 0:4], st4[:, 8:12], 64, st4[:, 4:8])
                            dve.tensor_tensor(t2[:].rearrange("p (h e) -> p h e", h=4), osum[:].rearrange("p (h e) -> p h e", h=4),
                                              st4[:, 8:12].unsqueeze(2).broadcast_to([128, 4, 64]), ALU.mult)
                            dve.tensor_tensor(t1[:], t2[:], gln[:], ALU.mult)
                            dve.tensor_tensor(obb[:], t1[:], rsl[n % 2][:], ALU.mult)
                            for c in range(2):
                                pe.transpose(wbb[:, c * 128:(c + 1) * 128], obb[:, c * 128:(c + 1) * 128], identb[:])
                            act.copy(mxT[:].rearrange("p c t -> p (c t)"), wbb[:, 0:256])
                            for c in range(2):
                                k.dma(out=MIXT[256 + c * 128:256 + (c + 1) * 128, i * 128:(i + 1) * 128], in_=mxT[:, c, :], wkey=("b", i, c))
                    dve.tensor_tensor(S[:].rearrange("p (h e) -> p h e", h=4), S[:].rearrange("p (h e) -> p h e", h=4),
                                      kv_[:, 256:260].unsqueeze(2).broadcast_to([64, 4, 64]), ALU.mult)
                    dve.tensor_tensor(S[:], S[:], kv_[:, 0:256], ALU.add)
                k.barrier()

    def phase3(l, ctxout, last=False):
        LC = 512
        OWNM = (H + 512) if (last and split) else nx
        with ExitStack() as ph:
            pr = sbuf(ph, "pr", [128, 2, 3, 8], F32)
            bre = sbuf(ph, "bre", [128, 2, 8, 16], F32)
            bim = sbuf(ph, "bim", [128, 2, 8, 16], F32)
            cre = sbuf(ph, "cre", [128, 2, 8, 16], F32)
            cim = sbuf(ph, "cim", [128, 2, 8, 16], F32)
            dT = sbuf(ph, "dT", [128, 2], F32)
            bgl = sbuf(ph, "bgl", [128, 2], F32)
            wgl_f = sbuf(ph, "wgl_f", [128, 2, 256], F32)
            wgl = sbuf(ph, "wgl", [128, 2, 256], BF16)
            for d in range(2):
                k.dma(out=pr[:, d, 0, :], in_=I["s5are"][l, d, :, :])
                k.dma(out=pr[:, d, 1, :], in_=I["s5aim"][l, d, :, :])
                k.dma(out=pr[:, d, 2, :], in_=I["s5ldt"][l, d, :, :])
                k.dma(out=bre[:, d], in_=I["s5bre"][l, d, :, :, :])
                k.dma(out=bim[:, d], in_=I["s5bim"][l, d, :, :, :])
                k.dma(out=cre[:, d], in_=I["s5cre"][l, d, :, :, :])
                k.dma(out=cim[:, d], in_=I["s5cim"][l, d, :, :, :])
            k.dma(out=dT[:], in_=I["s5dT"][l, :, :])
            k.dma(out=bgl[:], in_=I["s5bgluT"][l, :, :])
            k.dma(out=wgl_f[:], in_=I["s5wglu"][l, :, :].rearrange("(c p) n -> p c n", p=128))
            dve.tensor_copy(wgl[:], wgl_f[:])
            e = sbuf(ph, "e", [128, 2, 16, 8], F32)
            dve.memset(e[:], 0.0)
            A_RE, A_IM, LDT = pr[:, :, 0, :], pr[:, :, 1, :], pr[:, :, 2, :]
            DT, AR, TH, RM, S16, S8, C8, T0, T1, LR, LI = [e[:, :, n, :] for n in range(11)]
            act.activation(DT, LDT, AF.Exp)
            dve.tensor_tensor(AR, A_RE, DT, ALU.mult)
            dve.tensor_tensor(TH, A_IM, DT, ALU.mult)
            act.activation(RM, AR, AF.Exp)
            act.activation(S16, TH, AF.Sin, scale=1.0 / 16)
            act.activation(S8, TH, AF.Sin, scale=1.0 / 8)
            dve.tensor_tensor(T0, S16, S16, ALU.mult)
            dve.tensor_scalar(C8, T0, -2.0, 1.0, ALU.mult, ALU.add)
            cc, ss = C8, S8
            for it in range(3):
                dve.tensor_tensor(T0, cc, cc, ALU.mult)
                dve.tensor_tensor(T1, ss, ss, ALU.mult)
                dve.tensor_tensor(LI, cc, ss, ALU.mult)
                dve.tensor_tensor(T0, T0, T1, ALU.subtract)
                dve.tensor_scalar(S8, LI, 2.0, None, ALU.mult)
                dve.tensor_copy(C8, T0)
                cc, ss = C8, S8
            CT, ST = C8, S8
            dve.tensor_tensor(LR, RM, CT, ALU.mult)
            dve.tensor_tensor(LI, RM, ST, ALU.mult)
            NR, DEN, CR, CI, T2 = [e[:, :, n, :] for n in range(11, 16)]
            dve.tensor_scalar(NR, LR, -1.0, None, ALU.add)
            dve.tensor_tensor(T0, A_RE, A_RE, ALU.mult)
            dve.tensor_tensor(T1, A_IM, A_IM, ALU.mult)
            dve.tensor_tensor(DEN, T0, T1, ALU.add)
            dve.reciprocal(DEN, DEN)
            dve.tensor_tensor(T0, NR, A_RE, ALU.mult)
            dve.tensor_tensor(T1, LI, A_IM, ALU.mult)
            dve.tensor_tensor(T0, T0, T1, ALU.add)
            dve.tensor_tensor(CR, T0, DEN, ALU.mult)
            dve.tensor_tensor(T0, LI, A_RE, ALU.mult)
            dve.tensor_tensor(T1, NR, A_IM, ALU.mult)
            dve.tensor_tensor(T0, T0, T1, ALU.subtract)
            dve.tensor_tensor(CI, T0, DEN, ALU.mult)
            bbr = sbuf(ph, "bbr", [128, 2, 8, 16], F32)
            bbi = sbuf(ph, "bbi", [128, 2, 8, 16], F32)
            tb = sbuf(ph, "tb", [128, 2, 8, 16], F32)
            for d in range(2):
                crb = e[:, d, 13, :].unsqueeze(2).broadcast_to([128, 8, 16])
                cib = e[:, d, 14, :].unsqueeze(2).broadcast_to([128, 8, 16])
                dve.tensor_tensor(bbr[:, d], bre[:, d], crb, ALU.mult)
                dve.tensor_tensor(tb[:, d], bim[:, d], cib, ALU.mult)
                dve.tensor_tensor(bbr[:, d], bbr[:, d], tb[:, d], ALU.subtract)
                dve.tensor_tensor(bbi[:, d], bim[:, d], crb, ALU.mult)
                dve.tensor_tensor(tb[:, d], bre[:, d], cib, ALU.mult)
                dve.tensor_tensor(bbi[:, d], bbi[:, d], tb[:, d], ALU.add)
            BT = sbuf(ph, "BT", [128, 2, 2, 8, 128], BF16)
            CP = sbuf(ph, "CP", [128, 2, 2, 8, 128], BF16)
            Z = [sbuf(ph, "Z%d" % i, [128, 128], F32) for i in range(2)]
            pz = [psum(ph, "pz%d" % i, [128, 512], F32) for i in range(2)]
            dve.memset(CP[:], 0.0)
            dve.memset(Z[0][:], 0.0)
            dve.memset(Z[1][:], 0.0)
            n = 0
            for d in range(2):
                for ri, src in enumerate((bbr, bbi)):
                    for j in range(8):
                        jj = j % 4
                        z = Z[n % 2]
                        if jj != (j - 1) % 4 or True:
                            pool.memset(z[:], 0.0)
                        pool.tensor_copy(z[0:64, 32 * jj:32 * jj + 16], src[0:64, d, j, :])
                        pool.tensor_copy(z[64:128, 32 * jj + 16:32 * jj + 32], src[64:128, d, j, :])
                        pe.transpose(pz[n % 2][:, 0:128], z[:], identf[:])
                        act.copy(BT[:, d, ri, j, :], pz[n % 2][:, 0:128])
                        n += 1
                for j in range(8):
                    jj = j % 4
                    dve.tensor_copy(CP[0:64, d, 0, j, 32 * jj:32 * jj + 16], cre[0:64, d, j, :])
                    dve.tensor_copy(CP[64:128, d, 0, j, 32 * jj + 16:32 * jj + 32], cre[64:128, d, j, :])
                    dve.tensor_scalar(CP[0:64, d, 1, j, 32 * jj:32 * jj + 16], cim[0:64, d, j, :], -1.0, None, ALU.mult)
                    dve.tensor_scalar(CP[64:128, d, 1, j, 32 * jj + 16:32 * jj + 32], cim[64:128, d, j, :], -1.0, None, ALU.mult)
            RR = sbuf(ph, "RR", [128, 2, 8, LC], F32)
            RI = sbuf(ph, "RI", [128, 2, 8, LC], F32)
            tt = sbuf(ph, "tt", [128, LC], F32)
            for d in range(2):
                for j in range(8):
                    dve.tensor_copy(RR[:, d, j, 0:1], e[:, d, 6, j:j + 1])
                    dve.tensor_copy(RI[:, d, j, 0:1], e[:, d, 5, j:j + 1])
                    wdt = 1
                    while wdt < LC:
                        cw = RR[:, d, j, wdt - 1:wdt]
                        sw = RI[:, d, j, wdt - 1:wdt]
                        a_r, a_i = RR[:, d, j, 0:wdt], RI[:, d, j, 0:wdt]
                        dve.tensor_scalar(tt[:, 0:wdt], a_i, sw, None, ALU.mult)
                        dve.scalar_tensor_tensor(RR[:, d, j, wdt:2 * wdt], a_r, cw, tt[:, 0:wdt], ALU.mult, ALU.subtract)
                        dve.tensor_scalar(tt[:, 0:wdt], a_i, cw, None, ALU.mult)
                        dve.scalar_tensor_tensor(RI[:, d, j, wdt:2 * wdt], a_r, sw, tt[:, 0:wdt], ALU.mult, ALU.add)
                        wdt *= 2
            uTf = [sbuf(ph, "uTf%d" % i, [128, 2, LC], F32) for i in range(2)]
            uTb = sbuf(ph, "uTb", [128, 2, LC], BF16)
            yfl = [sbuf(ph, "yfl%d" % i, [128, 2, LC], F32) for i in range(2)]
            ta_ = [sbuf(ph, "ta%d" % i, [128, LC], F32) for i in range(2)]
            tb2_ = [sbuf(ph, "tb2%d" % i, [128, LC], F32) for i in range(2)]
            tc_ = [sbuf(ph, "tc%d" % i, [128, LC], F32) for i in range(2)]
            td_ = [sbuf(ph, "td%d" % i, [128, LC], F32) for i in range(2)]
            btr_ = [sbuf(ph, "btr%d" % i, [128, LC], F32) for i in range(2)]
            bti_ = [sbuf(ph, "bti%d" % i, [128, LC], F32) for i in range(2)]
            wr_ = [sbuf(ph, "wr%d" % i, [128, LC], F32) for i in range(2)]
            wi_ = [sbuf(ph, "wi%d" % i, [128, LC], F32) for i in range(2)]
            PR = [sbuf(ph, "PR0", [128, 4, 4, LC], BF16)] * 2
            h0 = sbuf(ph, "h0", [128, 2, 8], F32)
            hc_ = [sbuf(ph, "hc%d" % i, [128, 4], F32) for i in range(2)]
            ysum = sbuf(ph, "ysum", [128, 2, LC], F32)
            ygf = sbuf(ph, "ygf", [128, 2, LC], F32)
            ygb = sbuf(ph, "ygb", [128, 2, LC], BF16)
            sg = sbuf(ph, "sg", [128, LC], F32)
            ocb = sbuf(ph, "ocb", [128, 2, LC], BF16)
            pbr_ = pz
            pbi_ = [psum(ph, "pbi%d" % i, [128, 512], F32) for i in range(2)]
            py = [psum(ph, "py%d" % i, [128, 512], F32) for i in range(2)]
            pg = psum(ph, "pg", [128, 512], F32)
            chunks = [(0, NCTX)] + [(NCTX + c * LC, min(LC, nx - c * LC)) for c in range((nx + LC - 1) // LC)]
            for d in range(2):
                nownc = 1 + min(len(chunks) - 1, OWNM // LC)
                order = list(range(nownc)) if d == 0 else [0] + list(range(len(chunks) - 1, 0, -1))
                dve.memset(h0[:], 0.0)

                def load(n):
                    t0, Lc = chunks[order[n]]
                    k.dma(out=uTf[n % 2][:, :, 0:Lc], in_=UT[:, t0:t0 + Lc].rearrange("(c p) t -> p c t", p=128), rkey=None)
                    if d == 1:
                        k.dma(out=yfl[n % 2][:, :, 0:Lc], in_=YF[:, t0:t0 + Lc].rearrange("(c p) t -> p c t", p=128), rkey=None)
                load(0)
                for n, ci in enumerate(order):
                    if n + 1 < len(order):
                        load(n + 1)
                    t0, Lc = chunks[ci]
                    need_y = ((ci > 0) or ctxout) and ci < nownc
                    uf = uTf[n % 2]
                    for c_ in range(2):
                        act.copy(uTb[:, c_, 0:Lc], uf[:, c_, 0:Lc])

                    def tv(ap):
                        return ap if d == 0 else ap[:, ::-1]
                    def ymm(c):
                        pp = PR[0]
                        for j2 in range(4):
                            for pi in range(4):
                                pe.matmul(py[c][:, 0:Lc], lhsT=CP[:, d, pi // 2, 4 * c + j2, :], rhs=pp[:, j2, pi, 0:Lc],
                                          start=(j2 == 0 and pi == 0), stop=(j2 == 3 and pi == 3))
                    chains = []
                    for j in range(8):
                        chains.append(k.record())
                        jj = j % 4
                        jb = j % 2
                        ta, tb2, tc, td, btr, bti, wr, wi, hc, pbr, pbi = (ta_[jb], tb2_[jb], tc_[jb], td_[jb], btr_[jb], bti_[jb],
                                                                       wr_[jb], wi_[jb], hc_[jb], pbr_[jb], pbi_[jb])
                        Rr = tv(RR[:, d, j, 0:Lc])
                        Ri = tv(RI[:, d, j, 0:Lc])
                        pe.matmul(pbr[:, 0:Lc], lhsT=BT[:, d, 0, j, :], rhs=uTb[:, j // 4, 0:Lc], start=True, stop=True)
                        pe.matmul(pbi[:, 0:Lc], lhsT=BT[:, d, 1, j, :], rhs=uTb[:, j // 4, 0:Lc], start=True, stop=True)
                        dve.tensor_tensor(ta[:, 0:Lc], pbr[:, 0:Lc], Rr, ALU.mult)
                        dve.tensor_tensor(tb2[:, 0:Lc], pbi[:, 0:Lc], Ri, ALU.mult)
                        dve.tensor_tensor(btr[:, 0:Lc], ta[:, 0:Lc], tb2[:, 0:Lc], ALU.add)
                        dve.tensor_tensor(tc[:, 0:Lc], pbi[:, 0:Lc], Rr, ALU.mult)
                        dve.tensor_tensor(td[:, 0:Lc], pbr[:, 0:Lc], Ri, ALU.mult)
                        dve.tensor_tensor(bti[:, 0:Lc], tc[:, 0:Lc], td[:, 0:Lc], ALU.subtract)
                        rm = e[:, d, 3, j:j + 1].broadcast_to([128, Lc])
                        dve.tensor_tensor_scan(tv(wr[:, 0:Lc]), rm, tv(btr[:, 0:Lc]), h0[:, 0, j:j + 1], ALU.mult, ALU.add)
                        dve.tensor_tensor_scan(tv(wi[:, 0:Lc]), rm, tv(bti[:, 0:Lc]), h0[:, 1, j:j + 1], ALU.mult, ALU.add)
                        tl = Lc - 1 if d == 0 else 0
                        rl = RR[:, d, j, Lc - 1:Lc]
                        il = RI[:, d, j, Lc - 1:Lc]
                        dve.tensor_scalar(hc[:, 0:1], wi[:, tl:tl + 1], il, None, ALU.mult)
                        dve.tensor_scalar(hc[:, 1:2], wr[:, tl:tl + 1], il, None, ALU.mult)
                        dve.scalar_tensor_tensor(h0[:, 0, j:j + 1], wr[:, tl:tl + 1], rl, hc[:, 0:1], ALU.mult, ALU.subtract)
                        dve.scalar_tensor_tensor(h0[:, 1, j:j + 1], wi[:, tl:tl + 1], rl, hc[:, 1:2], ALU.mult, ALU.add)
                        if need_y:
                            pp = PR[(j // 4) % 2]
                            pool.tensor_tensor(pp[:, jj, 0, 0:Lc], wr[:, 0:Lc], Rr, ALU.mult)
                            dve.scalar_tensor_tensor(pp[:, jj, 1, 0:Lc], wi[:, 0:Lc], -1.0, Ri, ALU.mult, ALU.mult)
                            dve.tensor_tensor(pp[:, jj, 2, 0:Lc], wi[:, 0:Lc], Rr, ALU.mult)
                            dve.tensor_tensor(pp[:, jj, 3, 0:Lc], wr[:, 0:Lc], Ri, ALU.mult)
                        k.stop()
                        if j % 2 == 1:
                            k.replay(chains)
                            chains = []
                            if need_y and jj == 3:
                                ymm(j // 4)
                    if need_y:
                        if d == 0:
                            for c in range(2):
                                act.copy(ysum[:, c, 0:Lc], py[c][:, 0:Lc])
                            for c in range(2):
                                k.dma(out=YF[c * 128:(c + 1) * 128, t0:t0 + Lc], in_=ysum[:, c, 0:Lc], wkey=(ci, c))
                        else:
                            for c in range(2):
                                dve.tensor_tensor(ysum[:, c, 0:Lc], py[c][:, 0:Lc], yfl[n % 2][:, c, 0:Lc], ALU.add)
                                dve.scalar_tensor_tensor(ysum[:, c, 0:Lc], uf[:, c, 0:Lc], dT[:, c:c + 1], ysum[:, c, 0:Lc], ALU.mult, ALU.add)
                                act.activation(ygf[:, c, 0:Lc], ysum[:, c, 0:Lc], AF.Gelu_apprx_tanh)
                                act.copy(ygb[:, c, 0:Lc], ygf[:, c, 0:Lc])
                            for c2 in range(2):
                                for c in range(2):
                                    pe.matmul(pg[:, 0:Lc], lhsT=wgl[:, c, c2 * 128:(c2 + 1) * 128], rhs=ygb[:, c, 0:Lc],
                                              start=(c == 0), stop=(c == 1))
                                act.activation(sg[:, 0:Lc], pg[:, 0:Lc], AF.Sigmoid, bias=bgl[:, c2:c2 + 1])
                                dve.tensor_tensor(ocb[:, c2, 0:Lc], ygf[:, c2, 0:Lc], sg[:, 0:Lc], ALU.mult)
                            for c in range(2):
                                k.dma(out=MIXT[512 + c * 128:512 + (c + 1) * 128, t0:t0 + Lc], in_=ocb[:, c, 0:Lc], wkey=("c", ci, c))
                k.barrier()

    def phase4(l, ctxout, last=False):
        OWNM = (H + 512) if (last and split) else nx
        scale = 96.0 ** -0.5
        nkc = ntile
        with ExitStack() as ph:
            KT = sbuf(ph, "KT", [128, 4, NT], BF16)
            VAs = sbuf(ph, "VAs", [128, nkc, 264], BF16)
            QTb = [sbuf(ph, "QTb%d" % i, [128, 512], BF16) for i in range(2)]
            Pe = [sbuf(ph, "Pe%d" % i, [128, 1024], BF16) for i in range(3)]
            negb = sbuf(ph, "negb", [128, 1], F32)
            m2 = sbuf(ph, "m2", [128, 2], F32)
            m2t = sbuf(ph, "m2t", [2, 2], F32)
            rrow = sbuf(ph, "rrow", [128, 512], F32)
            bcs = sbuf(ph, "bcs", [64, 512], F32)
            odb = [sbuf(ph, "odb%d" % i, [64, 512], BF16) for i in range(2)]
            sp_ = [psum(ph, "sps%d" % i, [128, 1024], F32) for i in range(3)]
            acc = [psum(ph, "acc0", [128, 512], F32)] * 2
            pbc = psum(ph, "pbc", [128, 512], F32)
            for h in range(4):
                k.dma(out=KT[:, h, :], in_=KTD[h, :, :])
            k.dma(out=VAs[:], in_=VAD[:, :].rearrange("(kc p) c -> p kc c", p=128))
            dve.tensor_reduce(m2[:, 0:1], QMAX[:], AX.X, ALU.max)
            dve.tensor_reduce(m2[:, 1:2], KMAX[:], AX.X, ALU.max)
            pe.transpose(pbc[0:2, 0:128], m2[:], identf[:])
            dve.tensor_reduce(m2t[:, 0:1], pbc[0:2, 0:128], AX.X, ALU.max)
            act.activation(m2t[:, 1:2], m2t[:, 0:1], AF.Ln)
            pe.matmul(pbc[:, 256:257], lhsT=onesf[0:2, :], rhs=m2t[:, 1:2], start=True, stop=True)
            act.activation(negb[:], pbc[:, 256:257], AF.Exp, scale=0.5)
            dve.tensor_scalar(negb[:], negb[:], -scale, None, ALU.mult)
            jobs = []
            for h in range(4):
                if ctxout:
                    jobs.append((h, 0, NCTX, [0, 1]))
                for qb in range(min(nx, OWNM) // 512):
                    jobs.append((h, NCTX + qb * 512, 512, list(range(nkc))))

            def load(n):
                h, q0, qw, _ = jobs[n]
                k.dma(out=QTb[n % 2][:, 0:qw], in_=QTD[h, :, q0:q0 + qw])
            load(0)
            for n, (h, q0, qw, kcs) in enumerate(jobs):
                if n + 1 < len(jobs):
                    load(n + 1)
                Q = QTb[n % 2]
                ac = acc[n % 2]

                def scores(pi):
                    s__ = sp_[(pi // 2) % 3]
                    for u in range(2):
                        kc = kcs[pi + u]
                        pe.matmul(s__[:, u * 512:u * 512 + qw], lhsT=KT[:, h, kc * 128:(kc + 1) * 128], rhs=Q[:, 0:qw], start=True, stop=True)
                scores(0)
                if 2 < len(kcs):
                    scores(2)
                for pi in range(0, len(kcs), 2):
                    s_ = sp_[(pi // 2) % 3]
                    P_ = Pe[(pi // 2) % 3]
                    if pi + 4 < len(kcs):
                        scores(pi + 4)
                    if qw == 512:
                        act.activation(P_[:], s_[:], AF.Exp, bias=negb[:, 0:1], scale=scale)
                    else:
                        for u in range(2):
                            act.activation(P_[:, u * 512:u * 512 + qw], s_[:, u * 512:u * 512 + qw], AF.Exp, bias=negb[:, 0:1], scale=scale)
                    for u in range(2):
                        kc = kcs[pi + u]
                        pe.matmul(ac[0:65, 0:qw], lhsT=VAs[:, kc, h * 66:h * 66 + 65], rhs=P_[:, u * 512:u * 512 + qw],
                                  start=(pi == 0 and u == 0), stop=(pi + u == len(kcs) - 1))
                dve.reciprocal(rrow[64:65, 0:qw], ac[64:65, 0:qw])
                pe.matmul(pbc[0:64, 0:qw], lhsT=onesf[64:65, 0:64], rhs=rrow[64:65, 0:qw], start=True, stop=True)
                act.copy(bcs[:, 0:qw], pbc[0:64, 0:qw])
                dve.tensor_tensor(odb[n % 2][:, 0:qw], ac[0:64, 0:qw], bcs[:, 0:qw], ALU.mult)
                k.dma(out=MIXT[768 + 64 * h:768 + 64 * (h + 1), q0:q0 + qw], in_=odb[n % 2][:, 0:qw], wkey=("d", n))
        k.barrier()

    def epilogue(po, x_, gb, xo, st4, junk, tmpf):
        act.activation(junk[:], po[:], AF.Square, accum_out=st4[:, 0:1])
        rstd_from_ssq(st4[:, 0:1], st4[:, 2:3], D, st4[:, 1:2])
        dve.scalar_tensor_tensor(tmpf[:], po[:], st4[:, 2:3], gb[:], ALU.mult, ALU.mult)
        dve.tensor_tensor(xo[:], tmpf[:], x_[:], ALU.add)

    def load_cast_rows(dst, src_rows, nk, ncols, stg):
        for kc in range(nk):
            s_ = stg[kc % 2]
            k.dma(out=s_[:, 0:ncols], in_=src_rows[kc * 128:(kc + 1) * 128, :])
            (pool if kc % 2 else dve).tensor_copy(dst[:, kc, :], s_[:, 0:ncols])

    def phase5(l, ctxout, last=False):
        OWN5 = (H + 128) if (last and split) else nx
        with ExitStack() as ph:
            wo = sbuf(ph, "wo", [128, 8, D], BF16)
            stg = [sbuf(ph, "stg%d" % i, [128, D], F32) for i in range(2)]
            load_cast_rows(wo, I["w_out"][l], 8, D, stg)
            mx = [sbuf(ph, "mx%d" % i, [128, 8, 128], BF16) for i in range(2)]
            xt = [sbuf(ph, "xt%d" % i, [128, D], F32) for i in range(2)]
            xo = [sbuf(ph, "xo%d" % i, [128, D], F32) for i in range(2)]
            junk = sbuf(ph, "junk", [128, D], BF16)
            tmpf = sbuf(ph, "tmpf", [128, D], F32)
            st4 = sbuf(ph, "st4", [128, 4], F32)
            po = [psum(ph, "po%d" % i, [128, 1024], F32) for i in range(2)]
            tiles = list(range(0 if ctxout else 2, min(ntile, 2 + OWN5 // 128)))

            def load(n):
                i = tiles[n]
                k.dma(out=mx[n % 2][:], in_=MIXT[:, i * 128:(i + 1) * 128].rearrange("(c p) t -> p c t", p=128))
                k.dma(out=xt[n % 2][:], in_=XS[i * 128:(i + 1) * 128, :], rkey=i)
            load(0)
            for n, i in enumerate(tiles):
                if n + 1 < len(tiles):
                    load(n + 1)
                p_ = po[n % 2]
                for nn in range(2):
                    for kc in range(8):
                        pe.matmul(p_[:, nn * 512:(nn + 1) * 512], lhsT=mx[n % 2][:, kc, :], rhs=wo[:, kc, nn * 512:(nn + 1) * 512],
                                  start=(kc == 0), stop=(kc == 7))
                epilogue(p_, xt[n % 2], GB[0][1 if i < 2 else 0], xo[n % 2], st4, junk, tmpf)
                k.dma(out=XS[i * 128:(i + 1) * 128, :], in_=xo[n % 2][:], wkey=i)
        k.barrier()

    def phase6(l, ctxout, last):
        OWN = 510
        with ExitStack() as ph:
            wdn = sbuf(ph, "wdn", [128, 22, D], BF16)
            cw = sbuf(ph, "cw", [128, 44, 3], F32)
            cb = sbuf(ph, "cb", [128, 44], F32)
            k.dma(out=cw[:], in_=I["convwT"][l, :, :, :])
            k.dma(out=cb[:], in_=I["convbT"][l, :, :])
            with ExitStack() as pp_:
                stg = [sbuf(pp_, "stg%d" % i, [128, DFF], F32) for i in range(2)]
                cbf = [sbuf(pp_, "cbf%d" % i, [128, DFF], BF16) for i in range(2)]
                load_cast_rows(wdn, I["w_dn"][l], 22, D, stg)
                n = 0
                for kc in range(8):
                    for half in range(2):
                        s_, c_ = stg[n % 2], cbf[n % 2]
                        k.dma(out=s_[:], in_=I["w_up"][l, kc * 128:(kc + 1) * 128, half * DFF:(half + 1) * DFF])
                        (pool if n % 2 else dve).tensor_copy(c_[:], s_[:])
                        for i_ in range(22):
                            k.dma(out=WUPS[i_, :, kc * 256 + half * 128:kc * 256 + (half + 1) * 128], in_=c_[:, i_ * 128:(i_ + 1) * 128],
                                  wkey=(kc, half, i_))
                        n += 1
                k.barrier()
            NW = 6
            wst = [sbuf(ph, "wst%d" % i, [128, 8, 256], BF16) for i in range(NW)]
            xb = [sbuf(ph, "xb%d" % i, [128, 4, D], F32) for i in range(2)]
            xsb = sbuf(ph, "xsb", [128, D], BF16)
            junk = sbuf(ph, "junk", [128, D], BF16)
            hxT = sbuf(ph, "hxT", [128, 8, 512], BF16)
            hid = sbuf(ph, "hid", [128, 22, 512], BF16)
            ca2 = [sbuf(ph, "ca%d" % i, [128, 512], F32) for i in range(2)]
            cg2 = [sbuf(ph, "cg%d" % i, [128, 512], F32) for i in range(2)]
            ga2 = [sbuf(ph, "ga%d" % i, [128, 512], F32) for i in range(2)]
            st4 = sbuf(ph, "st4", [128, 4], F32)
            tmpf = sbuf(ph, "tmpf", [128, D], F32)
            xo = [sbuf(ph, "xo%d" % i, [128, D], F32) for i in range(2)]
            tpb = psum(ph, "tpb", [128, 1024], BF16)
            pz = [psum(ph, "pz%d" % i, [128, 512], F32) for i in range(4)]
            po = psum(ph, "po", [128, 1024], F32)
            pool.memset(hid[:], 0.0)
            xown = H if (last and split) else nx
            xval = min(nx, xown + 1)
            segs = ([(0, NCTX, NCTX, 1)] if ctxout else []) + [(NCTX, xown, xval, 0)]
            blocks = []
            for (s0, so, sl, w) in segs:
                for b0 in range(0, so, OWN):
                    blocks.append((s0, sl, w, b0, min(OWN, so - b0)))
            nwl = [0]
            total_w = len(blocks) * 22

            def loadw():
                if nwl[0] < total_w:
                    i_ = nwl[0] % 22
                    k.dma(out=wst[nwl[0] % NW][:].rearrange("p k c -> p (k c)"), in_=WUPS[i_, :, :])
                    nwl[0] += 1

            def load(n):
                s0, sl, w, b0, own = blocks[n]
                X = xb[n % 2]
                lo, hi = b0 - 1, b0 - 1 + 512
                vlo, vhi = max(lo, 0), min(hi, sl)
                for s in range(4):
                    a, b_ = lo + 128 * s, lo + 128 * (s + 1)
                    va, vb_ = max(a, vlo), min(b_, vhi)
                    if va >= vb_:
                        pool.memset(X[:, s, :], 0.0)
                        continue
                    if va > a or vb_ < b_:
                        pool.memset(X[:, s, :], 0.0)
                    k.dma(out=X[va - a:vb_ - a, s, :], in_=XS[s0 + va:s0 + vb_, :])
            load(0)
            for _ in range(3):
                loadw()
            nuse = 0
            for n, (s0, sl, w, b0, own) in enumerate(blocks):
                if n + 1 < len(blocks):
                    load(n + 1)
                X = xb[n % 2]
                lo = b0 - 1
                vlo, vhi = max(lo, 0), min(lo + 512, sl)
                for s in range(4):
                    act.activation(junk[:], X[:, s, :], AF.Square, accum_out=st4[:, 0:1])
                    rstd_from_ssq(st4[:, 0:1], st4[:, 2:3], D, st4[:, 1:2])
                    dve.tensor_scalar(xsb[:], X[:, s, :], st4[:, 2:3], None, ALU.mult)
                    for c in range(8):
                        pe.transpose(tpb[:, c * 128:(c + 1) * 128], xsb[:, c * 128:(c + 1) * 128], identb[:])
                    for c in range(8):
                        if c % 2 == 0:
                            dve.tensor_scalar(hxT[:, c, s * 128:(s + 1) * 128], tpb[:, c * 128:(c + 1) * 128], GS[:, 1, c, w:w + 1],
                                              SH[:, 1, c, w:w + 1], ALU.mult, ALU.add)
                        else:
                            act.activation(hxT[:, c, s * 128:(s + 1) * 128], tpb[:, c * 128:(c + 1) * 128], AF.Identity,
                                           bias=SH[:, 1, c, w:w + 1], scale=GS[:, 1, c, w:w + 1])
                if vlo - lo > 0:
                    pool.memset(hxT[:, :, 0:vlo - lo], 0.0)
                if vhi - lo < 512:
                    pool.memset(hxT[:, :, vhi - lo:512], 0.0)
                chains = []
                for i in range(22):
                    ca, cg, ga = ca2[i % 2], cg2[i % 2], ga2[i % 2]
                    chains.append(k.record())
                    loadw()
                    wt = wst[nuse % NW]
                    nuse += 1
                    pa, pg_ = pz[(2 * i) % 4], pz[(2 * i + 1) % 4]
                    for (pp, hf) in ((pa, 0), (pg_, 1)):
                        for kc in range(8):
                            pe.matmul(pp[:], lhsT=wt[:, kc, hf * 128:(hf + 1) * 128], rhs=hxT[:, kc, :], start=(kc == 0), stop=(kc == 7))
                    for (pp, f, cc_) in ((pa, i, ca), (pg_, 22 + i, cg)):
                        act.activation(cc_[:, 1:511], pp[:, 1:511], AF.Identity, bias=cb[:, f:f + 1], scale=cw[:, f, 1:2])
                        dve.scalar_tensor_tensor(cc_[:, 1:511], pp[:, 0:510], cw[:, f, 0:1], cc_[:, 1:511], ALU.mult, ALU.add)
                        dve.scalar_tensor_tensor(cc_[:, 1:511], pp[:, 2:512], cw[:, f, 2:3], cc_[:, 1:511], ALU.mult, ALU.add)
                    act.activation(ga[:, 1:511], ca[:, 1:511], AF.Gelu_apprx_tanh)
                    dve.tensor_tensor(hid[:, i, 1:511], ga[:, 1:511], cg[:, 1:511], ALU.mult)
                    k.stop()
                    if i % 2 == 1:
                        k.replay(chains)
                        chains = []
                for s in range(4):
                    ca_, cb_ = max(1, 128 * s), min(own + 1, 128 * (s + 1))
                    if ca_ >= cb_:
                        continue
                    for nn in range(2):
                        for i in range(22):
                            pe.matmul(po[:, nn * 512:(nn + 1) * 512], lhsT=hid[:, i, s * 128:(s + 1) * 128], rhs=wdn[:, i, nn * 512:(nn + 1) * 512],
                                      start=(i == 0), stop=(i == 21))
                    xo_ = xo[s % 2]
                    epilogue(po, X[:, s, :], GB[1][w], xo_, st4, junk, tmpf)
                    r0 = lo + ca_
                    p0, p1 = ca_ - 128 * s, cb_ - 128 * s
                    if last and w == 0:
                        k.dma(out=yout[r0:r0 + (p1 - p0), :], in_=xo_[p0:p1, :], wkey=("y", n, s))
                    else:
                        k.dma(out=XS[s0 + r0:s0 + r0 + (p1 - p0), :], in_=xo_[p0:p1, :])
        k.barrier()

    phases = []
    for l in range(depth):
        ctxout = l < depth - 1
        last = l == depth - 1
        phases += [("p0", lambda l=l: phase0(l)), ("p1", lambda l=l, c=ctxout, la=last: phase1(l, c, la)),
                   ("p2", lambda l=l, c=ctxout, la=last: phase2(l, c, la)), ("p3", lambda l=l, c=ctxout, la=last: phase3(l, c, la)),
                   ("p4", lambda l=l, c=ctxout, la=last: phase4(l, c, la)), ("p5", lambda l=l, c=ctxout, la=last: phase5(l, c, la)),
                   ("p6", lambda l=l, c=ctxout, la=last: phase6(l, c, la))]
    for n, (nm, fn) in enumerate(phases):
        fn()
        if stop_after is not None and n + 1 >= stop_after:
            break
    k.finish()
    top.close()
    return nc, k


_CACHE = {}


def _shapes(m):
    return {k_: (v.shape, "bf16" if v.dtype == ml_dtypes.bfloat16 else "f32") for k_, v in m.items()}


def run(inputs, nx, batches, stop_after=None, dbg=False, depth=DEPTH):
    inputs = {k_: np.asarray(v) for k_, v in inputs.items()}
    maps = [_layout_inputs(inputs, b, nx, fl) for (b, fl) in batches]
    key = (nx, stop_after, dbg, depth)
    if key not in _CACHE:
        _CACHE[key] = build(nx, _shapes(maps[0]), depth=depth, stop_after=stop_after, dbg=dbg)
    nc, kb = _CACHE[key]
    res = run_bass_kernel_spmd(nc, maps, core_ids=list(range(len(maps))))
    return res


def kernel(**inputs):
    nx = inputs["x"].shape[1]
    nb = inputs["x"].shape[0]
    batches = [(i // 2, bool(i % 2)) for i in range(2 * nb)]
    res = run(inputs, nx, batches)
    out = np.empty((nb, nx, D), np.float32)
    h = nx // 2
    for b in range(nb):
        out[b, :h] = np.asarray(res.results[2 * b]["y"], dtype=np.float32)
        out[b, h:] = np.asarray(res.results[2 * b + 1]["y"], dtype=np.float32)[::-1]
    return out
```
